# Optimizing a Trainium2 kernel written in Bass

```python
import math
import jax, jax.numpy as jnp
from jax import lax
import numpy as np

D_MODEL = 1024
BATCH = 16
SEQ = 2048
DEPTH = 1

MEM_LEN = 256
D_S5 = D_MODEL // 2
D_LRU = D_MODEL - D_S5
S5_GROUP = 16
S5_GROUPS = D_S5 // S5_GROUP
S5_STATE = 64
LRU_HEADS = 8
LRU_HEAD_DIM = D_LRU // LRU_HEADS
CONV_WIDTH = 4
LRU_C = 8.0
D_FF = 4 * D_MODEL
CA_HEADS = 4
CA_HEAD_DIM = D_MODEL // CA_HEADS
D_IN = D_S5 + 2 * D_LRU
ALPHA = (2 * DEPTH) ** 0.25
BETA = (8 * DEPTH) ** -0.25
LN_EPS = 1e-5
RMS_EPS = 1e-6

kernel_name = "hymba_s5_rglru_deepnorm_memxattn"


def layer_norm(x, g, b):
    xf = x.astype(jnp.float32)
    mu = jnp.mean(xf, axis=-1, keepdims=True)
    var = jnp.mean(jnp.square(xf - mu), axis=-1, keepdims=True)
    return ((xf - mu) * lax.rsqrt(var + LN_EPS) * g + b).astype(x.dtype)


def rms_norm(x, g):
    xf = x.astype(jnp.float32)
    ms = jnp.mean(jnp.square(xf), axis=-1, keepdims=True)
    return (xf * lax.rsqrt(ms + RMS_EPS) * g).astype(x.dtype)


def _complex_diag_combine(left, right):
    a1r, a1i, b1r, b1i = left
    a2r, a2i, b2r, b2i = right
    ar = a2r * a1r - a2i * a1i
    ai = a2r * a1i + a2i * a1r
    br = a2r * b1r - a2i * b1i + b2r
    bi = a2r * b1i + a2i * b1r + b2i
    return ar, ai, br, bi


def s5_mixer(u, a_re, a_im, log_dt, b_re, b_im, c_re, c_im, d, w_glu, b_glu):
    f32 = jnp.float32
    bsz, seq, _ = u.shape
    uf = u.astype(f32).reshape(bsz, seq, S5_GROUPS, S5_GROUP)
    a_re = a_re.astype(f32)
    a_im = a_im.astype(f32)
    dt = jnp.exp(log_dt.astype(f32))[:, None]
    mag = jnp.exp(dt * a_re)
    abar_re = mag * jnp.cos(dt * a_im)
    abar_im = mag * jnp.sin(dt * a_im)
    den = a_re * a_re + a_im * a_im
    q_re = ((abar_re - 1.0) * a_re + abar_im * a_im) / den
    q_im = (abar_im * a_re - (abar_re - 1.0) * a_im) / den
    b_re = b_re.astype(f32)
    b_im = b_im.astype(f32)
    bbar_re = q_re[..., None] * b_re - q_im[..., None] * b_im
    bbar_im = q_re[..., None] * b_im + q_im[..., None] * b_re
    bu_re = jnp.einsum('bsgh,gph->bsgp', uf, bbar_re)
    bu_im = jnp.einsum('bsgh,gph->bsgp', uf, bbar_im)
    lam_re = jnp.broadcast_to(abar_re, (1, seq) + abar_re.shape)
    lam_im = jnp.broadcast_to(abar_im, (1, seq) + abar_im.shape)
    _, _, h_re, h_im = lax.associative_scan(
        _complex_diag_combine, (lam_re, lam_im, bu_re, bu_im), axis=1)
    y = (jnp.einsum('bsgp,ghp->bsgh', h_re, c_re.astype(f32))
         - jnp.einsum('bsgp,ghp->bsgh', h_im, c_im.astype(f32)))
    y = y + d.astype(f32) * uf
    y = jax.nn.gelu(y.reshape(bsz, seq, D_S5))
    y = y * jax.nn.sigmoid(y @ w_glu.astype(f32) + b_glu.astype(f32))
    return y.astype(u.dtype)


def causal_depthwise_conv(x, w, b):
    seq = x.shape[1]
    xp = jnp.pad(x, ((0, 0), (CONV_WIDTH - 1, 0), (0, 0)))
    out = b
    for k in range(CONV_WIDTH):
        out = out + xp[:, k:k + seq] * w[k]
    return out


def _linear_recurrence_combine(left, right):
    a1, b1 = left
    a2, b2 = right
    return a1 * a2, a2 * b1 + b2


def rglru_mixer(xb, gate, conv_w, conv_b, w_a, b_a, w_x, b_x, lam):
    f32 = jnp.float32
    bsz, seq, _ = xb.shape
    xc = causal_depthwise_conv(xb.astype(f32), conv_w.astype(f32), conv_b.astype(f32))
    xh = xc.reshape(bsz, seq, LRU_HEADS, LRU_HEAD_DIM)
    r = jax.nn.sigmoid(jnp.einsum('bshi,hij->bshj', xh, w_a.astype(f32)) + b_a.astype(f32))
    ig = jax.nn.sigmoid(jnp.einsum('bshi,hij->bshj', xh, w_x.astype(f32)) + b_x.astype(f32))
    r = r.reshape(bsz, seq, D_LRU)
    ig = ig.reshape(bsz, seq, D_LRU)
    log_a = -LRU_C * r * jax.nn.softplus(-lam.astype(f32))
    a = jnp.exp(log_a)
    mult = jnp.sqrt(-jnp.expm1(2.0 * log_a))
    _, h = lax.associative_scan(_linear_recurrence_combine, (a, mult * (ig * xc)), axis=1)
    y = h * jax.nn.gelu(gate.astype(f32))
    return y.astype(xb.dtype)


def memory_cross_attention(h, mem_n, w_q, w_k, w_v, w_o):
    bsz, seq, _ = h.shape
    mlen = mem_n.shape[1]
    q = (h @ w_q).reshape(bsz, seq, CA_HEADS, CA_HEAD_DIM)
    k = (mem_n @ w_k).reshape(bsz, mlen, CA_HEADS, CA_HEAD_DIM)
    v = (mem_n @ w_v).reshape(bsz, mlen, CA_HEADS, CA_HEAD_DIM)
    scores = jnp.einsum('bshd,bmhd->bhsm', q.astype(jnp.float32), k.astype(jnp.float32))
    probs = jax.nn.softmax(scores * (CA_HEAD_DIM ** -0.5), axis=-1)
    out = jnp.einsum('bhsm,bmhd->bshd', probs, v.astype(jnp.float32))
    out = out.reshape(bsz, seq, D_MODEL).astype(h.dtype)
    return out @ w_o


def setup_inputs(seed: int = 0) -> dict:
    key = jax.random.key(seed)
    ks = iter(jax.random.split(key, 48))
    nrm = lambda shape, s: s * jax.random.normal(next(ks), shape, jnp.float32)
    L = DEPTH
    x = jax.random.normal(next(ks), (BATCH, SEQ, D_MODEL), jnp.float32)
    mem = jax.random.normal(next(ks), (BATCH, MEM_LEN, D_MODEL), jnp.float32)
    n = jnp.arange(S5_STATE, dtype=jnp.float32)
    s5_a_re = -0.5 + nrm((L, S5_GROUPS, S5_STATE), 0.01)
    s5_a_im = jnp.pi * n + nrm((L, S5_GROUPS, S5_STATE), 0.01)
    s5_log_dt = jax.random.uniform(next(ks), (L, S5_GROUPS), jnp.float32,
                                   math.log(1e-3), math.log(1e-1))
    u_a = jax.random.uniform(next(ks), (L, D_LRU), jnp.float32, 0.9, 0.999)
    a0 = u_a ** (1.0 / LRU_C)
    lru_lambda = jnp.log(a0) - jnp.log1p(-a0)
    return {
        "x": x,
        "mem": mem,
        "ln_in_g": 1.0 + nrm((D_MODEL,), 0.02),
        "ln_in_b": nrm((D_MODEL,), 0.02),
        "w_in": nrm((L, D_MODEL, D_IN), D_MODEL ** -0.5),
        "s5_a_re": s5_a_re,
        "s5_a_im": s5_a_im,
        "s5_log_dt": s5_log_dt,
        "s5_b_re": nrm((L, S5_GROUPS, S5_STATE, S5_GROUP), (2.0 * S5_GROUP) ** -0.5),
        "s5_b_im": nrm((L, S5_GROUPS, S5_STATE, S5_GROUP), (2.0 * S5_GROUP) ** -0.5),
        "s5_c_re": nrm((L, S5_GROUPS, S5_GROUP, S5_STATE), (2.0 * S5_STATE) ** -0.5),
        "s5_c_im": nrm((L, S5_GROUPS, S5_GROUP, S5_STATE), (2.0 * S5_STATE) ** -0.5),
        "s5_d": nrm((L, S5_GROUPS, S5_GROUP), 1.0),
        "s5_w_glu": nrm((L, D_S5, D_S5), D_S5 ** -0.5),
        "s5_b_glu": nrm((L, D_S5), 0.02),
        "conv_w": nrm((L, CONV_WIDTH, D_LRU), CONV_WIDTH ** -0.5),
        "conv_b": nrm((L, D_LRU), 0.02),
        "lru_w_a": nrm((L, LRU_HEADS, LRU_HEAD_DIM, LRU_HEAD_DIM), LRU_HEAD_DIM ** -0.5),
        "lru_b_a": nrm((L, LRU_HEADS, LRU_HEAD_DIM), 0.02),
        "lru_w_x": nrm((L, LRU_HEADS, LRU_HEAD_DIM, LRU_HEAD_DIM), LRU_HEAD_DIM ** -0.5),
        "lru_b_x": nrm((L, LRU_HEADS, LRU_HEAD_DIM), 0.02),
        "lru_lambda": lru_lambda,
        "g_s5": 1.0 + nrm((L, D_S5), 0.02),
        "g_lru": 1.0 + nrm((L, D_LRU), 0.02),
        "w_mix_out": nrm((L, D_MODEL, D_MODEL), BETA * D_MODEL ** -0.5),
        "ln1_g": 1.0 + nrm((L, D_MODEL), 0.02),
        "ln1_b": nrm((L, D_MODEL), 0.02),
        "mem_ln_g": 1.0 + nrm((L, D_MODEL), 0.02),
        "mem_ln_b": nrm((L, D_MODEL), 0.02),
        "w_q": nrm((L, D_MODEL, D_MODEL), D_MODEL ** -0.5),
        "w_k": nrm((L, D_MODEL, D_MODEL), D_MODEL ** -0.5),
        "w_v": nrm((L, D_MODEL, D_MODEL), BETA * D_MODEL ** -0.5),
        "w_o": nrm((L, D_MODEL, D_MODEL), BETA * D_MODEL ** -0.5),
        "ln2_g": 1.0 + nrm((L, D_MODEL), 0.02),
        "ln2_b": nrm((L, D_MODEL), 0.02),
        "w_ff1": nrm((L, D_MODEL, D_FF), BETA * D_MODEL ** -0.5),
        "w_ff2": nrm((L, D_FF, D_MODEL), BETA * D_FF ** -0.5),
        "ln3_g": 1.0 + nrm((L, D_MODEL), 0.02),
        "ln3_b": nrm((L, D_MODEL), 0.02),
    }


def reference(x, mem, ln_in_g, ln_in_b, w_in, s5_a_re, s5_a_im, s5_log_dt, s5_b_re, s5_b_im,
              s5_c_re, s5_c_im, s5_d, s5_w_glu, s5_b_glu, conv_w, conv_b, lru_w_a, lru_b_a,
              lru_w_x, lru_b_x, lru_lambda, g_s5, g_lru, w_mix_out, ln1_g, ln1_b,
              mem_ln_g, mem_ln_b, w_q, w_k, w_v, w_o, ln2_g, ln2_b, w_ff1, w_ff2,
              ln3_g, ln3_b):
    h = layer_norm(x, ln_in_g, ln_in_b)
    for l in range(DEPTH):
        z = h @ w_in[l]
        u_s5 = z[..., :D_S5]
        x_lru = z[..., D_S5:D_S5 + D_LRU]
        gate_lru = z[..., D_S5 + D_LRU:]
        y_s5 = s5_mixer(u_s5, s5_a_re[l], s5_a_im[l], s5_log_dt[l], s5_b_re[l], s5_b_im[l],
                        s5_c_re[l], s5_c_im[l], s5_d[l], s5_w_glu[l], s5_b_glu[l])
        y_lru = rglru_mixer(x_lru, gate_lru, conv_w[l], conv_b[l], lru_w_a[l], lru_b_a[l],
                            lru_w_x[l], lru_b_x[l], lru_lambda[l])
        y = jnp.concatenate([rms_norm(y_s5, g_s5[l]), rms_norm(y_lru, g_lru[l])], axis=-1)
        h = layer_norm(ALPHA * h + y @ w_mix_out[l], ln1_g[l], ln1_b[l])
        mem_n = layer_norm(mem, mem_ln_g[l], mem_ln_b[l])
        ca = memory_cross_attention(h, mem_n, w_q[l], w_k[l], w_v[l], w_o[l])
        h = layer_norm(ALPHA * h + ca, ln2_g[l], ln2_b[l])
        ff = jnp.square(jax.nn.relu(h @ w_ff1[l])) @ w_ff2[l]
        h = layer_norm(ALPHA * h + ff, ln3_g[l], ln3_b[l])
    return h
```

```python
import contextlib
import math
import numpy as np
import concourse.bass as bass
import concourse.mybir as mybir
from concourse.bass_utils import run_bass_kernel_spmd

F32 = mybir.dt.float32
BF16 = mybir.dt.bfloat16
AF = mybir.ActivationFunctionType
ALU = mybir.AluOpType

NCORES = 8
D = 1024
SEQ = 2048
NTOK = 4096
MEM = 256
DFF = 4096
ALPHA = 2.0 ** 0.25
LN_EPS = 1e-5
RMS_EPS = 1e-6
NT = 512
PI = math.pi

VC = {}
_o = 0
for _n in ["ln_in_g", "ln_in_b", "ln1_g", "ln1_b", "ln2_g", "ln2_b", "ln3_g", "ln3_b", "mem_g", "mem_b"]:
    VC[_n] = _o
    _o += 8
for _n in ["b_glu", "g_s5", "g_lru", "conv_b", "b_a", "b_x", "lam"]:
    VC[_n] = _o
    _o += 4
VC["conv_w"] = _o
_o += 16
VC["s5_d"] = _o
_o += 4
NV = _o


class DSem:
    def __init__(self, fw, name):
        self.h = fw.root.enter_context(fw.nc.semaphore(name))
        self.count = 0
        self.name = name


class Buf:
    def __init__(self, name, t=None, dsem=None):
        self.name = name
        self.t = t
        self.w = None
        self.r = []
        self.dsem = dsem

    def v(self):
        return V(self, self.t[:])

    def __getitem__(self, k):
        return V(self, self.t[k])


class V:
    def __init__(self, buf, ap):
        self.buf = buf
        self.ap = ap

    def __getitem__(self, k):
        return V(self.buf, self.ap[k])

    def re(self, pat, **kw):
        return V(self.buf, self.ap.rearrange(pat, **kw))

    def bc(self, shape):
        return V(self.buf, self.ap.to_broadcast(list(shape)))

    def un(self, axis):
        return V(self.buf, self.ap.unsqueeze(axis))


COMPUTE = ("pe", "act", "dve", "pool")


def _bufs(*vs):
    out = []
    for v in vs:
        if isinstance(v, V) and v.buf is not None and v.buf not in out:
            out.append(v.buf)
    return out


def _ap(x):
    return x.ap if isinstance(x, V) else x


class FW:
    def __init__(self, nc, same_engine_sync=True):
        self.nc = nc
        self.root = contextlib.ExitStack()
        self.es = self.root
        self.eng = {"pe": nc.tensor, "act": nc.scalar, "dve": nc.vector, "pool": nc.gpsimd, "sp": nc.sync}
        self.sem = {}
        self.cnt = {}
        for e in COMPUTE:
            self.sem[e] = self.root.enter_context(nc.semaphore("sem_" + e))
            self.cnt[e] = 0
        self.known = {e: {} for e in self.eng}
        self.same_engine_sync = same_engine_sync
        self.dsems = []
        self.dry = False
        self.uid = 0
        self.banks = []
        self.bank_i = 0
        self.ninst = 0

    def _name(self, name):
        self.uid += 1
        return f"{name}_{self.uid}"

    def sbuf(self, name, shape, dt, dsem=None):
        t = self.es.enter_context(self.nc.sbuf_tensor(self._name(name), list(shape), dt))
        return Buf(name, t, dsem)

    def psum(self, name, shape, dt):
        t = self.es.enter_context(self.nc.psum_tensor(self._name(name), list(shape), dt))
        return Buf(name, t)

    def dsem(self, name):
        d = DSem(self, name)
        self.dsems.append(d)
        return d

    def sub(self, parent, name):
        return Buf(name, parent.t, parent.dsem)

    @contextlib.contextmanager
    def scope(self):
        outer = self.es
        self.es = contextlib.ExitStack()
        try:
            yield
        finally:
            self.barrier()
            self.es.close()
            self.es = outer

    def bank(self):
        b = self.banks[self.bank_i % len(self.banks)]
        self.bank_i += 1
        return b

    def _wait(self, e, dep):
        kind, key, val = dep
        if kind == "eng":
            if key == e and (e == "pe" or not self.same_engine_sync):
                return
            kk = ("eng", key)
            if self.known[e].get(kk, 0) >= val:
                return
            self.eng[e].wait_ge(self.sem[key], val)
            self.known[e][kk] = val
        else:
            val = key.count
            kk = ("dma", key.name)
            if self.known[e].get(kk, 0) >= val:
                return
            self.eng[e].wait_ge(key.h, val)
            self.known[e][kk] = val
        self.ninst += 1

    def _deps(self, e, reads, writes):
        for b in reads:
            if b.w is not None:
                self._wait(e, b.w)
        for b in writes:
            if b.w is not None:
                self._wait(e, b.w)
            for d in b.r:
                self._wait(e, d)

    def op(self, e, reads, writes, fn):
        if self.dry:
            return None
        self._deps(e, reads, writes)
        ins = fn(self.eng[e])
        self.cnt[e] += 1
        self.ninst += 1
        ins.then_inc(self.sem[e], 1)
        tag = ("eng", e, self.cnt[e])
        for b in reads:
            if b not in writes:
                b.r.append(tag)
        for b in writes:
            b.w = tag
            b.r = []
        return ins

    def mm(self, out, lhsT, rhs, start, stop, **kw):
        if self.dry:
            return None
        e = "pe"
        psb = out.buf
        rd = _bufs(lhsT, rhs)
        self._deps(e, rd, [psb] if start else [])
        ins = self.nc.tensor.matmul(out.ap, lhsT.ap, rhs.ap, start=start, stop=stop, **kw)
        self.ninst += 1
        if stop:
            self.cnt[e] += 1
            ins.then_inc(self.sem[e], 1)
            tag = ("eng", e, self.cnt[e])
        else:
            tag = ("eng", e, self.cnt[e] + 1)
        for b in rd:
            b.r.append(tag)
        psb.w = tag
        if start:
            psb.r = []
        return ins

    def transpose(self, out, in_, ident):
        if self.dry:
            return None
        e = "pe"
        rd = _bufs(in_, ident)
        self._deps(e, rd, [out.buf])
        ins = self.nc.tensor.transpose(out.ap, in_.ap, ident.ap)
        self.ninst += 1
        self.cnt[e] += 1
        ins.then_inc(self.sem[e], 1)
        tag = ("eng", e, self.cnt[e])
        for b in rd:
            b.r.append(tag)
        out.buf.w = tag
        out.buf.r = []
        return ins

    def dma(self, q, out, in_, dsem=None, **kw):
        if self.dry:
            return None
        reads = _bufs(in_)
        writes = _bufs(out)
        self._deps(q, reads, writes)
        if dsem is None:
            for b in writes + reads:
                if b.dsem is not None:
                    dsem = b.dsem
                    break
        assert dsem is not None, "dma needs a DSem"
        ins = self.eng[q].dma_start(out=out.ap, in_=in_.ap, **kw)
        self.ninst += 1
        dsem.count += 16
        ins.then_inc(dsem.h, 16)
        tag = ("dma", dsem, dsem.count)
        for b in reads:
            b.r.append(tag)
        for b in writes:
            b.w = tag
            b.r = []
        return ins

    def barrier(self, engines=("pe", "act", "dve", "pool", "sp")):
        if self.dry:
            return
        for e in engines:
            for f in COMPUTE:
                if f != e and self.cnt[f] > 0:
                    self._wait(e, ("eng", f, self.cnt[f]))
            for d in self.dsems:
                if d.count > 0:
                    self._wait(e, ("dma", d, d.count))

    def tt(self, e, out, a, b, op):
        return self.op(e, _bufs(a, b), _bufs(out),
                       lambda E: E.tensor_tensor(out=out.ap, in0=a.ap, in1=b.ap, op=op))

    def ts(self, e, out, a, s1, op0, s2=None, op1=None):
        kw = {}
        if op1 is not None:
            kw["op1"] = op1
        return self.op(e, _bufs(a, s1, s2), _bufs(out),
                       lambda E: E.tensor_scalar(out=out.ap, in0=a.ap, scalar1=_ap(s1), scalar2=_ap(s2), op0=op0, **kw))

    def stt(self, e, out, in0, scalar, in1, op0, op1):
        return self.op(e, _bufs(in0, scalar, in1), _bufs(out),
                       lambda E: E.scalar_tensor_tensor(out=out.ap, in0=in0.ap, scalar=_ap(scalar), in1=in1.ap,
                                                        op0=op0, op1=op1))

    def act(self, out, in_, func, bias=None, scale=None):
        kw = {}
        if bias is not None:
            kw["bias"] = _ap(bias)
        if scale is not None:
            kw["scale"] = _ap(scale)
        return self.op("act", _bufs(in_, bias, scale), _bufs(out),
                       lambda E: E.activation(out=out.ap, in_=in_.ap, func=func, **kw))

    def cp(self, e, out, in_):
        if e == "act":
            return self.act(out, in_, AF.Copy)
        return self.op(e, _bufs(in_), _bufs(out), lambda E: E.tensor_copy(out=out.ap, in_=in_.ap))

    def memset(self, e, out, val):
        return self.op(e, [], _bufs(out), lambda E: E.memset(out.ap, val))

    def scan(self, out, d0, d1, init, op0=ALU.mult, op1=ALU.add):
        return self.op("dve", _bufs(d0, d1, init), _bufs(out),
                       lambda E: E.tensor_tensor_scan(out=out.ap, data0=d0.ap, data1=d1.ap, initial=_ap(init),
                                                      op0=op0, op1=op1))

    def recip(self, out, in_):
        return self.op("dve", _bufs(in_), _bufs(out), lambda E: E.reciprocal(out=out.ap, in_=in_.ap))


class WS:
    def __init__(self, fw, nslots, elems):
        self.fw = fw
        self.n = nslots
        self.slots = [fw.sbuf(f"ring{i}", [128, elems], BF16, fw.dsem(f"dring{i}")) for i in range(nslots)]
        self.plan = []
        self.pos = 0
        self.issued = 0

    def reset_for_real(self):
        self.pos = 0
        self.issued = 0

    def get(self, dram_v, n):
        fw = self.fw
        if fw.dry:
            self.plan.append((dram_v, n))
            self.pos += 1
            return self.slots[(self.pos - 1) % self.n].v()
        i = self.pos
        self.pos += 1
        lim = min(len(self.plan), i + self.n)
        while self.issued < lim:
            j = self.issued
            dv, nn = self.plan[j]
            fw.dma("pool", self.slots[j % self.n][:, 0:nn], dv)
            self.issued += 1
        return self.slots[i % self.n].v()


def build_program(debug=False):
    nc = bass.Bass("TRN2", target_bir_lowering=False)
    fw = FW(nc)

    def din(name, shape, dt=F32):
        return V(None, nc.dram_tensor(name, list(shape), dt, kind="ExternalInput").ap())

    xT = din("xT", [D, NTOK])
    memT = din("memT", [D, 2 * MEM])
    vecs_d = din("vecs", [128, NV])
    ident_d = din("ident", [128, 128])
    s5sl_d = din("s5sl", [128, 3, 16])
    bsl_d = din("bsl", [128, 2, 16, 32])
    csl_d = din("csl", [128, 2, 16, 32])
    wab_d = din("wab", [128, 2, 4, 128])
    wb_in = din("wb_in", [3, 128, 4096])
    wb_mix = din("wb_mix", [2, 128, 4096])
    wb_q = din("wb_q", [2, 128, 4096])
    wb_k = din("wb_k", [2, 128, 4096])
    wb_o = din("wb_o", [2, 128, 4096])
    wb_ff1 = din("wb_ff1", [8, 128, 4096])
    wb_ff2 = din("wb_ff2", [8, 128, 4096])
    wv_d = din("w_v", [D, D])
    wglu_d = din("w_glu", [512, 512])
    outT = V(None, nc.dram_tensor("outT", [D, NTOK], F32, kind="ExternalOutput").ap())
    dbg = {}
    if debug:
        for nm, shp, dt_ in [("d_ys5", [128, 4, SEQ], BF16), ("d_ylru", [128, 4, SEQ], BF16), ("d_u", [128, 4, SEQ], BF16),
                        ("d_h1", [128, 8, NT], F32), ("d_h2", [128, 8, NT], F32), ("d_hprev", [128, 16, 2, 256], BF16),
                        ("d_yg", [128, 4, SEQ], BF16)]:
            dbg[nm] = V(None, nc.dram_tensor(nm, shp, dt_, kind="ExternalOutput").ap())

    xT_v = xT.re("(kt p) n -> p kt n", p=128)
    memT_v = memT.re("(kt p) n -> p kt n", p=128)
    outT_v = outT.re("(kt p) n -> p kt n", p=128)

    def dscratch(name, shape, dt):
        return Buf(name, nc.dram_tensor(name, list(shape), dt).ap())

    sc_ktl = dscratch("sc_ktl", [128, 4 * 8 * 128], BF16)
    sc_bct = dscratch("sc_bct", [128, 4 * 8 * 2 * 128], BF16)
    sc_fb = dscratch("sc_fb", [128, 2 * 8 * 16 * 32], BF16)
    sc_cos = dscratch("sc_cos", [128, 16 * 256], F32)
    sc_sin = dscratch("sc_sin", [128, 16 * 256], F32)
    sc_rho = dscratch("sc_rho", [128, 16], F32)

    d_init = fw.dsem("d_init")
    d_ph = fw.dsem("d_ph")
    d_ph2 = fw.dsem("d_ph2")
    d_initp = fw.dsem("d_initp")
    d_php = fw.dsem("d_php")
    d_x = fw.dsem("d_x")
    d_x2 = fw.dsem("d_x2")
    d_out = fw.dsem("d_out")
    d_sv = fw.dsem("d_sv")
    d_dbg = fw.dsem("d_dbg")
    for b_ in (sc_ktl, sc_bct, sc_fb, sc_cos, sc_sin, sc_rho):
        b_.dsem = d_sv

    fw.banks = [fw.psum(f"bank{i}", [128, 512], F32) for i in range(8)]
    vecs = fw.sbuf("vecs", [128, NV], F32, d_init)
    ones_ln = fw.sbuf("ones_ln", [128, 128], BF16)
    ones_rms = fw.sbuf("ones_rms", [128, 128], BF16)
    ones_1 = fw.sbuf("ones_1", [128, 128], BF16)
    zeros_bf = fw.sbuf("zeros_bf", [128, 128], BF16)
    cneg = fw.sbuf("cneg", [128, 4], F32)
    cneg2 = fw.sbuf("cneg2", [128, 4], F32)
    KTs = fw.sbuf("KTs", [128, 8, 2 * MEM], BF16)
    Vs = fw.sbuf("Vs", [128, 4, D], BF16)
    YS5 = fw.sbuf("YS5", [128, 4, SEQ], BF16)
    YLRU = fw.sbuf("YLRU", [128, 4, SEQ], BF16)
    ws = WS(fw, 4, 4096)

    def vc(name, i):
        c = VC[name] + i
        return vecs[:, c:c + 1]

    def layernorm_g(r, gname, bname, N, scr_xb, scr_sq, tm, out_bf=None, keep_f32=True, dst_f32=None):
        if dst_f32 is None:
            dst_f32 = r
        fw.cp("dve", scr_xb, r)
        fw.act(scr_sq, r, AF.Square)
        yield
        pm = fw.bank()
        pq = fw.bank()
        for kt in range(8):
            fw.mm(pm[:, 0:N], ones_ln.v(), scr_xb[:, kt, :], kt == 0, kt == 7)
        for kt in range(8):
            fw.mm(pq[:, 0:N], ones_ln.v(), scr_sq[:, kt, :], kt == 0, kt == 7)
        mean, t = tm
        fw.act(t, pm[:, 0:N], AF.Square)
        fw.act(mean, pm[:, 0:N], AF.Copy)
        fw.stt("dve", t, pq[:, 0:N], LN_EPS, t, ALU.add, ALU.subtract)
        fw.act(t, t, AF.Ln)
        fw.act(t, t, AF.Exp, scale=-0.5)
        fw.tt("dve", r, r, mean.un(1).bc([128, 8, N]), ALU.subtract)
        fw.tt("dve", r, r, t.un(1).bc([128, 8, N]), ALU.mult)
        yield
        for kt in range(8):
            if keep_f32:
                fw.act(dst_f32[:, kt, :], r[:, kt, :], AF.Identity, bias=vc(bname, kt), scale=vc(gname, kt))
            else:
                fw.act(out_bf[:, kt, :], r[:, kt, :], AF.Identity, bias=vc(bname, kt), scale=vc(gname, kt))
        if keep_f32 and out_bf is not None:
            fw.cp("dve", out_bf, dst_f32)
        yield

    def layernorm(*a, **kw):
        for _ in layernorm_g(*a, **kw):
            pass

    GC1 = 2.0 * math.sqrt(2.0 / math.pi)

    def gelu_tanh(dst, ps, tmp):
        fw.act(tmp, ps, AF.Square)
        fw.stt("dve", tmp, tmp, 1.0 / 0.044715, ps, ALU.add, ALU.mult)
        fw.act(tmp, tmp, AF.Sigmoid, scale=GC1 * 0.044715)
        fw.tt("dve", dst, tmp, ps, ALU.mult)

    def rmsnorm_finish(y32, gname, out_bf, sq4, rs):
        fw.act(sq4, y32, AF.Square)
        ps = fw.bank()
        for t in range(4):
            fw.mm(ps.v(), ones_rms.v(), sq4[:, t, :], t == 0, t == 3)
        fw.act(rs, ps.v(), AF.Ln, bias=RMS_EPS)
        fw.act(rs, rs, AF.Exp, scale=-0.5)
        for t in range(4):
            fw.stt("dve", out_bf[:, t, :], y32[:, t, :], vc(gname, t), rs, ALU.mult, ALU.mult)

    def linear_g(wb, nslots, G, KT, rhs_fn, epi, N=NT):
        for s_ in range(nslots):
            slot = ws.get(wb[s_], G * KT * 128)
            sv = slot.re("p (g kt c) -> p g kt c", g=G, kt=KT)
            for g in range(G):
                m = s_ * G + g
                ps = fw.bank()
                for kt in range(KT):
                    fw.mm(ps[:, 0:N], sv[:, g, kt, :], rhs_fn(kt), kt == 0, kt == KT - 1)
                epi(m, ps[:, 0:N])
            yield

    def linear(*a, **kw):
        for _ in linear_g(*a, **kw):
            pass

    def interleave(gy, gx, ny, nx):
        xi = 0
        for yi in range(ny):
            next(gy, None)
            tgt = ((yi + 1) * nx) // ny if gx is not None else 0
            while xi < tgt:
                next(gx, None)
                xi += 1
        for _ in gy:
            pass
        if gx is not None:
            for _ in gx:
                pass

    def emit():
        fw.dma("sp", vecs.v(), vecs_d)
        fw.memset("dve", ones_ln.v(), 1.0 / 1024.0)
        fw.memset("dve", ones_rms.v(), 1.0 / 512.0)
        fw.memset("dve", ones_1.v(), 1.0)
        fw.memset("dve", zeros_bf.v(), 0.0)
        lamv = vecs[:, VC["lam"]:VC["lam"] + 4]
        fw.act(cneg.v(), lamv, AF.Exp, scale=-1.0)
        fw.ts("dve", cneg.v(), cneg.v(), 1.0, ALU.add)
        fw.act(cneg.v(), cneg.v(), AF.Ln)
        fw.ts("dve", cneg.v(), cneg.v(), -8.0, ALU.mult)
        fw.ts("dve", cneg2.v(), cneg.v(), 2.0, ALU.mult)

        with fw.scope():
            ident = fw.sbuf("ident", [128, 128], F32, d_ph)
            sl = fw.sbuf("sl", [128, 3, 16], F32, d_ph)
            bsl = fw.sbuf("bsl", [128, 2, 16, 32], F32, d_ph)
            csl = fw.sbuf("csl", [128, 2, 16, 32], F32, d_ph)
            fw.dma("sp", ident.v(), ident_d)
            fw.dma("sp", sl.v(), s5sl_d)
            fw.dma("sp", bsl.v(), bsl_d)
            fw.dma("sp", csl.v(), csl_d)
            SM = fw.sbuf("SM", [128, 48, 16], F32)
            _k = [0]

            def sm():
                _k[0] += 1
                return SM[:, _k[0] - 1, :]

            a_re, a_im, logdt = sl[:, 0, :], sl[:, 1, :], sl[:, 2, :]
            dt_, are, ang, mag = sm(), sm(), sm(), sm()
            fw.act(dt_, logdt, AF.Exp)
            fw.tt("dve", are, dt_, a_re, ALU.mult)
            fw.tt("dve", ang, dt_, a_im, ALU.mult)
            fw.act(mag, are, AF.Exp)
            tmp, tmp2 = sm(), sm()

            TWO_PI_HI = float(np.float32(2.0 * PI))
            TWO_PI_LO = float(np.float32(2.0 * PI - TWO_PI_HI))
            HALF_PI_HI = float(np.float32(PI / 2))
            HALF_PI_LO = float(np.float32(PI / 2 - HALF_PI_HI))

            def sin_reduced(dst, src):
                red, kf = sm(), sm()
                fw.ts("dve", kf, src, PI, ALU.is_gt)
                for k in range(1, 6):
                    fw.ts("dve", tmp, src, (2 * k + 1) * PI, ALU.is_gt)
                    fw.tt("dve", kf, kf, tmp, ALU.add)
                fw.stt("dve", red, kf, -TWO_PI_HI, src, ALU.mult, ALU.add)
                fw.stt("dve", red, kf, -TWO_PI_LO, red, ALU.mult, ALU.add)
                fw.act(dst, red, AF.Sin)

            sn, cs, angc = sm(), sm(), sm()
            sin_reduced(sn, ang)
            fw.ts("dve", angc, ang, HALF_PI_HI, ALU.add, HALF_PI_LO, ALU.add)
            sin_reduced(cs, angc)
            lr, li = sm(), sm()
            fw.tt("dve", lr, mag, cs, ALU.mult)
            fw.tt("dve", li, mag, sn, ALU.mult)
            lm1, den, qre, qim = sm(), sm(), sm(), sm()
            fw.ts("dve", lm1, lr, -1.0, ALU.add)
            fw.tt("dve", den, a_re, a_re, ALU.mult)
            fw.tt("dve", tmp, a_im, a_im, ALU.mult)
            fw.tt("dve", den, den, tmp, ALU.add)
            fw.recip(den, den)
            fw.tt("dve", qre, lm1, a_re, ALU.mult)
            fw.tt("dve", tmp, li, a_im, ALU.mult)
            fw.tt("dve", qre, qre, tmp, ALU.add)
            fw.tt("dve", qre, qre, den, ALU.mult)
            fw.tt("dve", qim, li, a_re, ALU.mult)
            fw.tt("dve", tmp, lm1, a_im, ALU.mult)
            fw.tt("dve", qim, qim, tmp, ALU.subtract)
            fw.tt("dve", qim, qim, den, ALU.mult)

            def b3(v):
                return v.un(2).bc([128, 16, 32])

            with fw.scope():
                Gb = [fw.sbuf(f"G{i}", [128, 2, 16, 32], F32) for i in range(2)]
                Fb = [fw.sbuf(f"F{i}", [128, 2, 16, 32], F32) for i in range(2)]
                F0 = fw.sbuf("F0", [128, 2, 16, 32], F32)
                T1 = fw.sbuf("T1", [128, 16, 32], F32)
                T2 = fw.sbuf("T2", [128, 16, 32], F32)
                KTl = fw.sbuf("KTl", [128, 4, 8, 128], BF16)
                BcT = fw.sbuf("BcT", [128, 4, 8, 2, 128], BF16)
                FB = fw.sbuf("FB", [128, 2, 8, 16, 32], BF16)
                fw.memset("dve", KTl.v(), 0.0)

                T3 = fw.sbuf("T3", [128, 16, 32], F32)
                T4 = fw.sbuf("T4", [128, 16, 32], F32)

                def cmul(dst, src, xr, xi, conj=False, e="dve"):
                    Ta, Tb = (T1, T2) if e == "dve" else (T3, T4)
                    sre, sim = src[:, 0], src[:, 1]
                    fw.tt(e, Ta.v(), sre, b3(xr), ALU.mult)
                    fw.tt(e, Tb.v(), sim, b3(xi), ALU.mult)
                    fw.tt(e, dst[:, 0], Ta.v(), Tb.v(), ALU.add if conj else ALU.subtract)
                    fw.tt(e, Ta.v(), sre, b3(xi), ALU.mult)
                    fw.tt(e, Tb.v(), sim, b3(xr), ALU.mult)
                    fw.tt(e, dst[:, 1], Tb.v(), Ta.v(), ALU.subtract if conj else ALU.add)

                lrp = fw.sbuf("lrp", [128, 2, 16], F32)
                fw.cp("dve", lrp[:, 0, :], lr)
                fw.cp("dve", lrp[:, 1, :], li)
                fw.cp("dve", F0[:, 0], csl[:, 0])
                fw.ts("dve", F0[:, 1], csl[:, 1], -1.0, ALU.mult)
                cmul(Gb[0].v(), bsl.v(), qre, qim)
                dvec = vecs[:, VC["s5_d"]:VC["s5_d"] + 4]
                for n in range(8):
                    G = Gb[n % 2]
                    for j in range(4):
                        kb = fw.bank()
                        for k in range(4):
                            q = 4 * j + k
                            for ri in range(2):
                                fw.mm(kb[32 * k:32 * k + 32, 32 * k:32 * k + 32], G[:, ri, q, :], F0[:, ri, q, :],
                                      ri == 0, ri == 1, tile_position=(0, 32 * k))
                        for k in range(4):
                            sl_ = slice(32 * k, 32 * k + 32)
                            if n == 0:
                                fw.stt("dve", KTl[sl_, j, 0, sl_], ident[sl_, sl_], dvec[sl_, j:j + 1], kb[sl_, sl_],
                                       ALU.mult, ALU.add)
                            else:
                                fw.cp("act", KTl[sl_, j, n, sl_], kb[sl_, sl_])
                    for j in range(4):
                        tb = fw.bank()
                        for ri in range(2):
                            fw.transpose(tb[:, ri * 128:(ri + 1) * 128], G[:, ri, 4 * j:4 * j + 4, :].re("p a b -> p (a b)"),
                                         ident.v())
                        fw.cp("act", BcT[:, j, n, :, :].re("p a b -> p (a b)"), tb[:, 0:256])
                    if n < 7:
                        cmul(Gb[(n + 1) % 2].v(), G.v(), lr, li)
                prev = F0
                for n in range(1, 9):
                    cur = Fb[n % 2]
                    cmul(cur.v(), prev.v(), lrp[:, 0, :], lrp[:, 1, :], conj=True, e="pool")
                    for ri in range(2):
                        fw.cp("pool", FB[:, ri, n - 1], cur[:, ri])
                    prev = cur
                pr, pi_, pr2, pi2 = sm(), sm(), sm(), sm()
                fw.cp("dve", pr, lr)
                fw.cp("dve", pi_, li)
                for _ in range(3):
                    fw.tt("dve", pr2, pr, pr, ALU.mult)
                    fw.tt("dve", tmp, pi_, pi_, ALU.mult)
                    fw.tt("dve", pr2, pr2, tmp, ALU.subtract)
                    fw.tt("dve", pi2, pr, pi_, ALU.mult)
                    fw.ts("dve", pi2, pi2, 2.0, ALU.mult)
                    fw.cp("dve", pr, pr2)
                    fw.cp("dve", pi_, pi2)
                rho8 = sm()
                fw.tt("dve", rho8, pr, pr, ALU.mult)
                fw.tt("dve", tmp, pi_, pi_, ALU.mult)
                fw.tt("dve", rho8, rho8, tmp, ALU.add)
                fw.act(rho8, rho8, AF.Sqrt)
                ure, uim = sm(), sm()
                fw.recip(tmp, rho8)
                fw.tt("dve", ure, pr, tmp, ALU.mult)
                fw.tt("dve", uim, pi_, tmp, ALU.mult)
                fw.dma("sp", sc_ktl.v(), KTl.v().re("p a b c -> p (a b c)"))
                fw.dma("sp", sc_bct.v(), BcT.v().re("p a b c d -> p (a b c d)"))
                fw.dma("sp", sc_fb.v(), FB.v().re("p a b c d -> p (a b c d)"))
                fw.dma("sp", sc_rho.v(), rho8)
            COS = fw.sbuf("COS", [128, 16, 256], F32)
            SIN = fw.sbuf("SIN", [128, 16, 256], F32)
            TA = fw.sbuf("TA", [128, 16, 128], F32)
            TB = fw.sbuf("TB", [128, 16, 128], F32)
            fw.memset("dve", COS[:, :, 0:1], 1.0)
            fw.memset("dve", SIN[:, :, 0:1], 0.0)
            for kk in range(8):
                L = 1 << kk

                def bL(v):
                    return v.un(2).bc([128, 16, L])

                c0, s0 = COS[:, :, 0:L], SIN[:, :, 0:L]
                fw.tt("dve", TA[:, :, 0:L], c0, bL(ure), ALU.mult)
                fw.tt("dve", TB[:, :, 0:L], s0, bL(uim), ALU.mult)
                fw.tt("dve", COS[:, :, L:2 * L], TA[:, :, 0:L], TB[:, :, 0:L], ALU.subtract)
                fw.tt("dve", TA[:, :, 0:L], s0, bL(ure), ALU.mult)
                fw.tt("dve", TB[:, :, 0:L], c0, bL(uim), ALU.mult)
                fw.tt("dve", SIN[:, :, L:2 * L], TA[:, :, 0:L], TB[:, :, 0:L], ALU.add)
                if kk < 7:
                    fw.tt("dve", pr2, ure, ure, ALU.mult)
                    fw.tt("dve", tmp, uim, uim, ALU.mult)
                    fw.tt("dve", pr2, pr2, tmp, ALU.subtract)
                    fw.tt("dve", pi2, ure, uim, ALU.mult)
                    fw.ts("dve", pi2, pi2, 2.0, ALU.mult)
                    fw.cp("dve", ure, pr2)
                    fw.cp("dve", uim, pi2)
            fw.dma("sp", sc_cos.v(), COS.v().re("p a b -> p (a b)"))
            fw.dma("sp", sc_sin.v(), SIN.v().re("p a b -> p (a b)"))

        with fw.scope():
            memt = fw.sbuf("memt", [128, 8, 2 * MEM], F32, d_ph)
            scr1 = fw.sbuf("scr1", [128, 8, 2 * MEM], BF16)
            scr2 = fw.sbuf("scr2", [128, 8, 2 * MEM], BF16)
            memb = fw.sbuf("memb", [128, 8, 2 * MEM], BF16)
            wv = fw.sbuf("wv", [128, 8, D], BF16, d_php)
            tmA = fw.sbuf("tmA", [128, 2 * MEM], F32)
            tmB = fw.sbuf("tmB", [128, 2 * MEM], F32)
            fw.dma("sp", memt.v(), memT_v)
            fw.dma("pool", wv.v(), wv_d.re("(kt p) n -> p kt n", p=128))
            layernorm(memt.v(), "mem_g", "mem_b", 2 * MEM, scr1.v(), scr2.v(), [tmA.v(), tmB.v()],
                      out_bf=memb.v(), keep_f32=False)
            linear(wb_k, 2, 4, 8, lambda kt: memb[:, kt, :], lambda m, ps: fw.cp("act", KTs[:, m, :], ps), N=2 * MEM)
            for sm_ in range(4):
                for half in range(2):
                    ps = fw.bank()
                    for kt in range(8):
                        fw.mm(ps.v(), memb[:, kt, sm_ * 128:(sm_ + 1) * 128], wv[:, kt, half * 512:(half + 1) * 512],
                              kt == 0, kt == 7)
                    fw.cp("act", Vs[:, sm_, half * 512:(half + 1) * 512], ps.v())

        for s in range(2):
            with fw.scope():
                U = fw.sbuf("U", [128, 4, SEQ], BF16)
                Uj = [fw.sub(U, f"U{j}") for j in range(4)]
                with fw.scope():
                    XLp = fw.sbuf("XLp", [128, 4, SEQ + 4], BF16)
                    GG = fw.sbuf("GG", [128, 4, SEQ], BF16)
                    fw.memset("dve", XLp[:, :, 0:4], 0.0)
                    with fw.scope():
                        xts = [fw.sbuf("xt0", [128, 8, NT], F32, d_x), fw.sbuf("xt1", [128, 8, NT], F32, d_x2)]
                        hbs = [fw.sbuf("hb0", [128, 8, NT], BF16), fw.sbuf("hb1", [128, 8, NT], BF16)]
                        scr1 = fw.sbuf("scr1", [128, 8, NT], BF16)
                        scr2 = fw.sbuf("scr2", [128, 8, NT], BF16)
                        tmA = fw.sbuf("tmA", [128, NT], F32)
                        tmB = fw.sbuf("tmB", [128, NT], F32)
                        gtms = [fw.sbuf(f"gtm{i}", [128, NT], F32) for i in range(4)]

                        def lnA(i):
                            tok0 = s * SEQ + i * NT
                            fw.dma("sp", xts[i % 2].v(), xT_v[:, :, tok0:tok0 + NT])
                            yield from layernorm_g(xts[i % 2].v(), "ln_in_g", "ln_in_b", NT, scr1.v(), scr2.v(),
                                                   [tmA.v(), tmB.v()], out_bf=hbs[i % 2].v(), keep_f32=False)

                        def linA(i):
                            hb = hbs[i % 2]

                            def epiA(m, ps):
                                if m < 4:
                                    dst = V(Uj[m], U.t[:, m, :]).re("p (t c) -> p t c", t=8)[:, :, i * 64:(i + 1) * 64]
                                    fw.cp("act", dst, ps.re("p (c t) -> p t c", t=8))
                                elif m < 8:
                                    fw.cp("act", XLp[:, m - 4, 4 + i * NT:4 + (i + 1) * NT], ps)
                                else:
                                    gelu_tanh(GG[:, m - 8, i * NT:(i + 1) * NT], ps, gtms[m - 8].v())

                            yield from linear_g(wb_in, 3, 4, 8, lambda kt: hb[:, kt, :], epiA)

                        for _ in lnA(0):
                            pass
                        for i in range(4):
                            gl = linA(i)
                            gn = lnA(i + 1) if i < 3 else iter(())
                            for _ in range(3):
                                next(gn, None)
                                next(gl, None)
                            for _ in gn:
                                pass
                            for _ in gl:
                                pass
                    with fw.scope():
                        wab = fw.sbuf("wab", [128, 2, 4, 128], BF16, d_php)
                        fw.dma("pool", wab.v(), wab_d)
                        xc = fw.sbuf("xc", [128, 4, NT], F32)
                        xcb = fw.sbuf("xcb", [128, 4, NT], BF16)
                        hs = [fw.sbuf(f"hs{i}", [128, 4, NT], F32) for i in range(2)]
                        y32 = fw.sbuf("y32", [128, 4, NT], F32)
                        sq4 = fw.sbuf("sq4", [128, 4, NT], BF16)
                        rs = fw.sbuf("rs", [128, NT], F32)
                        rrs = [fw.sbuf(f"rr{t}", [128, NT], F32) for t in range(4)]
                        igs = [fw.sbuf(f"ig{t}", [128, NT], F32) for t in range(4)]
                        aas = [fw.sbuf(f"aa{t}", [128, NT], F32) for t in range(4)]
                        mus = [fw.sbuf(f"mu{t}", [128, NT], F32) for t in range(4)]
                        xcs = [fw.sub(xc, f"xc{t}") for t in range(4)]
                        xcbs = [fw.sub(xcb, f"xcb{t}") for t in range(4)]
                        y32s = [fw.sub(y32, f"y32_{t}") for t in range(4)]
                        hss = [[fw.sub(hs[k], f"hs{k}_{t}") for t in range(4)] for k in range(2)]
                        cw = VC["conv_w"]

                        def chain(i, t):
                            rr, ig, aa, mu = rrs[t], igs[t], aas[t], mus[t]
                            xcv = V(xcs[t], xc.t[:, t, :])
                            xcbv = V(xcbs[t], xcb.t[:, t, :])
                            xin = XLp[:, t, i * NT:i * NT + NT + 4]
                            fw.act(xcv, xin[:, 4:4 + NT], AF.Identity, bias=vc("conv_b", t),
                                   scale=vecs[:, cw + 4 * t + 3:cw + 4 * t + 4])
                            yield
                            for k in range(3):
                                fw.stt("dve", xcv, xin[:, 1 + k:1 + k + NT],
                                       vecs[:, cw + 4 * t + k:cw + 4 * t + k + 1], xcv, ALU.mult, ALU.add)
                                yield
                            fw.cp("dve", xcbv, xcv)
                            yield
                            psr = fw.bank()
                            fw.mm(psr.v(), wab[:, 0, t, :], xcbv, True, True)
                            psi = fw.bank()
                            fw.mm(psi.v(), wab[:, 1, t, :], xcbv, True, True)
                            fw.act(rr.v(), psr.v(), AF.Sigmoid, bias=vc("b_a", t))
                            fw.act(ig.v(), psi.v(), AF.Sigmoid, bias=vc("b_x", t))
                            yield
                            fw.act(aa.v(), rr.v(), AF.Exp, scale=cneg[:, t:t + 1])
                            fw.tt("dve", ig.v(), ig.v(), xcv, ALU.mult)
                            yield
                            fw.act(mu.v(), rr.v(), AF.Exp, scale=cneg2[:, t:t + 1])
                            yield
                            fw.ts("dve", mu.v(), mu.v(), -1.0, ALU.mult, 1.0, ALU.add)
                            yield
                            fw.act(mu.v(), mu.v(), AF.Sqrt)
                            yield
                            fw.tt("dve", ig.v(), ig.v(), mu.v(), ALU.mult)
                            yield
                            hcv = V(hss[i % 2][t], hs[i % 2].t[:, t, :])
                            init = 0.0 if i == 0 else V(hss[(i + 1) % 2][t], hs[(i + 1) % 2].t[:, t, NT - 1:NT])
                            fw.scan(hcv, aa.v(), ig.v(), init)
                            yield
                            fw.tt("dve", V(y32s[t], y32.t[:, t, :]), hcv, GG[:, t, i * NT:(i + 1) * NT], ALU.mult)
                            yield

                        def rms_lru(i):
                            y4 = [V(y32s[t], y32.t[:, t, :]) for t in range(4)]
                            for t in range(4):
                                fw.act(sq4[:, t, :], y4[t], AF.Square)
                            ps = fw.bank()
                            for t in range(4):
                                fw.mm(ps.v(), ones_rms.v(), sq4[:, t, :], t == 0, t == 3)
                            fw.act(rs.v(), ps.v(), AF.Ln, bias=RMS_EPS)
                            fw.act(rs.v(), rs.v(), AF.Exp, scale=-0.5)
                            for t in range(4):
                                fw.stt("dve", YLRU[:, t, i * NT:(i + 1) * NT], y4[t], vc("g_lru", t), rs.v(),
                                       ALU.mult, ALU.mult)

                        for i in range(4):
                            gens = [chain(i, t) for t in range(4)]
                            alive = True
                            while alive:
                                alive = False
                                for g in gens:
                                    if next(g, "done") != "done":
                                        alive = True
                            rms_lru(i)
                Hprev = fw.sbuf("Hprev", [128, 16, 2, 256], BF16)
                KTl = fw.sbuf("KTl", [128, 4, 8, 128], BF16, d_ph2)
                FB = fw.sbuf("FB", [128, 2, 8, 16, 32], BF16, d_ph2)
                wglu = fw.sbuf("wglu", [128, 4, 512], BF16, d_php)
                with fw.scope():
                    BcT = fw.sbuf("BcT", [128, 4, 8, 2, 128], BF16, d_ph)
                    COS = fw.sbuf("COS", [128, 16, 256], F32, d_ph)
                    SIN = fw.sbuf("SIN", [128, 16, 256], F32, d_ph)
                    RHO = fw.sbuf("RHO", [128, 16], F32, d_ph)
                    fw.dma("sp", BcT.v().re("p a b c d -> p (a b c d)"), sc_bct.v())
                    fw.dma("sp", COS.v().re("p a b -> p (a b)"), sc_cos.v())
                    fw.dma("sp", SIN.v().re("p a b -> p (a b)"), sc_sin.v())
                    fw.dma("sp", RHO.v(), sc_rho.v())
                    fw.dma("sp", KTl.v().re("p a b c -> p (a b c)"), sc_ktl.v())
                    fw.dma("sp", FB.v().re("p a b c d -> p (a b c d)"), sc_fb.v())
                    fw.dma("pool", wglu.v(), wglu_d.re("(kt p) n -> p kt n", p=128))
                    A_ = fw.sbuf("A_", [128, 2, 256], F32)
                    B_ = fw.sbuf("B_", [128, 2, 256], F32)
                    R_ = fw.sbuf("R_", [128, 2, 256], F32)
                    Qs = [fw.sbuf(f"Q_{i}", [128, 2, 256], F32) for i in range(2)]
                    A2s = [fw.sbuf(f"A2_{i}", [128, 2, 256], F32) for i in range(2)]
                    B2s = [fw.sbuf(f"B2_{i}", [128, 2, 256], F32) for i in range(2)]
                    fw.memset("dve", Hprev[:, :, :, 0:1], 0.0)
                    for q in range(16):
                        j, k = divmod(q, 4)
                        Q_ = Qs[q % 2]
                        ps = fw.bank()
                        sl_ = slice(32 * k, 32 * k + 32)
                        for ri in range(2):
                            for s_ in range(8):
                                fw.mm(ps[:, ri * 256:(ri + 1) * 256], BcT[sl_, j, 7 - s_, ri, :],
                                      V(Uj[j], U.t[sl_, j, s_ * 256:(s_ + 1) * 256]), s_ == 0, s_ == 7,
                                      tile_position=(32 * k, 0))
                        S = ps.v().re("p (r c) -> p r c", r=2)
                        cb_ = COS[:, q, :].un(1).bc([128, 2, 256])
                        sb_ = SIN[:, q, :].un(1).bc([128, 2, 256])
                        fw.tt("dve", A_.v(), S, cb_, ALU.mult)
                        fw.tt("dve", B_.v(), S, sb_, ALU.mult)
                        fw.tt("dve", R_[:, 0, :], A_[:, 0, :], B_[:, 1, :], ALU.add)
                        fw.tt("dve", R_[:, 1, :], A_[:, 1, :], B_[:, 0, :], ALU.subtract)
                        rb = RHO[:, q:q + 1].bc([128, 256])
                        fw.scan(Q_[:, 0, :], rb, R_[:, 0, :], 0.0)
                        fw.scan(Q_[:, 1, :], rb, R_[:, 1, :], 0.0)
                        A2, B2 = A2s[q % 2], B2s[q % 2]
                        fw.tt("pool", A2.v(), Q_.v(), cb_, ALU.mult)
                        fw.tt("pool", B2.v(), Q_.v(), sb_, ALU.mult)
                        fw.tt("pool", Hprev[:, q, 0, 1:256], A2[:, 0, 0:255], B2[:, 1, 0:255], ALU.subtract)
                        fw.tt("pool", Hprev[:, q, 1, 1:256], A2[:, 1, 0:255], B2[:, 0, 0:255], ALU.add)
                    if debug and s == 0:
                        fw.dma("sp", dbg["d_hprev"], Hprev.v(), dsem=d_dbg)
                with fw.scope():
                    gtm2s = [fw.sbuf(f"gtm2_{i}", [128, NT], F32) for i in range(4)]
                    if debug and s == 0:
                        for j in range(4):
                            fw.dma("sp", dbg["d_u"][:, j, :], V(Uj[j], U.t[:, j, :]), dsem=d_dbg)
                    for j in range(4):
                        bk = [fw.bank() for _ in range(4)]
                        Ujv = V(Uj[j], U.t[:, j, :]).re("p (t c) -> p t c", t=8)
                        for b in range(4):
                            for d_ in range(0, 2 * b + 2):
                                if 2 * b >= d_:
                                    fw.mm(bk[b].v(), KTl[:, j, d_, :],
                                          Ujv[:, 2 * b - d_:2 * b - d_ + 2, :].re("p t c -> p (t c)"), d_ == 0, False)
                                else:
                                    fw.mm(bk[b][:, 256:512], KTl[:, j, d_, :], Ujv[:, 0, :], False, False)
                            for k in range(4):
                                q = 4 * j + k
                                sl_ = slice(32 * k, 32 * k + 32)
                                for tl in range(2):
                                    tau = 2 * b + tl
                                    for ri in range(2):
                                        last = False
                                        fw.mm(bk[b][sl_, tl * 256:(tl + 1) * 256], FB[:, ri, tau, q, :],
                                              Hprev[:, q, ri, :], False, last, tile_position=(0, 32 * k))
                        for b in range(4):
                            fw.mm(bk[b].v(), zeros_bf.v(), Ujv[:, 0:2, :].re("p t c -> p (t c)"), False, True)
                        Ynat = V(Uj[j], U.t[:, j, :]).re("p (c t) -> p t c", t=8)
                        for b in range(4):
                            gelu_tanh(Ynat[:, 2 * b:2 * b + 2, :], bk[b].v().re("p (t c) -> p t c", t=2),
                                      gtm2s[b].v().re("p (t c) -> p t c", t=2))
                    if debug and s == 0:
                        for j in range(4):
                            fw.dma("sp", dbg["d_yg"][:, j, :], V(Uj[j], U.t[:, j, :]), dsem=d_dbg)
                    sgs = [fw.sbuf(f"sg{i}", [128, NT], F32) for i in range(2)]
                    y32s_ = [fw.sbuf(f"y32b{i}", [128, 4, NT], F32) for i in range(2)]
                    sq4s = [fw.sbuf(f"sq4b{i}", [128, 4, NT], BF16) for i in range(2)]
                    rss = [fw.sbuf(f"rsb{i}", [128, NT], F32) for i in range(2)]
                    for i in range(4):
                        tsl = slice(i * NT, (i + 1) * NT)
                        y32 = y32s_[i % 2]
                        for m in range(4):
                            sg = sgs[m % 2]
                            ps = fw.bank()
                            for kt in range(4):
                                fw.mm(ps.v(), wglu[:, kt, m * 128:(m + 1) * 128], V(Uj[kt], U.t[:, kt, tsl]),
                                      kt == 0, kt == 3)
                            fw.act(sg.v(), ps.v(), AF.Sigmoid, bias=vc("b_glu", m))
                            fw.tt("dve", y32[:, m, :], V(Uj[m], U.t[:, m, tsl]), sg.v(), ALU.mult)
                        rmsnorm_finish(y32.v(), "g_s5", YS5[:, :, tsl], sq4s[i % 2].v(), rss[i % 2].v())
                if debug and s == 0:
                    fw.dma("sp", dbg["d_ys5"], YS5.v(), dsem=d_dbg)
                    fw.dma("sp", dbg["d_ylru"], YLRU.v(), dsem=d_dbg)
            with fw.scope():
                AB = [fw.sbuf("fa0", [128, 8, NT], F32), fw.sbuf("fa1", [128, 8, NT], F32)]
                fb_ = fw.sbuf("fb", [128, 8, NT], F32, d_x)
                ba = fw.sbuf("ba", [128, 8, NT], BF16)
                bb = fw.sbuf("bb", [128, 8, NT], BF16)
                h2b = fw.sbuf("h2b", [128, 8, NT], BF16)
                ffb = fw.sbuf("ffb", [128, 32, NT], BF16)
                tmx = [fw.sbuf("tmxA", [128, NT], F32).v(), fw.sbuf("tmxB", [128, NT], F32).v()]
                tmy = [fw.sbuf("tmyA", [128, NT], F32).v(), fw.sbuf("tmyB", [128, NT], F32).v()]
                tmz = [fw.sbuf("tmzA", [128, NT], F32).v(), fw.sbuf("tmzB", [128, NT], F32).v()]
                Es = [fw.sbuf(f"E{i}", [128, 2, NT], BF16) for i in range(2)]
                rdens = [fw.sbuf(f"rden{i}", [128, NT], F32) for i in range(2)]
                rl = fw.sbuf("rl", [128, NT], F32)

                def X_gen(i):
                    fa = AB[i % 2]
                    tok0 = s * SEQ + i * NT
                    tsl = slice(i * NT, (i + 1) * NT)
                    fw.dma("sp", fb_.v(), xT_v[:, :, tok0:tok0 + NT])
                    layernorm(fb_.v(), "ln_in_g", "ln_in_b", NT, ba.v(), bb.v(), tmz)
                    yield
                    yield from linear_g(wb_mix, 2, 4, 8,
                                        lambda kt: (YS5[:, kt, tsl] if kt < 4 else YLRU[:, kt - 4, tsl]),
                                        lambda m, ps: fw.stt("dve", fb_[:, m, :], fb_[:, m, :], ALPHA, ps, ALU.mult, ALU.add))
                    layernorm(fb_.v(), "ln1_g", "ln1_b", NT, ba.v(), bb.v(), tmx, out_bf=ba.v())
                    if debug and s == 0 and i == 0:
                        fw.dma("sp", dbg["d_h1"], fb_.v(), dsem=d_dbg)
                    yield
                    yield from linear_g(wb_q, 2, 4, 8, lambda kt: ba[:, kt, :],
                                        lambda m, ps: fw.act(bb[:, m, :], ps, AF.Copy, scale=0.0625))
                    for hp in range(2):
                        heads = (2 * hp, 2 * hp + 1)
                        pss = {}
                        for hd in heads:
                            for mt in range(2):
                                ps = fw.bank()
                                for kk in range(2):
                                    fw.mm(ps.v(), KTs[:, 2 * hd + kk, s * MEM + mt * 128:s * MEM + (mt + 1) * 128],
                                          bb[:, 2 * hd + kk, :], kk == 0, kk == 1)
                                pss[(hd, mt)] = ps
                        for hd in heads:
                            for mt in range(2):
                                fw.act(Es[hd % 2][:, mt, :], pss[(hd, mt)].v(), AF.Exp)
                        for hd in heads:
                            E, rden = Es[hd % 2], rdens[hd % 2]
                            pd = fw.bank()
                            for mt in range(2):
                                fw.mm(pd.v(), ones_1.v(), E[:, mt, :], mt == 0, mt == 1)
                            fw.act(rden.v(), pd.v(), AF.Ln)
                            fw.act(rden.v(), rden.v(), AF.Exp, scale=-1.0)
                            for dt in range(2):
                                po = fw.bank()
                                for mt in range(2):
                                    fw.mm(po.v(), Vs[:, s * 2 + mt, (2 * hd + dt) * 128:(2 * hd + dt + 1) * 128],
                                          E[:, mt, :], mt == 0, mt == 1)
                                fw.tt("dve", ba[:, 2 * hd + dt, :], po.v(), rden.v(), ALU.mult)
                        yield
                    yield from linear_g(wb_o, 2, 4, 8, lambda kt: ba[:, kt, :],
                                        lambda m, ps: fw.stt("dve", fa[:, m, :], fb_[:, m, :], ALPHA, ps, ALU.mult, ALU.add))
                    layernorm(fa.v(), "ln2_g", "ln2_b", NT, h2b.v(), bb.v(), tmx, out_bf=h2b.v())
                    if debug and s == 0 and i == 0:
                        fw.dma("sp", dbg["d_h2"], fa.v(), dsem=d_dbg)
                    yield

                def Y_gen(i):
                    h2 = AB[i % 2]
                    tok0 = s * SEQ + i * NT

                    def epiF(m, ps):
                        fw.act(rl.v(), ps, AF.Relu)
                        if m % 2 == 0:
                            fw.act(ffb[:, m, :], rl.v(), AF.Square)
                        else:
                            fw.tt("dve", ffb[:, m, :], rl.v(), rl.v(), ALU.mult)

                    yield from linear_g(wb_ff1, 8, 4, 8, lambda kt: h2b[:, kt, :], epiF)
                    yield from linear_g(wb_ff2, 8, 1, 32, lambda kt: ffb[:, kt, :],
                                        lambda m, ps: fw.stt("dve", h2[:, m, :], h2[:, m, :], ALPHA, ps, ALU.mult, ALU.add))
                    layernorm(h2.v(), "ln3_g", "ln3_b", NT, ffb[:, 0:8, :], ffb[:, 8:16, :], tmy)
                    fw.dma("sp", outT_v[:, :, tok0:tok0 + NT], h2.v(), dsem=d_out)
                    yield

                SCHED0 = "y x y y x x x y y y y x y x y x y x y x y x y x y x x n y n y n y"
                SCHED1 = "y x y y y y y x y x y x y x y x y x x y y y n y n y n y"
                gens = {i: X_gen(i) for i in range(4)}
                for _ in gens[0]:
                    pass
                for _ in range(3):
                    next(gens[1], None)
                for i in range(4):
                    gy = Y_gen(i)
                    gx = gens.get(i + 1)
                    gn = gens.get(i + 2)
                    for c in SCHED1.split():
                        if c == "y":
                            next(gy, None)
                        elif c == "x":
                            if gx is not None:
                                next(gx, None)
                        elif gn is not None:
                            next(gn, None)
                    for _ in gy:
                        pass
                    if gx is not None:
                        for _ in gx:
                            pass

    fw.dry = True
    emit()
    fw.dry = False
    ws.reset_for_real()
    fw.bank_i = 0
    emit()
    fw._wait("sp", ("dma", d_out, d_out.count))
    if debug:
        fw._wait("sp", ("dma", d_dbg, d_dbg.count))
    fw.barrier()
    fw.root.close()
    return nc, fw


def _blk(W, G):
    K, N = W.shape
    KT, MT = K // 128, N // 128
    a = W.reshape(KT, 128, MT // G, G, 128).transpose(2, 1, 3, 0, 4)
    return np.ascontiguousarray(a.reshape(MT // G, 128, G * KT * 128)).astype(np.float32)


def host_layout(inp):
    f = lambda k: np.asarray(inp[k], dtype=np.float32)
    shared = {}
    shared["wb_in"] = _blk(f("w_in")[0], 4)
    shared["wb_mix"] = _blk(f("w_mix_out")[0], 4)
    shared["wb_q"] = _blk(f("w_q")[0], 4)
    shared["wb_k"] = _blk(f("w_k")[0], 4)
    shared["wb_o"] = _blk(f("w_o")[0], 4)
    shared["wb_ff1"] = _blk(f("w_ff1")[0], 4)
    shared["wb_ff2"] = _blk(f("w_ff2")[0], 1)
    shared["w_v"] = np.ascontiguousarray(f("w_v")[0])
    shared["w_glu"] = np.ascontiguousarray(f("s5_w_glu")[0])
    shared["ident"] = np.eye(128, dtype=np.float32)
    vecs = np.zeros((128, NV), np.float32)

    def put8(name, v):
        vecs[:, VC[name]:VC[name] + 8] = v.reshape(8, 128).T

    def put4(name, v):
        vecs[:, VC[name]:VC[name] + 4] = v.reshape(4, 128).T

    put8("ln_in_g", f("ln_in_g")); put8("ln_in_b", f("ln_in_b"))
    put8("ln1_g", f("ln1_g")[0]); put8("ln1_b", f("ln1_b")[0])
    put8("ln2_g", f("ln2_g")[0]); put8("ln2_b", f("ln2_b")[0])
    put8("ln3_g", f("ln3_g")[0]); put8("ln3_b", f("ln3_b")[0])
    put8("mem_g", f("mem_ln_g")[0]); put8("mem_b", f("mem_ln_b")[0])
    put4("b_glu", f("s5_b_glu")[0]); put4("g_s5", f("g_s5")[0]); put4("g_lru", f("g_lru")[0])
    put4("conv_b", f("conv_b")[0]); put4("b_a", f("lru_b_a")[0].reshape(512)); put4("b_x", f("lru_b_x")[0].reshape(512))
    put4("lam", f("lru_lambda")[0]); put4("s5_d", f("s5_d")[0].reshape(512))
    cw = f("conv_w")[0]
    vecs[:, VC["conv_w"]:VC["conv_w"] + 16] = cw.reshape(4, 4, 128).transpose(2, 1, 0).reshape(128, 16)
    shared["vecs"] = vecs
    def sl16(a):
        return a.reshape(16, 2, 64).transpose(1, 2, 0).reshape(128, 16)

    a_re, a_im = f("s5_a_re")[0], f("s5_a_im")[0]
    logdt = np.repeat(f("s5_log_dt")[0][:, None], 64, axis=1)
    shared["s5sl"] = np.ascontiguousarray(np.stack([sl16(a_re), sl16(a_im), sl16(logdt)], axis=1))

    def bpad(b):
        o = np.zeros((2, 64, 16, 2, 16), np.float32)
        bb = b.reshape(16, 2, 64, 16)
        for g2 in range(2):
            o[g2, :, :, g2, :] = bb[:, g2].transpose(1, 0, 2)
        return o.reshape(128, 16, 32)

    def cpad(c):
        return bpad(c.transpose(0, 2, 1))

    shared["bsl"] = np.ascontiguousarray(np.stack([bpad(f("s5_b_re")[0]), bpad(f("s5_b_im")[0])], axis=1))
    shared["csl"] = np.ascontiguousarray(np.stack([cpad(f("s5_c_re")[0]), cpad(f("s5_c_im")[0])], axis=1))

    def bd(w):
        o = np.zeros((2, 64, 4, 2, 64), np.float32)
        ww = w.reshape(4, 2, 64, 64)
        for hh in range(2):
            o[hh, :, :, hh, :] = ww[:, hh].transpose(1, 0, 2)
        return o.reshape(128, 4, 128)

    shared["wab"] = np.ascontiguousarray(np.stack([bd(f("lru_w_a")[0]), bd(f("lru_w_x")[0])], axis=1))
    x = f("x")
    mem = f("mem")
    per_core = []
    for c in range(NCORES):
        d = dict(shared)
        d["xT"] = np.ascontiguousarray(x[2 * c:2 * c + 2].reshape(NTOK, D).T)
        d["memT"] = np.ascontiguousarray(mem[2 * c:2 * c + 2].reshape(2 * MEM, D).T)
        per_core.append(d)
    return per_core


_CACHE = {}


def kernel(**inputs):
    if "nc" not in _CACHE:
        _CACHE["nc"] = build_program(debug=False)[0]
    nc = _CACHE["nc"]
    in_maps = host_layout(inputs)
    res = run_bass_kernel_spmd(nc, in_maps, core_ids=list(range(NCORES)))
    out = np.empty((16, SEQ, D), np.float32)
    for c in range(NCORES):
        o = res.results[c]["outT"]
        out[2 * c:2 * c + 2] = o.T.reshape(2, SEQ, D)
    return out
```

```python
import contextlib
import math
import numpy as np
import concourse.bass as bass
import concourse.mybir as mybir
from concourse.bass_utils import run_bass_kernel_spmd

F32 = mybir.dt.float32
BF16 = mybir.dt.bfloat16
AF = mybir.ActivationFunctionType
ALU = mybir.AluOpType

NCORES = 8
D = 1024
SEQ = 2048
NTOK = 4096
MEM = 256
DFF = 4096
ALPHA = 2.0 ** 0.25
LN_EPS = 1e-5
RMS_EPS = 1e-6
NT = 512
PI = math.pi

VC = {}
_o = 0
for _n in ["ln_in_g", "ln_in_b", "ln1_g", "ln1_b", "ln2_g", "ln2_b", "ln3_g", "ln3_b", "mem_g", "mem_b"]:
    VC[_n] = _o
    _o += 8
for _n in ["b_glu", "g_s5", "g_lru", "conv_b", "b_a", "b_x", "lam"]:
    VC[_n] = _o
    _o += 4
VC["conv_w"] = _o
_o += 16
VC["s5_d"] = _o
_o += 4
NV = _o


class DSem:
    def __init__(self, fw, name):
        self.h = fw.root.enter_context(fw.nc.semaphore(name))
        self.count = 0
        self.name = name


class Buf:
    def __init__(self, name, t=None, dsem=None):
        self.name = name
        self.t = t
        self.w = None
        self.r = []
        self.dsem = dsem

    def v(self):
        return V(self, self.t[:])

    def __getitem__(self, k):
        return V(self, self.t[k])


class V:
    def __init__(self, buf, ap):
        self.buf = buf
        self.ap = ap

    def __getitem__(self, k):
        return V(self.buf, self.ap[k])

    def re(self, pat, **kw):
        return V(self.buf, self.ap.rearrange(pat, **kw))

    def bc(self, shape):
        return V(self.buf, self.ap.to_broadcast(list(shape)))

    def un(self, axis):
        return V(self.buf, self.ap.unsqueeze(axis))


COMPUTE = ("pe", "act", "dve", "pool")


def _bufs(*vs):
    out = []
    for v in vs:
        if isinstance(v, V) and v.buf is not None and v.buf not in out:
            out.append(v.buf)
    return out


def _ap(x):
    return x.ap if isinstance(x, V) else x


class FW:
    def __init__(self, nc, same_engine_sync=True):
        self.nc = nc
        self.root = contextlib.ExitStack()
        self.es = self.root
        self.eng = {"pe": nc.tensor, "act": nc.scalar, "dve": nc.vector, "pool": nc.gpsimd, "sp": nc.sync}
        self.sem = {}
        self.cnt = {}
        for e in COMPUTE:
            self.sem[e] = self.root.enter_context(nc.semaphore("sem_" + e))
            self.cnt[e] = 0
        self.known = {e: {} for e in self.eng}
        self.same_engine_sync = same_engine_sync
        self.dsems = []
        self.dry = False
        self.uid = 0
        self.banks = []
        self.bank_i = 0
        self.ninst = 0

    def _name(self, name):
        self.uid += 1
        return f"{name}_{self.uid}"

    def sbuf(self, name, shape, dt, dsem=None):
        t = self.es.enter_context(self.nc.sbuf_tensor(self._name(name), list(shape), dt))
        return Buf(name, t, dsem)

    def psum(self, name, shape, dt):
        t = self.es.enter_context(self.nc.psum_tensor(self._name(name), list(shape), dt))
        return Buf(name, t)

    def dsem(self, name):
        d = DSem(self, name)
        self.dsems.append(d)
        return d

    def sub(self, parent, name):
        return Buf(name, parent.t, parent.dsem)

    @contextlib.contextmanager
    def scope(self):
        outer = self.es
        self.es = contextlib.ExitStack()
        try:
            yield
        finally:
            self.barrier()
            self.es.close()
            self.es = outer

    def bank(self):
        b = self.banks[self.bank_i % len(self.banks)]
        self.bank_i += 1
        return b

    def _wait(self, e, dep):
        kind, key, val = dep
        if kind == "eng":
            if key == e and (e == "pe" or not self.same_engine_sync):
                return
            kk = ("eng", key)
            if self.known[e].get(kk, 0) >= val:
                return
            self.eng[e].wait_ge(self.sem[key], val)
            self.known[e][kk] = val
        else:
            val = key.count
            kk = ("dma", key.name)
            if self.known[e].get(kk, 0) >= val:
                return
            self.eng[e].wait_ge(key.h, val)
            self.known[e][kk] = val
        self.ninst += 1

    def _deps(self, e, reads, writes):
        for b in reads:
            if b.w is not None:
                self._wait(e, b.w)
        for b in writes:
            if b.w is not None:
                self._wait(e, b.w)
            for d in b.r:
                self._wait(e, d)

    def op(self, e, reads, writes, fn):
        if self.dry:
            return None
        self._deps(e, reads, writes)
        ins = fn(self.eng[e])
        self.cnt[e] += 1
        self.ninst += 1
        ins.then_inc(self.sem[e], 1)
        tag = ("eng", e, self.cnt[e])
        for b in reads:
            if b not in writes:
                b.r.append(tag)
        for b in writes:
            b.w = tag
            b.r = []
        return ins

    def mm(self, out, lhsT, rhs, start, stop, **kw):
        if self.dry:
            return None
        e = "pe"
        psb = out.buf
        rd = _bufs(lhsT, rhs)
        self._deps(e, rd, [psb] if start else [])
        ins = self.nc.tensor.matmul(out.ap, lhsT.ap, rhs.ap, start=start, stop=stop, **kw)
        self.ninst += 1
        if stop:
            self.cnt[e] += 1
            ins.then_inc(self.sem[e], 1)
            tag = ("eng", e, self.cnt[e])
        else:
            tag = ("eng", e, self.cnt[e] + 1)
        for b in rd:
            b.r.append(tag)
        psb.w = tag
        if start:
            psb.r = []
        return ins

    def transpose(self, out, in_, ident):
        if self.dry:
            return None
        e = "pe"
        rd = _bufs(in_, ident)
        self._deps(e, rd, [out.buf])
        ins = self.nc.tensor.transpose(out.ap, in_.ap, ident.ap)
        self.ninst += 1
        self.cnt[e] += 1
        ins.then_inc(self.sem[e], 1)
        tag = ("eng", e, self.cnt[e])
        for b in rd:
            b.r.append(tag)
        out.buf.w = tag
        out.buf.r = []
        return ins

    def dma(self, q, out, in_, dsem=None, **kw):
        if self.dry:
            return None
        reads = _bufs(in_)
        writes = _bufs(out)
        self._deps(q, reads, writes)
        if dsem is None:
            for b in writes + reads:
                if b.dsem is not None:
                    dsem = b.dsem
                    break
        assert dsem is not None, "dma needs a DSem"
        ins = self.eng[q].dma_start(out=out.ap, in_=in_.ap, **kw)
        self.ninst += 1
        dsem.count += 16
        ins.then_inc(dsem.h, 16)
        tag = ("dma", dsem, dsem.count)
        for b in reads:
            b.r.append(tag)
        for b in writes:
            b.w = tag
            b.r = []
        return ins

    def barrier(self, engines=("pe", "act", "dve", "pool", "sp")):
        if self.dry:
            return
        for e in engines:
            for f in COMPUTE:
                if f != e and self.cnt[f] > 0:
                    self._wait(e, ("eng", f, self.cnt[f]))
            for d in self.dsems:
                if d.count > 0:
                    self._wait(e, ("dma", d, d.count))

    def tt(self, e, out, a, b, op):
        return self.op(e, _bufs(a, b), _bufs(out),
                       lambda E: E.tensor_tensor(out=out.ap, in0=a.ap, in1=b.ap, op=op))

    def ts(self, e, out, a, s1, op0, s2=None, op1=None):
        kw = {}
        if op1 is not None:
            kw["op1"] = op1
        return self.op(e, _bufs(a, s1, s2), _bufs(out),
                       lambda E: E.tensor_scalar(out=out.ap, in0=a.ap, scalar1=_ap(s1), scalar2=_ap(s2), op0=op0, **kw))

    def stt(self, e, out, in0, scalar, in1, op0, op1):
        return self.op(e, _bufs(in0, scalar, in1), _bufs(out),
                       lambda E: E.scalar_tensor_tensor(out=out.ap, in0=in0.ap, scalar=_ap(scalar), in1=in1.ap,
                                                        op0=op0, op1=op1))

    def act(self, out, in_, func, bias=None, scale=None):
        kw = {}
        if bias is not None:
            kw["bias"] = _ap(bias)
        if scale is not None:
            kw["scale"] = _ap(scale)
        return self.op("act", _bufs(in_, bias, scale), _bufs(out),
                       lambda E: E.activation(out=out.ap, in_=in_.ap, func=func, **kw))

    def cp(self, e, out, in_):
        if e == "act":
            return self.act(out, in_, AF.Copy)
        return self.op(e, _bufs(in_), _bufs(out), lambda E: E.tensor_copy(out=out.ap, in_=in_.ap))

    def memset(self, e, out, val):
        return self.op(e, [], _bufs(out), lambda E: E.memset(out.ap, val))

    def scan(self, out, d0, d1, init, op0=ALU.mult, op1=ALU.add):
        return self.op("dve", _bufs(d0, d1, init), _bufs(out),
                       lambda E: E.tensor_tensor_scan(out=out.ap, data0=d0.ap, data1=d1.ap, initial=_ap(init),
                                                      op0=op0, op1=op1))

    def recip(self, out, in_):
        return self.op("dve", _bufs(in_), _bufs(out), lambda E: E.reciprocal(out=out.ap, in_=in_.ap))


class WS:
    def __init__(self, fw, nslots, elems):
        self.fw = fw
        self.n = nslots
        self.slots = [fw.sbuf(f"ring{i}", [128, elems], BF16, fw.dsem(f"dring{i}")) for i in range(nslots)]
        self.plan = []
        self.pos = 0
        self.issued = 0

    def reset_for_real(self):
        self.pos = 0
        self.issued = 0

    def get(self, dram_v, n):
        fw = self.fw
        if fw.dry:
            self.plan.append((dram_v, n))
            self.pos += 1
            return self.slots[(self.pos - 1) % self.n].v()
        i = self.pos
        self.pos += 1
        lim = min(len(self.plan), i + self.n)
        while self.issued < lim:
            j = self.issued
            dv, nn = self.plan[j]
            fw.dma("pool", self.slots[j % self.n][:, 0:nn], dv)
            self.issued += 1
        return self.slots[i % self.n].v()


def build_program(debug=False):
    nc = bass.Bass("TRN2", target_bir_lowering=False)
    fw = FW(nc)

    def din(name, shape, dt=F32):
        return V(None, nc.dram_tensor(name, list(shape), dt, kind="ExternalInput").ap())

    xT = din("xT", [D, NTOK])
    memT = din("memT", [D, 2 * MEM])
    vecs_d = din("vecs", [128, NV])
    ident_d = din("ident", [128, 128])
    s5sl_d = din("s5sl", [128, 3, 16])
    bsl_d = din("bsl", [128, 2, 16, 32])
    csl_d = din("csl", [128, 2, 16, 32])
    wab_d = din("wab", [128, 2, 4, 128])
    wb_in = din("wb_in", [3, 128, 4096])
    wb_mix = din("wb_mix", [2, 128, 4096])
    wb_q = din("wb_q", [2, 128, 4096])
    wb_k = din("wb_k", [2, 128, 4096])
    wb_o = din("wb_o", [2, 128, 4096])
    wb_ff1 = din("wb_ff1", [8, 128, 4096])
    wb_ff2 = din("wb_ff2", [8, 128, 4096])
    wv_d = din("w_v", [D, D])
    wglu_d = din("w_glu", [512, 512])
    outT = V(None, nc.dram_tensor("outT", [D, NTOK], F32, kind="ExternalOutput").ap())
    dbg = {}
    if debug:
        for nm, shp, dt_ in [("d_ys5", [128, 4, SEQ], BF16), ("d_ylru", [128, 4, SEQ], BF16), ("d_u", [128, 4, SEQ], BF16),
                        ("d_h1", [128, 8, NT], F32), ("d_h2", [128, 8, NT], F32), ("d_hprev", [128, 16, 2, 256], BF16),
                        ("d_yg", [128, 4, SEQ], BF16)]:
            dbg[nm] = V(None, nc.dram_tensor(nm, shp, dt_, kind="ExternalOutput").ap())

    xT_v = xT.re("(kt p) n -> p kt n", p=128)
    memT_v = memT.re("(kt p) n -> p kt n", p=128)
    outT_v = outT.re("(kt p) n -> p kt n", p=128)

    def dscratch(name, shape, dt):
        return Buf(name, nc.dram_tensor(name, list(shape), dt).ap())

    sc_ktl = dscratch("sc_ktl", [128, 4 * 8 * 128], BF16)
    sc_bct = dscratch("sc_bct", [128, 4 * 8 * 2 * 128], BF16)
    sc_fb = dscratch("sc_fb", [128, 2 * 8 * 16 * 32], BF16)
    sc_cos = dscratch("sc_cos", [128, 16 * 256], F32)
    sc_sin = dscratch("sc_sin", [128, 16 * 256], F32)
    sc_rho = dscratch("sc_rho", [128, 16], F32)

    d_init = fw.dsem("d_init")
    d_ph = fw.dsem("d_ph")
    d_ph2 = fw.dsem("d_ph2")
    d_initp = fw.dsem("d_initp")
    d_php = fw.dsem("d_php")
    d_x = fw.dsem("d_x")
    d_x2 = fw.dsem("d_x2")
    d_out = fw.dsem("d_out")
    d_sv = fw.dsem("d_sv")
    d_dbg = fw.dsem("d_dbg")
    for b_ in (sc_ktl, sc_bct, sc_fb, sc_cos, sc_sin, sc_rho):
        b_.dsem = d_sv

    fw.banks = [fw.psum(f"bank{i}", [128, 512], F32) for i in range(8)]
    vecs = fw.sbuf("vecs", [128, NV], F32, d_init)
    ones_ln = fw.sbuf("ones_ln", [128, 128], BF16)
    ones_rms = fw.sbuf("ones_rms", [128, 128], BF16)
    ones_1 = fw.sbuf("ones_1", [128, 128], BF16)
    zeros_bf = fw.sbuf("zeros_bf", [128, 128], BF16)
    cneg = fw.sbuf("cneg", [128, 4], F32)
    cneg2 = fw.sbuf("cneg2", [128, 4], F32)
    KTs = fw.sbuf("KTs", [128, 8, 2 * MEM], BF16)
    Vs = fw.sbuf("Vs", [128, 4, D], BF16)
    YS5 = fw.sbuf("YS5", [128, 4, SEQ], BF16)
    YLRU = fw.sbuf("YLRU", [128, 4, SEQ], BF16)
    ws = WS(fw, 4, 4096)

    def vc(name, i):
        c = VC[name] + i
        return vecs[:, c:c + 1]

    def layernorm_g(r, gname, bname, N, scr_xb, scr_sq, tm, out_bf=None, keep_f32=True, dst_f32=None):
        if dst_f32 is None:
            dst_f32 = r
        fw.cp("dve", scr_xb, r)
        fw.act(scr_sq, r, AF.Square)
        yield
        pm = fw.bank()
        pq = fw.bank()
        for kt in range(8):
            fw.mm(pm[:, 0:N], ones_ln.v(), scr_xb[:, kt, :], kt == 0, kt == 7)
        for kt in range(8):
            fw.mm(pq[:, 0:N], ones_ln.v(), scr_sq[:, kt, :], kt == 0, kt == 7)
        mean, t = tm
        fw.act(t, pm[:, 0:N], AF.Square)
        fw.act(mean, pm[:, 0:N], AF.Copy)
        fw.stt("dve", t, pq[:, 0:N], LN_EPS, t, ALU.add, ALU.subtract)
        fw.act(t, t, AF.Ln)
        fw.act(t, t, AF.Exp, scale=-0.5)
        fw.tt("dve", r, r, mean.un(1).bc([128, 8, N]), ALU.subtract)
        fw.tt("dve", r, r, t.un(1).bc([128, 8, N]), ALU.mult)
        yield
        for kt in range(8):
            if keep_f32:
                fw.act(dst_f32[:, kt, :], r[:, kt, :], AF.Identity, bias=vc(bname, kt), scale=vc(gname, kt))
            else:
                fw.act(out_bf[:, kt, :], r[:, kt, :], AF.Identity, bias=vc(bname, kt), scale=vc(gname, kt))
        if keep_f32 and out_bf is not None:
            fw.cp("dve", out_bf, dst_f32)
        yield

    def layernorm(*a, **kw):
        for _ in layernorm_g(*a, **kw):
            pass

    GC1 = 2.0 * math.sqrt(2.0 / math.pi)

    def gelu_tanh(dst, ps, tmp):
        fw.act(tmp, ps, AF.Square)
        fw.stt("dve", tmp, tmp, 1.0 / 0.044715, ps, ALU.add, ALU.mult)
        fw.act(tmp, tmp, AF.Sigmoid, scale=GC1 * 0.044715)
        fw.tt("dve", dst, tmp, ps, ALU.mult)

    def rmsnorm_finish(y32, gname, out_bf, sq4, rs):
        fw.act(sq4, y32, AF.Square)
        ps = fw.bank()
        for t in range(4):
            fw.mm(ps.v(), ones_rms.v(), sq4[:, t, :], t == 0, t == 3)
        fw.act(rs, ps.v(), AF.Ln, bias=RMS_EPS)
        fw.act(rs, rs, AF.Exp, scale=-0.5)
        for t in range(4):
            fw.stt("dve", out_bf[:, t, :], y32[:, t, :], vc(gname, t), rs, ALU.mult, ALU.mult)

    def linear_g(wb, nslots, G, KT, rhs_fn, epi, N=NT):
        for s_ in range(nslots):
            slot = ws.get(wb[s_], G * KT * 128)
            sv = slot.re("p (g kt c) -> p g kt c", g=G, kt=KT)
            for g in range(G):
                m = s_ * G + g
                ps = fw.bank()
                for kt in range(KT):
                    fw.mm(ps[:, 0:N], sv[:, g, kt, :], rhs_fn(kt), kt == 0, kt == KT - 1)
                epi(m, ps[:, 0:N])
            yield

    def linear(*a, **kw):
        for _ in linear_g(*a, **kw):
            pass

    def interleave(gy, gx, ny, nx):
        xi = 0
        for yi in range(ny):
            next(gy, None)
            tgt = ((yi + 1) * nx) // ny if gx is not None else 0
            while xi < tgt:
                next(gx, None)
                xi += 1
        for _ in gy:
            pass
        if gx is not None:
            for _ in gx:
                pass

    def emit():
        fw.dma("sp", vecs.v(), vecs_d)
        fw.memset("dve", ones_ln.v(), 1.0 / 1024.0)
        fw.memset("dve", ones_rms.v(), 1.0 / 512.0)
        fw.memset("dve", ones_1.v(), 1.0)
        fw.memset("dve", zeros_bf.v(), 0.0)
        lamv = vecs[:, VC["lam"]:VC["lam"] + 4]
        fw.act(cneg.v(), lamv, AF.Exp, scale=-1.0)
        fw.ts("dve", cneg.v(), cneg.v(), 1.0, ALU.add)
        fw.act(cneg.v(), cneg.v(), AF.Ln)
        fw.ts("dve", cneg.v(), cneg.v(), -8.0, ALU.mult)
        fw.ts("dve", cneg2.v(), cneg.v(), 2.0, ALU.mult)

        with fw.scope():
            ident = fw.sbuf("ident", [128, 128], F32, d_ph)
            sl = fw.sbuf("sl", [128, 3, 16], F32, d_ph)
            bsl = fw.sbuf("bsl", [128, 2, 16, 32], F32, d_ph)
            csl = fw.sbuf("csl", [128, 2, 16, 32], F32, d_ph)
            fw.dma("sp", ident.v(), ident_d)
            fw.dma("sp", sl.v(), s5sl_d)
            fw.dma("sp", bsl.v(), bsl_d)
            fw.dma("sp", csl.v(), csl_d)
            SM = fw.sbuf("SM", [128, 48, 16], F32)
            _k = [0]

            def sm():
                _k[0] += 1
                return SM[:, _k[0] - 1, :]

            a_re, a_im, logdt = sl[:, 0, :], sl[:, 1, :], sl[:, 2, :]
            dt_, are, ang, mag = sm(), sm(), sm(), sm()
            fw.act(dt_, logdt, AF.Exp)
            fw.tt("dve", are, dt_, a_re, ALU.mult)
            fw.tt("dve", ang, dt_, a_im, ALU.mult)
            fw.act(mag, are, AF.Exp)
            tmp, tmp2 = sm(), sm()

            TWO_PI_HI = float(np.float32(2.0 * PI))
            TWO_PI_LO = float(np.float32(2.0 * PI - TWO_PI_HI))
            HALF_PI_HI = float(np.float32(PI / 2))
            HALF_PI_LO = float(np.float32(PI / 2 - HALF_PI_HI))

            def sin_reduced(dst, src):
                red, kf = sm(), sm()
                fw.ts("dve", kf, src, PI, ALU.is_gt)
                for k in range(1, 6):
                    fw.ts("dve", tmp, src, (2 * k + 1) * PI, ALU.is_gt)
                    fw.tt("dve", kf, kf, tmp, ALU.add)
                fw.stt("dve", red, kf, -TWO_PI_HI, src, ALU.mult, ALU.add)
                fw.stt("dve", red, kf, -TWO_PI_LO, red, ALU.mult, ALU.add)
                fw.act(dst, red, AF.Sin)

            sn, cs, angc = sm(), sm(), sm()
            sin_reduced(sn, ang)
            fw.ts("dve", angc, ang, HALF_PI_HI, ALU.add, HALF_PI_LO, ALU.add)
            sin_reduced(cs, angc)
            lr, li = sm(), sm()
            fw.tt("dve", lr, mag, cs, ALU.mult)
            fw.tt("dve", li, mag, sn, ALU.mult)
            lm1, den, qre, qim = sm(), sm(), sm(), sm()
            fw.ts("dve", lm1, lr, -1.0, ALU.add)
            fw.tt("dve", den, a_re, a_re, ALU.mult)
            fw.tt("dve", tmp, a_im, a_im, ALU.mult)
            fw.tt("dve", den, den, tmp, ALU.add)
            fw.recip(den, den)
            fw.tt("dve", qre, lm1, a_re, ALU.mult)
            fw.tt("dve", tmp, li, a_im, ALU.mult)
            fw.tt("dve", qre, qre, tmp, ALU.add)
            fw.tt("dve", qre, qre, den, ALU.mult)
            fw.tt("dve", qim, li, a_re, ALU.mult)
            fw.tt("dve", tmp, lm1, a_im, ALU.mult)
            fw.tt("dve", qim, qim, tmp, ALU.subtract)
            fw.tt("dve", qim, qim, den, ALU.mult)

            def b3(v):
                return v.un(2).bc([128, 16, 32])

            with fw.scope():
                Gb = [fw.sbuf(f"G{i}", [128, 2, 16, 32], F32) for i in range(2)]
                Fb = [fw.sbuf(f"F{i}", [128, 2, 16, 32], F32) for i in range(2)]
                F0 = fw.sbuf("F0", [128, 2, 16, 32], F32)
                T1 = fw.sbuf("T1", [128, 16, 32], F32)
                T2 = fw.sbuf("T2", [128, 16, 32], F32)
                KTl = fw.sbuf("KTl", [128, 4, 8, 128], BF16)
                BcT = fw.sbuf("BcT", [128, 4, 8, 2, 128], BF16)
                FB = fw.sbuf("FB", [128, 2, 8, 16, 32], BF16)
                fw.memset("dve", KTl.v(), 0.0)

                T3 = fw.sbuf("T3", [128, 16, 32], F32)
                T4 = fw.sbuf("T4", [128, 16, 32], F32)

                def cmul(dst, src, xr, xi, conj=False, e="dve"):
                    Ta, Tb = (T1, T2) if e == "dve" else (T3, T4)
                    sre, sim = src[:, 0], src[:, 1]
                    fw.tt(e, Ta.v(), sre, b3(xr), ALU.mult)
                    fw.tt(e, Tb.v(), sim, b3(xi), ALU.mult)
                    fw.tt(e, dst[:, 0], Ta.v(), Tb.v(), ALU.add if conj else ALU.subtract)
                    fw.tt(e, Ta.v(), sre, b3(xi), ALU.mult)
                    fw.tt(e, Tb.v(), sim, b3(xr), ALU.mult)
                    fw.tt(e, dst[:, 1], Tb.v(), Ta.v(), ALU.subtract if conj else ALU.add)

                lrp = fw.sbuf("lrp", [128, 2, 16], F32)
                fw.cp("dve", lrp[:, 0, :], lr)
                fw.cp("dve", lrp[:, 1, :], li)
                fw.cp("dve", F0[:, 0], csl[:, 0])
                fw.ts("dve", F0[:, 1], csl[:, 1], -1.0, ALU.mult)
                cmul(Gb[0].v(), bsl.v(), qre, qim)
                dvec = vecs[:, VC["s5_d"]:VC["s5_d"] + 4]
                for n in range(8):
                    G = Gb[n % 2]
                    for j in range(4):
                        kb = fw.bank()
                        for k in range(4):
                            q = 4 * j + k
                            for ri in range(2):
                                fw.mm(kb[32 * k:32 * k + 32, 32 * k:32 * k + 32], G[:, ri, q, :], F0[:, ri, q, :],
                                      ri == 0, ri == 1, tile_position=(0, 32 * k))
                        for k in range(4):
                            sl_ = slice(32 * k, 32 * k + 32)
                            if n == 0:
                                fw.stt("dve", KTl[sl_, j, 0, sl_], ident[sl_, sl_], dvec[sl_, j:j + 1], kb[sl_, sl_],
                                       ALU.mult, ALU.add)
                            else:
                                fw.cp("act", KTl[sl_, j, n, sl_], kb[sl_, sl_])
                    for j in range(4):
                        tb = fw.bank()
                        for ri in range(2):
                            fw.transpose(tb[:, ri * 128:(ri + 1) * 128], G[:, ri, 4 * j:4 * j + 4, :].re("p a b -> p (a b)"),
                                         ident.v())
                        fw.cp("act", BcT[:, j, n, :, :].re("p a b -> p (a b)"), tb[:, 0:256])
                    if n < 7:
                        cmul(Gb[(n + 1) % 2].v(), G.v(), lr, li)
                prev = F0
                for n in range(1, 9):
                    cur = Fb[n % 2]
                    cmul(cur.v(), prev.v(), lrp[:, 0, :], lrp[:, 1, :], conj=True, e="pool")
                    for ri in range(2):
                        fw.cp("pool", FB[:, ri, n - 1], cur[:, ri])
                    prev = cur
                pr, pi_, pr2, pi2 = sm(), sm(), sm(), sm()
                fw.cp("dve", pr, lr)
                fw.cp("dve", pi_, li)
                for _ in range(3):
                    fw.tt("dve", pr2, pr, pr, ALU.mult)
                    fw.tt("dve", tmp, pi_, pi_, ALU.mult)
                    fw.tt("dve", pr2, pr2, tmp, ALU.subtract)
                    fw.tt("dve", pi2, pr, pi_, ALU.mult)
                    fw.ts("dve", pi2, pi2, 2.0, ALU.mult)
                    fw.cp("dve", pr, pr2)
                    fw.cp("dve", pi_, pi2)
                rho8 = sm()
                fw.tt("dve", rho8, pr, pr, ALU.mult)
                fw.tt("dve", tmp, pi_, pi_, ALU.mult)
                fw.tt("dve", rho8, rho8, tmp, ALU.add)
                fw.act(rho8, rho8, AF.Sqrt)
                ure, uim = sm(), sm()
                fw.recip(tmp, rho8)
                fw.tt("dve", ure, pr, tmp, ALU.mult)
                fw.tt("dve", uim, pi_, tmp, ALU.mult)
                fw.dma("sp", sc_ktl.v(), KTl.v().re("p a b c -> p (a b c)"))
                fw.dma("sp", sc_bct.v(), BcT.v().re("p a b c d -> p (a b c d)"))
                fw.dma("sp", sc_fb.v(), FB.v().re("p a b c d -> p (a b c d)"))
                fw.dma("sp", sc_rho.v(), rho8)
            COS = fw.sbuf("COS", [128, 16, 256], F32)
            SIN = fw.sbuf("SIN", [128, 16, 256], F32)
            TA = fw.sbuf("TA", [128, 16, 128], F32)
            TB = fw.sbuf("TB", [128, 16, 128], F32)
            fw.memset("dve", COS[:, :, 0:1], 1.0)
            fw.memset("dve", SIN[:, :, 0:1], 0.0)
            for kk in range(8):
                L = 1 << kk

                def bL(v):
                    return v.un(2).bc([128, 16, L])

                c0, s0 = COS[:, :, 0:L], SIN[:, :, 0:L]
                fw.tt("dve", TA[:, :, 0:L], c0, bL(ure), ALU.mult)
                fw.tt("dve", TB[:, :, 0:L], s0, bL(uim), ALU.mult)
                fw.tt("dve", COS[:, :, L:2 * L], TA[:, :, 0:L], TB[:, :, 0:L], ALU.subtract)
                fw.tt("dve", TA[:, :, 0:L], s0, bL(ure), ALU.mult)
                fw.tt("dve", TB[:, :, 0:L], c0, bL(uim), ALU.mult)
                fw.tt("dve", SIN[:, :, L:2 * L], TA[:, :, 0:L], TB[:, :, 0:L], ALU.add)
                if kk < 7:
                    fw.tt("dve", pr2, ure, ure, ALU.mult)
                    fw.tt("dve", tmp, uim, uim, ALU.mult)
                    fw.tt("dve", pr2, pr2, tmp, ALU.subtract)
                    fw.tt("dve", pi2, ure, uim, ALU.mult)
                    fw.ts("dve", pi2, pi2, 2.0, ALU.mult)
                    fw.cp("dve", ure, pr2)
                    fw.cp("dve", uim, pi2)
            fw.dma("sp", sc_cos.v(), COS.v().re("p a b -> p (a b)"))
            fw.dma("sp", sc_sin.v(), SIN.v().re("p a b -> p (a b)"))

        with fw.scope():
            memt = fw.sbuf("memt", [128, 8, 2 * MEM], F32, d_ph)
            scr1 = fw.sbuf("scr1", [128, 8, 2 * MEM], BF16)
            scr2 = fw.sbuf("scr2", [128, 8, 2 * MEM], BF16)
            memb = fw.sbuf("memb", [128, 8, 2 * MEM], BF16)
            wv = fw.sbuf("wv", [128, 8, D], BF16, d_php)
            tmA = fw.sbuf("tmA", [128, 2 * MEM], F32)
            tmB = fw.sbuf("tmB", [128, 2 * MEM], F32)
            fw.dma("sp", memt.v(), memT_v)
            fw.dma("pool", wv.v(), wv_d.re("(kt p) n -> p kt n", p=128))
            layernorm(memt.v(), "mem_g", "mem_b", 2 * MEM, scr1.v(), scr2.v(), [tmA.v(), tmB.v()],
                      out_bf=memb.v(), keep_f32=False)
            linear(wb_k, 2, 4, 8, lambda kt: memb[:, kt, :], lambda m, ps: fw.cp("act", KTs[:, m, :], ps), N=2 * MEM)
            for sm_ in range(4):
                for half in range(2):
                    ps = fw.bank()
                    for kt in range(8):
                        fw.mm(ps.v(), memb[:, kt, sm_ * 128:(sm_ + 1) * 128], wv[:, kt, half * 512:(half + 1) * 512],
                              kt == 0, kt == 7)
                    fw.cp("act", Vs[:, sm_, half * 512:(half + 1) * 512], ps.v())

        for s in range(2):
            with fw.scope():
                U = fw.sbuf("U", [128, 4, SEQ], BF16)
                Uj = [fw.sub(U, f"U{j}") for j in range(4)]
                with fw.scope():
                    XLp = fw.sbuf("XLp", [128, 4, SEQ + 4], BF16)
                    GG = fw.sbuf("GG", [128, 4, SEQ], BF16)
                    fw.memset("dve", XLp[:, :, 0:4], 0.0)
                    with fw.scope():
                        xts = [fw.sbuf("xt0", [128, 8, NT], F32, d_x), fw.sbuf("xt1", [128, 8, NT], F32, d_x2)]
                        hbs = [fw.sbuf("hb0", [128, 8, NT], BF16), fw.sbuf("hb1", [128, 8, NT], BF16)]
                        scr1 = fw.sbuf("scr1", [128, 8, NT], BF16)
                        scr2 = fw.sbuf("scr2", [128, 8, NT], BF16)
                        tmA = fw.sbuf("tmA", [128, NT], F32)
                        tmB = fw.sbuf("tmB", [128, NT], F32)
                        gtms = [fw.sbuf(f"gtm{i}", [128, NT], F32) for i in range(4)]

                        def lnA(i):
                            tok0 = s * SEQ + i * NT
                            fw.dma("sp", xts[i % 2].v(), xT_v[:, :, tok0:tok0 + NT])
                            yield from layernorm_g(xts[i % 2].v(), "ln_in_g", "ln_in_b", NT, scr1.v(), scr2.v(),
                                                   [tmA.v(), tmB.v()], out_bf=hbs[i % 2].v(), keep_f32=False)

                        def linA(i):
                            hb = hbs[i % 2]

                            def epiA(m, ps):
                                if m < 4:
                                    dst = V(Uj[m], U.t[:, m, :]).re("p (t c) -> p t c", t=8)[:, :, i * 64:(i + 1) * 64]
                                    fw.cp("act", dst, ps.re("p (c t) -> p t c", t=8))
                                elif m < 8:
                                    fw.cp("act", XLp[:, m - 4, 4 + i * NT:4 + (i + 1) * NT], ps)
                                else:
                                    gelu_tanh(GG[:, m - 8, i * NT:(i + 1) * NT], ps, gtms[m - 8].v())

                            yield from linear_g(wb_in, 3, 4, 8, lambda kt: hb[:, kt, :], epiA)

                        for _ in lnA(0):
                            pass
                        for i in range(4):
                            gl = linA(i)
                            gn = lnA(i + 1) if i < 3 else iter(())
                            for _ in range(3):
                                next(gn, None)
                                next(gl, None)
                            for _ in gn:
                                pass
                            for _ in gl:
                                pass
                    with fw.scope():
                        wab = fw.sbuf("wab", [128, 2, 4, 128], BF16, d_php)
                        fw.dma("pool", wab.v(), wab_d)
                        xc = fw.sbuf("xc", [128, 4, NT], F32)
                        xcb = fw.sbuf("xcb", [128, 4, NT], BF16)
                        hs = [fw.sbuf(f"hs{i}", [128, 4, NT], F32) for i in range(2)]
                        y32 = fw.sbuf("y32", [128, 4, NT], F32)
                        sq4 = fw.sbuf("sq4", [128, 4, NT], BF16)
                        rs = fw.sbuf("rs", [128, NT], F32)
                        rrs = [fw.sbuf(f"rr{t}", [128, NT], F32) for t in range(4)]
                        igs = [fw.sbuf(f"ig{t}", [128, NT], F32) for t in range(4)]
                        aas = [fw.sbuf(f"aa{t}", [128, NT], F32) for t in range(4)]
                        mus = [fw.sbuf(f"mu{t}", [128, NT], F32) for t in range(4)]
                        xcs = [fw.sub(xc, f"xc{t}") for t in range(4)]
                        xcbs = [fw.sub(xcb, f"xcb{t}") for t in range(4)]
                        y32s = [fw.sub(y32, f"y32_{t}") for t in range(4)]
                        hss = [[fw.sub(hs[k], f"hs{k}_{t}") for t in range(4)] for k in range(2)]
                        cw = VC["conv_w"]

                        def chain(i, t):
                            rr, ig, aa, mu = rrs[t], igs[t], aas[t], mus[t]
                            xcv = V(xcs[t], xc.t[:, t, :])
                            xcbv = V(xcbs[t], xcb.t[:, t, :])
                            xin = XLp[:, t, i * NT:i * NT + NT + 4]
                            fw.act(xcv, xin[:, 4:4 + NT], AF.Identity, bias=vc("conv_b", t),
                                   scale=vecs[:, cw + 4 * t + 3:cw + 4 * t + 4])
                            yield
                            for k in range(3):
                                fw.stt("dve", xcv, xin[:, 1 + k:1 + k + NT],
                                       vecs[:, cw + 4 * t + k:cw + 4 * t + k + 1], xcv, ALU.mult, ALU.add)
                                yield
                            fw.cp("dve", xcbv, xcv)
                            yield
                            psr = fw.bank()
                            fw.mm(psr.v(), wab[:, 0, t, :], xcbv, True, True)
                            psi = fw.bank()
                            fw.mm(psi.v(), wab[:, 1, t, :], xcbv, True, True)
                            fw.act(rr.v(), psr.v(), AF.Sigmoid, bias=vc("b_a", t))
                            fw.act(ig.v(), psi.v(), AF.Sigmoid, bias=vc("b_x", t))
                            yield
                            fw.act(aa.v(), rr.v(), AF.Exp, scale=cneg[:, t:t + 1])
                            fw.tt("dve", ig.v(), ig.v(), xcv, ALU.mult)
                            yield
                            fw.act(mu.v(), rr.v(), AF.Exp, scale=cneg2[:, t:t + 1])
                            yield
                            fw.ts("dve", mu.v(), mu.v(), -1.0, ALU.mult, 1.0, ALU.add)
                            yield
                            fw.act(mu.v(), mu.v(), AF.Sqrt)
                            yield
                            fw.tt("dve", ig.v(), ig.v(), mu.v(), ALU.mult)
                            yield
                            hcv = V(hss[i % 2][t], hs[i % 2].t[:, t, :])
                            init = 0.0 if i == 0 else V(hss[(i + 1) % 2][t], hs[(i + 1) % 2].t[:, t, NT - 1:NT])
                            fw.scan(hcv, aa.v(), ig.v(), init)
                            yield
                            fw.tt("dve", V(y32s[t], y32.t[:, t, :]), hcv, GG[:, t, i * NT:(i + 1) * NT], ALU.mult)
                            yield

                        def rms_lru(i):
                            y4 = [V(y32s[t], y32.t[:, t, :]) for t in range(4)]
                            for t in range(4):
                                fw.act(sq4[:, t, :], y4[t], AF.Square)
                            ps = fw.bank()
                            for t in range(4):
                                fw.mm(ps.v(), ones_rms.v(), sq4[:, t, :], t == 0, t == 3)
                            fw.act(rs.v(), ps.v(), AF.Ln, bias=RMS_EPS)
                            fw.act(rs.v(), rs.v(), AF.Exp, scale=-0.5)
                            for t in range(4):
                                fw.stt("dve", YLRU[:, t, i * NT:(i + 1) * NT], y4[t], vc("g_lru", t), rs.v(),
                                       ALU.mult, ALU.mult)

                        for i in range(4):
                            gens = [chain(i, t) for t in range(4)]
                            alive = True
                            while alive:
                                alive = False
                                for g in gens:
                                    if next(g, "done") != "done":
                                        alive = True
                            rms_lru(i)
                Hprev = fw.sbuf("Hprev", [128, 16, 2, 256], BF16)
                KTl = fw.sbuf("KTl", [128, 4, 8, 128], BF16, d_ph2)
                FB = fw.sbuf("FB", [128, 2, 8, 16, 32], BF16, d_ph2)
                wglu = fw.sbuf("wglu", [128, 4, 512], BF16, d_php)
                with fw.scope():
                    BcT = fw.sbuf("BcT", [128, 4, 8, 2, 128], BF16, d_ph)
                    COS = fw.sbuf("COS", [128, 16, 256], F32, d_ph)
                    SIN = fw.sbuf("SIN", [128, 16, 256], F32, d_ph)
                    RHO = fw.sbuf("RHO", [128, 16], F32, d_ph)
                    fw.dma("sp", BcT.v().re("p a b c d -> p (a b c d)"), sc_bct.v())
                    fw.dma("sp", COS.v().re("p a b -> p (a b)"), sc_cos.v())
                    fw.dma("sp", SIN.v().re("p a b -> p (a b)"), sc_sin.v())
                    fw.dma("sp", RHO.v(), sc_rho.v())
                    fw.dma("sp", KTl.v().re("p a b c -> p (a b c)"), sc_ktl.v())
                    fw.dma("sp", FB.v().re("p a b c d -> p (a b c d)"), sc_fb.v())
                    fw.dma("pool", wglu.v(), wglu_d.re("(kt p) n -> p kt n", p=128))
                    A_ = fw.sbuf("A_", [128, 2, 256], F32)
                    B_ = fw.sbuf("B_", [128, 2, 256], F32)
                    R_ = fw.sbuf("R_", [128, 2, 256], F32)
                    Qs = [fw.sbuf(f"Q_{i}", [128, 2, 256], F32) for i in range(2)]
                    A2s = [fw.sbuf(f"A2_{i}", [128, 2, 256], F32) for i in range(2)]
                    B2s = [fw.sbuf(f"B2_{i}", [128, 2, 256], F32) for i in range(2)]
                    fw.memset("dve", Hprev[:, :, :, 0:1], 0.0)
                    for q in range(16):
                        j, k = divmod(q, 4)
                        Q_ = Qs[q % 2]
                        ps = fw.bank()
                        sl_ = slice(32 * k, 32 * k + 32)
                        for ri in range(2):
                            for s_ in range(8):
                                fw.mm(ps[:, ri * 256:(ri + 1) * 256], BcT[sl_, j, 7 - s_, ri, :],
                                      V(Uj[j], U.t[sl_, j, s_ * 256:(s_ + 1) * 256]), s_ == 0, s_ == 7,
                                      tile_position=(32 * k, 0))
                        S = ps.v().re("p (r c) -> p r c", r=2)
                        cb_ = COS[:, q, :].un(1).bc([128, 2, 256])
                        sb_ = SIN[:, q, :].un(1).bc([128, 2, 256])
                        fw.tt("dve", A_.v(), S, cb_, ALU.mult)
                        fw.tt("dve", B_.v(), S, sb_, ALU.mult)
                        fw.tt("dve", R_[:, 0, :], A_[:, 0, :], B_[:, 1, :], ALU.add)
                        fw.tt("dve", R_[:, 1, :], A_[:, 1, :], B_[:, 0, :], ALU.subtract)
                        rb = RHO[:, q:q + 1].bc([128, 256])
                        fw.scan(Q_[:, 0, :], rb, R_[:, 0, :], 0.0)
                        fw.scan(Q_[:, 1, :], rb, R_[:, 1, :], 0.0)
                        A2, B2 = A2s[q % 2], B2s[q % 2]
                        fw.tt("pool", A2.v(), Q_.v(), cb_, ALU.mult)
                        fw.tt("pool", B2.v(), Q_.v(), sb_, ALU.mult)
                        fw.tt("pool", Hprev[:, q, 0, 1:256], A2[:, 0, 0:255], B2[:, 1, 0:255], ALU.subtract)
                        fw.tt("pool", Hprev[:, q, 1, 1:256], A2[:, 1, 0:255], B2[:, 0, 0:255], ALU.add)
                    if debug and s == 0:
                        fw.dma("sp", dbg["d_hprev"], Hprev.v(), dsem=d_dbg)
                with fw.scope():
                    gtm2s = [fw.sbuf(f"gtm2_{i}", [128, NT], F32) for i in range(4)]
                    if debug and s == 0:
                        for j in range(4):
                            fw.dma("sp", dbg["d_u"][:, j, :], V(Uj[j], U.t[:, j, :]), dsem=d_dbg)
                    for j in range(4):
                        bk = [fw.bank() for _ in range(4)]
                        Ujv = V(Uj[j], U.t[:, j, :]).re("p (t c) -> p t c", t=8)
                        for b in range(4):
                            for d_ in range(0, 2 * b + 2):
                                if 2 * b >= d_:
                                    fw.mm(bk[b].v(), KTl[:, j, d_, :],
                                          Ujv[:, 2 * b - d_:2 * b - d_ + 2, :].re("p t c -> p (t c)"), d_ == 0, False)
                                else:
                                    fw.mm(bk[b][:, 256:512], KTl[:, j, d_, :], Ujv[:, 0, :], False, False)
                            for k in range(4):
                                q = 4 * j + k
                                sl_ = slice(32 * k, 32 * k + 32)
                                for tl in range(2):
                                    tau = 2 * b + tl
                                    for ri in range(2):
                                        last = False
                                        fw.mm(bk[b][sl_, tl * 256:(tl + 1) * 256], FB[:, ri, tau, q, :],
                                              Hprev[:, q, ri, :], False, last, tile_position=(0, 32 * k))
                        for b in range(4):
                            fw.mm(bk[b].v(), zeros_bf.v(), Ujv[:, 0:2, :].re("p t c -> p (t c)"), False, True)
                        Ynat = V(Uj[j], U.t[:, j, :]).re("p (c t) -> p t c", t=8)
                        for b in range(4):
                            gelu_tanh(Ynat[:, 2 * b:2 * b + 2, :], bk[b].v().re("p (t c) -> p t c", t=2),
                                      gtm2s[b].v().re("p (t c) -> p t c", t=2))
                    if debug and s == 0:
                        for j in range(4):
                            fw.dma("sp", dbg["d_yg"][:, j, :], V(Uj[j], U.t[:, j, :]), dsem=d_dbg)
                    sgs = [fw.sbuf(f"sg{i}", [128, NT], F32) for i in range(2)]
                    y32s_ = [fw.sbuf(f"y32b{i}", [128, 4, NT], F32) for i in range(2)]
                    sq4s = [fw.sbuf(f"sq4b{i}", [128, 4, NT], BF16) for i in range(2)]
                    rss = [fw.sbuf(f"rsb{i}", [128, NT], F32) for i in range(2)]
                    for i in range(4):
                        tsl = slice(i * NT, (i + 1) * NT)
                        y32 = y32s_[i % 2]
                        for m in range(4):
                            sg = sgs[m % 2]
                            ps = fw.bank()
                            for kt in range(4):
                                fw.mm(ps.v(), wglu[:, kt, m * 128:(m + 1) * 128], V(Uj[kt], U.t[:, kt, tsl]),
                                      kt == 0, kt == 3)
                            fw.act(sg.v(), ps.v(), AF.Sigmoid, bias=vc("b_glu", m))
                            fw.tt("dve", y32[:, m, :], V(Uj[m], U.t[:, m, tsl]), sg.v(), ALU.mult)
                        rmsnorm_finish(y32.v(), "g_s5", YS5[:, :, tsl], sq4s[i % 2].v(), rss[i % 2].v())
                if debug and s == 0:
                    fw.dma("sp", dbg["d_ys5"], YS5.v(), dsem=d_dbg)
                    fw.dma("sp", dbg["d_ylru"], YLRU.v(), dsem=d_dbg)
            with fw.scope():
                AB = [fw.sbuf("fa0", [128, 8, NT], F32), fw.sbuf("fa1", [128, 8, NT], F32)]
                fb_ = fw.sbuf("fb", [128, 8, NT], F32, d_x)
                ba = fw.sbuf("ba", [128, 8, NT], BF16)
                bb = fw.sbuf("bb", [128, 8, NT], BF16)
                h2b = fw.sbuf("h2b", [128, 8, NT], BF16)
                ffb = fw.sbuf("ffb", [128, 32, NT], BF16)
                tmx = [fw.sbuf("tmxA", [128, NT], F32).v(), fw.sbuf("tmxB", [128, NT], F32).v()]
                tmy = [fw.sbuf("tmyA", [128, NT], F32).v(), fw.sbuf("tmyB", [128, NT], F32).v()]
                tmz = [fw.sbuf("tmzA", [128, NT], F32).v(), fw.sbuf("tmzB", [128, NT], F32).v()]
                Es = [fw.sbuf(f"E{i}", [128, 2, NT], BF16) for i in range(2)]
                rdens = [fw.sbuf(f"rden{i}", [128, NT], F32) for i in range(2)]
                rl = fw.sbuf("rl", [128, NT], F32)

                def X_gen(i):
                    fa = AB[i % 2]
                    tok0 = s * SEQ + i * NT
                    tsl = slice(i * NT, (i + 1) * NT)
                    fw.dma("sp", fb_.v(), xT_v[:, :, tok0:tok0 + NT])
                    layernorm(fb_.v(), "ln_in_g", "ln_in_b", NT, ba.v(), bb.v(), tmz)
                    yield
                    yield from linear_g(wb_mix, 2, 4, 8,
                                        lambda kt: (YS5[:, kt, tsl] if kt < 4 else YLRU[:, kt - 4, tsl]),
                                        lambda m, ps: fw.stt("dve", fb_[:, m, :], fb_[:, m, :], ALPHA, ps, ALU.mult, ALU.add))
                    layernorm(fb_.v(), "ln1_g", "ln1_b", NT, ba.v(), bb.v(), tmx, out_bf=ba.v())
                    if debug and s == 0 and i == 0:
                        fw.dma("sp", dbg["d_h1"], fb_.v(), dsem=d_dbg)
                    yield
                    yield from linear_g(wb_q, 2, 4, 8, lambda kt: ba[:, kt, :],
                                        lambda m, ps: fw.act(bb[:, m, :], ps, AF.Copy, scale=0.0625))
                    for hp in range(2):
                        heads = (2 * hp, 2 * hp + 1)
                        pss = {}
                        for hd in heads:
                            for mt in range(2):
                                ps = fw.bank()
                                for kk in range(2):
                                    fw.mm(ps.v(), KTs[:, 2 * hd + kk, s * MEM + mt * 128:s * MEM + (mt + 1) * 128],
                                          bb[:, 2 * hd + kk, :], kk == 0, kk == 1)
                                pss[(hd, mt)] = ps
                        for hd in heads:
                            for mt in range(2):
                                fw.act(Es[hd % 2][:, mt, :], pss[(hd, mt)].v(), AF.Exp)
                        for hd in heads:
                            E, rden = Es[hd % 2], rdens[hd % 2]
                            pd = fw.bank()
                            for mt in range(2):
                                fw.mm(pd.v(), ones_1.v(), E[:, mt, :], mt == 0, mt == 1)
                            fw.act(rden.v(), pd.v(), AF.Ln)
                            fw.act(rden.v(), rden.v(), AF.Exp, scale=-1.0)
                            for dt in range(2):
                                po = fw.bank()
                                for mt in range(2):
                                    fw.mm(po.v(), Vs[:, s * 2 + mt, (2 * hd + dt) * 128:(2 * hd + dt + 1) * 128],
                                          E[:, mt, :], mt == 0, mt == 1)
                                fw.tt("dve", ba[:, 2 * hd + dt, :], po.v(), rden.v(), ALU.mult)
                        yield
                    yield from linear_g(wb_o, 2, 4, 8, lambda kt: ba[:, kt, :],
                                        lambda m, ps: fw.stt("dve", fa[:, m, :], fb_[:, m, :], ALPHA, ps, ALU.mult, ALU.add))
                    layernorm(fa.v(), "ln2_g", "ln2_b", NT, h2b.v(), bb.v(), tmx, out_bf=h2b.v())
                    if debug and s == 0 and i == 0:
                        fw.dma("sp", dbg["d_h2"], fa.v(), dsem=d_dbg)
                    yield

                def Y_gen(i):
                    h2 = AB[i % 2]
                    tok0 = s * SEQ + i * NT

                    def epiF(m, ps):
                        fw.act(rl.v(), ps, AF.Relu)
                        if m % 2 == 0:
                            fw.act(ffb[:, m, :], rl.v(), AF.Square)
                        else:
                            fw.tt("dve", ffb[:, m, :], rl.v(), rl.v(), ALU.mult)

                    yield from linear_g(wb_ff1, 8, 4, 8, lambda kt: h2b[:, kt, :], epiF)
                    yield from linear_g(wb_ff2, 8, 1, 32, lambda kt: ffb[:, kt, :],
                                        lambda m, ps: fw.stt("dve", h2[:, m, :], h2[:, m, :], ALPHA, ps, ALU.mult, ALU.add))
                    layernorm(h2.v(), "ln3_g", "ln3_b", NT, ffb[:, 0:8, :], ffb[:, 8:16, :], tmy)
                    fw.dma("sp", outT_v[:, :, tok0:tok0 + NT], h2.v(), dsem=d_out)
                    yield

                SCHED0 = "y x y y x x x y y y y x y x y x y x y x y x y x y x x n y n y n y"
                SCHED1 = "y y x y y y y y x x y x y x y x y x x y y y n y n y n y"
                gens = {i: X_gen(i) for i in range(4)}
                for _ in gens[0]:
                    pass
                for _ in range(3):
                    next(gens[1], None)
                for i in range(4):
                    gy = Y_gen(i)
                    gx = gens.get(i + 1)
                    gn = gens.get(i + 2)
                    for c in SCHED1.split():
                        if c == "y":
                            next(gy, None)
                        elif c == "x":
                            if gx is not None:
                                next(gx, None)
                        elif gn is not None:
                            next(gn, None)
                    for _ in gy:
                        pass
                    if gx is not None:
                        for _ in gx:
                            pass

    fw.dry = True
    emit()
    fw.dry = False
    ws.reset_for_real()
    fw.bank_i = 0
    emit()
    fw._wait("sp", ("dma", d_out, d_out.count))
    if debug:
        fw._wait("sp", ("dma", d_dbg, d_dbg.count))
    fw.barrier()
    fw.root.close()
    return nc, fw


def _blk(W, G):
    K, N = W.shape
    KT, MT = K // 128, N // 128
    a = W.reshape(KT, 128, MT // G, G, 128).transpose(2, 1, 3, 0, 4)
    return np.ascontiguousarray(a.reshape(MT // G, 128, G * KT * 128)).astype(np.float32)


def host_layout(inp):
    f = lambda k: np.asarray(inp[k], dtype=np.float32)
    shared = {}
    shared["wb_in"] = _blk(f("w_in")[0], 4)
    shared["wb_mix"] = _blk(f("w_mix_out")[0], 4)
    shared["wb_q"] = _blk(f("w_q")[0], 4)
    shared["wb_k"] = _blk(f("w_k")[0], 4)
    shared["wb_o"] = _blk(f("w_o")[0], 4)
    shared["wb_ff1"] = _blk(f("w_ff1")[0], 4)
    shared["wb_ff2"] = _blk(f("w_ff2")[0], 1)
    shared["w_v"] = np.ascontiguousarray(f("w_v")[0])
    shared["w_glu"] = np.ascontiguousarray(f("s5_w_glu")[0])
    shared["ident"] = np.eye(128, dtype=np.float32)
    vecs = np.zeros((128, NV), np.float32)

    def put8(name, v):
        vecs[:, VC[name]:VC[name] + 8] = v.reshape(8, 128).T

    def put4(name, v):
        vecs[:, VC[name]:VC[name] + 4] = v.reshape(4, 128).T

    put8("ln_in_g", f("ln_in_g")); put8("ln_in_b", f("ln_in_b"))
    put8("ln1_g", f("ln1_g")[0]); put8("ln1_b", f("ln1_b")[0])
    put8("ln2_g", f("ln2_g")[0]); put8("ln2_b", f("ln2_b")[0])
    put8("ln3_g", f("ln3_g")[0]); put8("ln3_b", f("ln3_b")[0])
    put8("mem_g", f("mem_ln_g")[0]); put8("mem_b", f("mem_ln_b")[0])
    put4("b_glu", f("s5_b_glu")[0]); put4("g_s5", f("g_s5")[0]); put4("g_lru", f("g_lru")[0])
    put4("conv_b", f("conv_b")[0]); put4("b_a", f("lru_b_a")[0].reshape(512)); put4("b_x", f("lru_b_x")[0].reshape(512))
    put4("lam", f("lru_lambda")[0]); put4("s5_d", f("s5_d")[0].reshape(512))
    cw = f("conv_w")[0]
    vecs[:, VC["conv_w"]:VC["conv_w"] + 16] = cw.reshape(4, 4, 128).transpose(2, 1, 0).reshape(128, 16)
    shared["vecs"] = vecs
    def sl16(a):
        return a.reshape(16, 2, 64).transpose(1, 2, 0).reshape(128, 16)

    a_re, a_im = f("s5_a_re")[0], f("s5_a_im")[0]
    logdt = np.repeat(f("s5_log_dt")[0][:, None], 64, axis=1)
    shared["s5sl"] = np.ascontiguousarray(np.stack([sl16(a_re), sl16(a_im), sl16(logdt)], axis=1))

    def bpad(b):
        o = np.zeros((2, 64, 16, 2, 16), np.float32)
        bb = b.reshape(16, 2, 64, 16)
        for g2 in range(2):
            o[g2, :, :, g2, :] = bb[:, g2].transpose(1, 0, 2)
        return o.reshape(128, 16, 32)

    def cpad(c):
        return bpad(c.transpose(0, 2, 1))

    shared["bsl"] = np.ascontiguousarray(np.stack([bpad(f("s5_b_re")[0]), bpad(f("s5_b_im")[0])], axis=1))
    shared["csl"] = np.ascontiguousarray(np.stack([cpad(f("s5_c_re")[0]), cpad(f("s5_c_im")[0])], axis=1))

    def bd(w):
        o = np.zeros((2, 64, 4, 2, 64), np.float32)
        ww = w.reshape(4, 2, 64, 64)
        for hh in range(2):
            o[hh, :, :, hh, :] = ww[:, hh].transpose(1, 0, 2)
        return o.reshape(128, 4, 128)

    shared["wab"] = np.ascontiguousarray(np.stack([bd(f("lru_w_a")[0]), bd(f("lru_w_x")[0])], axis=1))
    x = f("x")
    mem = f("mem")
    per_core = []
    for c in range(NCORES):
        d = dict(shared)
        d["xT"] = np.ascontiguousarray(x[2 * c:2 * c + 2].reshape(NTOK, D).T)
        d["memT"] = np.ascontiguousarray(mem[2 * c:2 * c + 2].reshape(2 * MEM, D).T)
        per_core.append(d)
    return per_core


_CACHE = {}


def kernel(**inputs):
    if "nc" not in _CACHE:
        _CACHE["nc"] = build_program(debug=False)[0]
    nc = _CACHE["nc"]
    in_maps = host_layout(inputs)
    res = run_bass_kernel_spmd(nc, in_maps, core_ids=list(range(NCORES)))
    out = np.empty((16, SEQ, D), np.float32)
    for c in range(NCORES):
        o = res.results[c]["outT"]
        out[2 * c:2 * c + 2] = o.T.reshape(2, SEQ, D)
    return out
```

```python
import contextlib
import math
import numpy as np
import concourse.bass as bass
import concourse.mybir as mybir
from concourse.bass_utils import run_bass_kernel_spmd

F32 = mybir.dt.float32
BF16 = mybir.dt.bfloat16
AF = mybir.ActivationFunctionType
ALU = mybir.AluOpType

NCORES = 8
D = 1024
SEQ = 2048
NTOK = 4096
MEM = 256
DFF = 4096
ALPHA = 2.0 ** 0.25
LN_EPS = 1e-5
RMS_EPS = 1e-6
NT = 512
PI = math.pi

VC = {}
_o = 0
for _n in ["ln_in_g", "ln_in_b", "ln1_g", "ln1_b", "ln2_g", "ln2_b", "ln3_g", "ln3_b", "mem_g", "mem_b"]:
    VC[_n] = _o
    _o += 8
for _n in ["b_glu", "g_s5", "g_lru", "conv_b", "b_a", "b_x", "lam"]:
    VC[_n] = _o
    _o += 4
VC["conv_w"] = _o
_o += 16
VC["s5_d"] = _o
_o += 4
NV = _o


class DSem:
    def __init__(self, fw, name):
        self.h = fw.root.enter_context(fw.nc.semaphore(name))
        self.count = 0
        self.name = name


class Buf:
    def __init__(self, name, t=None, dsem=None):
        self.name = name
        self.t = t
        self.w = None
        self.r = []
        self.dsem = dsem

    def v(self):
        return V(self, self.t[:])

    def __getitem__(self, k):
        return V(self, self.t[k])


class V:
    def __init__(self, buf, ap):
        self.buf = buf
        self.ap = ap

    def __getitem__(self, k):
        return V(self.buf, self.ap[k])

    def re(self, pat, **kw):
        return V(self.buf, self.ap.rearrange(pat, **kw))

    def bc(self, shape):
        return V(self.buf, self.ap.to_broadcast(list(shape)))

    def un(self, axis):
        return V(self.buf, self.ap.unsqueeze(axis))


COMPUTE = ("pe", "act", "dve", "pool")


def _bufs(*vs):
    out = []
    for v in vs:
        if isinstance(v, V) and v.buf is not None and v.buf not in out:
            out.append(v.buf)
    return out


def _ap(x):
    return x.ap if isinstance(x, V) else x


class FW:
    def __init__(self, nc, same_engine_sync=True):
        self.nc = nc
        self.root = contextlib.ExitStack()
        self.es = self.root
        self.eng = {"pe": nc.tensor, "act": nc.scalar, "dve": nc.vector, "pool": nc.gpsimd, "sp": nc.sync}
        self.sem = {}
        self.cnt = {}
        for e in COMPUTE:
            self.sem[e] = self.root.enter_context(nc.semaphore("sem_" + e))
            self.cnt[e] = 0
        self.known = {e: {} for e in self.eng}
        self.same_engine_sync = same_engine_sync
        self.dsems = []
        self.dry = False
        self.uid = 0
        self.banks = []
        self.bank_i = 0
        self.ninst = 0

    def _name(self, name):
        self.uid += 1
        return f"{name}_{self.uid}"

    def sbuf(self, name, shape, dt, dsem=None):
        t = self.es.enter_context(self.nc.sbuf_tensor(self._name(name), list(shape), dt))
        return Buf(name, t, dsem)

    def psum(self, name, shape, dt):
        t = self.es.enter_context(self.nc.psum_tensor(self._name(name), list(shape), dt))
        return Buf(name, t)

    def dsem(self, name):
        d = DSem(self, name)
        self.dsems.append(d)
        return d

    def sub(self, parent, name):
        return Buf(name, parent.t, parent.dsem)

    @contextlib.contextmanager
    def scope(self):
        outer = self.es
        self.es = contextlib.ExitStack()
        try:
            yield
        finally:
            self.barrier()
            self.es.close()
            self.es = outer

    def bank(self):
        b = self.banks[self.bank_i % len(self.banks)]
        self.bank_i += 1
        return b

    def _wait(self, e, dep):
        kind, key, val = dep
        if kind == "eng":
            if key == e and (e == "pe" or not self.same_engine_sync):
                return
            kk = ("eng", key)
            if self.known[e].get(kk, 0) >= val:
                return
            self.eng[e].wait_ge(self.sem[key], val)
            self.known[e][kk] = val
        else:
            val = key.count
            kk = ("dma", key.name)
            if self.known[e].get(kk, 0) >= val:
                return
            self.eng[e].wait_ge(key.h, val)
            self.known[e][kk] = val
        self.ninst += 1

    def _deps(self, e, reads, writes):
        for b in reads:
            if b.w is not None:
                self._wait(e, b.w)
        for b in writes:
            if b.w is not None:
                self._wait(e, b.w)
            for d in b.r:
                self._wait(e, d)

    def op(self, e, reads, writes, fn):
        if self.dry:
            return None
        self._deps(e, reads, writes)
        ins = fn(self.eng[e])
        self.cnt[e] += 1
        self.ninst += 1
        ins.then_inc(self.sem[e], 1)
        tag = ("eng", e, self.cnt[e])
        for b in reads:
            if b not in writes:
                b.r.append(tag)
        for b in writes:
            b.w = tag
            b.r = []
        return ins

    def mm(self, out, lhsT, rhs, start, stop, **kw):
        if self.dry:
            return None
        e = "pe"
        psb = out.buf
        rd = _bufs(lhsT, rhs)
        self._deps(e, rd, [psb] if start else [])
        ins = self.nc.tensor.matmul(out.ap, lhsT.ap, rhs.ap, start=start, stop=stop, **kw)
        self.ninst += 1
        if stop:
            self.cnt[e] += 1
            ins.then_inc(self.sem[e], 1)
            tag = ("eng", e, self.cnt[e])
        else:
            tag = ("eng", e, self.cnt[e] + 1)
        for b in rd:
            b.r.append(tag)
        psb.w = tag
        if start:
            psb.r = []
        return ins

    def transpose(self, out, in_, ident):
        if self.dry:
            return None
        e = "pe"
        rd = _bufs(in_, ident)
        self._deps(e, rd, [out.buf])
        ins = self.nc.tensor.transpose(out.ap, in_.ap, ident.ap)
        self.ninst += 1
        self.cnt[e] += 1
        ins.then_inc(self.sem[e], 1)
        tag = ("eng", e, self.cnt[e])
        for b in rd:
            b.r.append(tag)
        out.buf.w = tag
        out.buf.r = []
        return ins

    def dma(self, q, out, in_, dsem=None, **kw):
        if self.dry:
            return None
        reads = _bufs(in_)
        writes = _bufs(out)
        self._deps(q, reads, writes)
        if dsem is None:
            for b in writes + reads:
                if b.dsem is not None:
                    dsem = b.dsem
                    break
        assert dsem is not None, "dma needs a DSem"
        ins = self.eng[q].dma_start(out=out.ap, in_=in_.ap, **kw)
        self.ninst += 1
        dsem.count += 16
        ins.then_inc(dsem.h, 16)
        tag = ("dma", dsem, dsem.count)
        for b in reads:
            b.r.append(tag)
        for b in writes:
            b.w = tag
            b.r = []
        return ins

    def barrier(self, engines=("pe", "act", "dve", "pool", "sp")):
        if self.dry:
            return
        for e in engines:
            for f in COMPUTE:
                if f != e and self.cnt[f] > 0:
                    self._wait(e, ("eng", f, self.cnt[f]))
            for d in self.dsems:
                if d.count > 0:
                    self._wait(e, ("dma", d, d.count))

    def tt(self, e, out, a, b, op):
        return self.op(e, _bufs(a, b), _bufs(out),
                       lambda E: E.tensor_tensor(out=out.ap, in0=a.ap, in1=b.ap, op=op))

    def ts(self, e, out, a, s1, op0, s2=None, op1=None):
        kw = {}
        if op1 is not None:
            kw["op1"] = op1
        return self.op(e, _bufs(a, s1, s2), _bufs(out),
                       lambda E: E.tensor_scalar(out=out.ap, in0=a.ap, scalar1=_ap(s1), scalar2=_ap(s2), op0=op0, **kw))

    def stt(self, e, out, in0, scalar, in1, op0, op1):
        return self.op(e, _bufs(in0, scalar, in1), _bufs(out),
                       lambda E: E.scalar_tensor_tensor(out=out.ap, in0=in0.ap, scalar=_ap(scalar), in1=in1.ap,
                                                        op0=op0, op1=op1))

    def act(self, out, in_, func, bias=None, scale=None):
        kw = {}
        if bias is not None:
            kw["bias"] = _ap(bias)
        if scale is not None:
            kw["scale"] = _ap(scale)
        return self.op("act", _bufs(in_, bias, scale), _bufs(out),
                       lambda E: E.activation(out=out.ap, in_=in_.ap, func=func, **kw))

    def cp(self, e, out, in_):
        if e == "act":
            return self.act(out, in_, AF.Copy)
        return self.op(e, _bufs(in_), _bufs(out), lambda E: E.tensor_copy(out=out.ap, in_=in_.ap))

    def memset(self, e, out, val):
        return self.op(e, [], _bufs(out), lambda E: E.memset(out.ap, val))

    def scan(self, out, d0, d1, init, op0=ALU.mult, op1=ALU.add):
        return self.op("dve", _bufs(d0, d1, init), _bufs(out),
                       lambda E: E.tensor_tensor_scan(out=out.ap, data0=d0.ap, data1=d1.ap, initial=_ap(init),
                                                      op0=op0, op1=op1))

    def recip(self, out, in_):
        return self.op("dve", _bufs(in_), _bufs(out), lambda E: E.reciprocal(out=out.ap, in_=in_.ap))


class WS:
    def __init__(self, fw, nslots, elems):
        self.fw = fw
        self.n = nslots
        self.slots = [fw.sbuf(f"ring{i}", [128, elems], BF16, fw.dsem(f"dring{i}")) for i in range(nslots)]
        self.plan = []
        self.pos = 0
        self.issued = 0

    def reset_for_real(self):
        self.pos = 0
        self.issued = 0

    def get(self, dram_v, n):
        fw = self.fw
        if fw.dry:
            self.plan.append((dram_v, n))
            self.pos += 1
            return self.slots[(self.pos - 1) % self.n].v()
        i = self.pos
        self.pos += 1
        lim = min(len(self.plan), i + self.n)
        while self.issued < lim:
            j = self.issued
            dv, nn = self.plan[j]
            fw.dma("pool", self.slots[j % self.n][:, 0:nn], dv)
            self.issued += 1
        return self.slots[i % self.n].v()


def build_program(debug=False):
    nc = bass.Bass("TRN2", target_bir_lowering=False)
    fw = FW(nc)

    def din(name, shape, dt=F32):
        return V(None, nc.dram_tensor(name, list(shape), dt, kind="ExternalInput").ap())

    xT = din("xT", [D, NTOK])
    memT = din("memT", [D, 2 * MEM])
    vecs_d = din("vecs", [128, NV])
    ident_d = din("ident", [128, 128])
    s5sl_d = din("s5sl", [128, 3, 16])
    bsl_d = din("bsl", [128, 2, 16, 32])
    csl_d = din("csl", [128, 2, 16, 32])
    wab_d = din("wab", [128, 2, 4, 128])
    wb_in = din("wb_in", [3, 128, 4096])
    wb_mix = din("wb_mix", [2, 128, 4096])
    wb_q = din("wb_q", [2, 128, 4096])
    wb_k = din("wb_k", [2, 128, 4096])
    wb_o = din("wb_o", [2, 128, 4096])
    wb_ff1 = din("wb_ff1", [8, 128, 4096])
    wb_ff2 = din("wb_ff2", [8, 128, 4096])
    wv_d = din("w_v", [D, D])
    wglu_d = din("w_glu", [512, 512])
    outT = V(None, nc.dram_tensor("outT", [D, NTOK], F32, kind="ExternalOutput").ap())
    dbg = {}
    if debug:
        for nm, shp, dt_ in [("d_ys5", [128, 4, SEQ], BF16), ("d_ylru", [128, 4, SEQ], BF16), ("d_u", [128, 4, SEQ], BF16),
                        ("d_h1", [128, 8, NT], F32), ("d_h2", [128, 8, NT], F32), ("d_hprev", [128, 16, 2, 256], BF16),
                        ("d_yg", [128, 4, SEQ], BF16)]:
            dbg[nm] = V(None, nc.dram_tensor(nm, shp, dt_, kind="ExternalOutput").ap())

    xT_v = xT.re("(kt p) n -> p kt n", p=128)
    memT_v = memT.re("(kt p) n -> p kt n", p=128)
    outT_v = outT.re("(kt p) n -> p kt n", p=128)

    def dscratch(name, shape, dt):
        return Buf(name, nc.dram_tensor(name, list(shape), dt).ap())

    sc_ktl = dscratch("sc_ktl", [128, 4 * 8 * 128], BF16)
    sc_bct = dscratch("sc_bct", [128, 4 * 8 * 2 * 128], BF16)
    sc_fb = dscratch("sc_fb", [128, 2 * 8 * 16 * 32], BF16)
    sc_cos = dscratch("sc_cos", [128, 16 * 256], F32)
    sc_sin = dscratch("sc_sin", [128, 16 * 256], F32)
    sc_rho = dscratch("sc_rho", [128, 16], F32)

    d_init = fw.dsem("d_init")
    d_ph = fw.dsem("d_ph")
    d_ph2 = fw.dsem("d_ph2")
    d_initp = fw.dsem("d_initp")
    d_php = fw.dsem("d_php")
    d_x = fw.dsem("d_x")
    d_x2 = fw.dsem("d_x2")
    d_out = fw.dsem("d_out")
    d_sv = fw.dsem("d_sv")
    d_dbg = fw.dsem("d_dbg")
    for b_ in (sc_ktl, sc_bct, sc_fb, sc_cos, sc_sin, sc_rho):
        b_.dsem = d_sv

    fw.banks = [fw.psum(f"bank{i}", [128, 512], F32) for i in range(8)]
    vecs = fw.sbuf("vecs", [128, NV], F32, d_init)
    ones_ln = fw.sbuf("ones_ln", [128, 128], BF16)
    ones_rms = fw.sbuf("ones_rms", [128, 128], BF16)
    ones_1 = fw.sbuf("ones_1", [128, 128], BF16)
    zeros_bf = fw.sbuf("zeros_bf", [128, 128], BF16)
    cneg = fw.sbuf("cneg", [128, 4], F32)
    cneg2 = fw.sbuf("cneg2", [128, 4], F32)
    KTs = fw.sbuf("KTs", [128, 8, 2 * MEM], BF16)
    Vs = fw.sbuf("Vs", [128, 4, D], BF16)
    YS5 = fw.sbuf("YS5", [128, 4, SEQ], BF16)
    YLRU = fw.sbuf("YLRU", [128, 4, SEQ], BF16)
    ws = WS(fw, 4, 4096)

    def vc(name, i):
        c = VC[name] + i
        return vecs[:, c:c + 1]

    def layernorm_g(r, gname, bname, N, scr_xb, scr_sq, tm, out_bf=None, keep_f32=True, dst_f32=None):
        if dst_f32 is None:
            dst_f32 = r
        fw.cp("dve", scr_xb, r)
        fw.act(scr_sq, r, AF.Square)
        yield
        pm = fw.bank()
        pq = fw.bank()
        for kt in range(8):
            fw.mm(pm[:, 0:N], ones_ln.v(), scr_xb[:, kt, :], kt == 0, kt == 7)
        for kt in range(8):
            fw.mm(pq[:, 0:N], ones_ln.v(), scr_sq[:, kt, :], kt == 0, kt == 7)
        mean, t = tm
        fw.act(t, pm[:, 0:N], AF.Square)
        fw.act(mean, pm[:, 0:N], AF.Copy)
        fw.stt("dve", t, pq[:, 0:N], LN_EPS, t, ALU.add, ALU.subtract)
        fw.act(t, t, AF.Ln)
        fw.act(t, t, AF.Exp, scale=-0.5)
        fw.tt("dve", r, r, mean.un(1).bc([128, 8, N]), ALU.subtract)
        fw.tt("dve", r, r, t.un(1).bc([128, 8, N]), ALU.mult)
        yield
        for kt in range(8):
            if keep_f32:
                fw.act(dst_f32[:, kt, :], r[:, kt, :], AF.Identity, bias=vc(bname, kt), scale=vc(gname, kt))
            else:
                fw.act(out_bf[:, kt, :], r[:, kt, :], AF.Identity, bias=vc(bname, kt), scale=vc(gname, kt))
        if keep_f32 and out_bf is not None:
            fw.cp("dve", out_bf, dst_f32)
        yield

    def layernorm(*a, **kw):
        for _ in layernorm_g(*a, **kw):
            pass

    GC1 = 2.0 * math.sqrt(2.0 / math.pi)

    def gelu_tanh(dst, ps, tmp):
        fw.act(tmp, ps, AF.Square)
        fw.stt("dve", tmp, tmp, 1.0 / 0.044715, ps, ALU.add, ALU.mult)
        fw.act(tmp, tmp, AF.Sigmoid, scale=GC1 * 0.044715)
        fw.tt("dve", dst, tmp, ps, ALU.mult)

    def rmsnorm_finish(y32, gname, out_bf, sq4, rs):
        fw.act(sq4, y32, AF.Square)
        ps = fw.bank()
        for t in range(4):
            fw.mm(ps.v(), ones_rms.v(), sq4[:, t, :], t == 0, t == 3)
        fw.act(rs, ps.v(), AF.Ln, bias=RMS_EPS)
        fw.act(rs, rs, AF.Exp, scale=-0.5)
        for t in range(4):
            fw.stt("dve", out_bf[:, t, :], y32[:, t, :], vc(gname, t), rs, ALU.mult, ALU.mult)

    def linear_g(wb, nslots, G, KT, rhs_fn, epi, N=NT):
        for s_ in range(nslots):
            slot = ws.get(wb[s_], G * KT * 128)
            sv = slot.re("p (g kt c) -> p g kt c", g=G, kt=KT)
            for g in range(G):
                m = s_ * G + g
                ps = fw.bank()
                for kt in range(KT):
                    fw.mm(ps[:, 0:N], sv[:, g, kt, :], rhs_fn(kt), kt == 0, kt == KT - 1)
                epi(m, ps[:, 0:N])
            yield

    def linear(*a, **kw):
        for _ in linear_g(*a, **kw):
            pass

    def interleave(gy, gx, ny, nx):
        xi = 0
        for yi in range(ny):
            next(gy, None)
            tgt = ((yi + 1) * nx) // ny if gx is not None else 0
            while xi < tgt:
                next(gx, None)
                xi += 1
        for _ in gy:
            pass
        if gx is not None:
            for _ in gx:
                pass

    def emit():
        fw.dma("sp", vecs.v(), vecs_d)
        fw.memset("dve", ones_ln.v(), 1.0 / 1024.0)
        fw.memset("dve", ones_rms.v(), 1.0 / 512.0)
        fw.memset("dve", ones_1.v(), 1.0)
        fw.memset("dve", zeros_bf.v(), 0.0)
        lamv = vecs[:, VC["lam"]:VC["lam"] + 4]
        fw.act(cneg.v(), lamv, AF.Exp, scale=-1.0)
        fw.ts("dve", cneg.v(), cneg.v(), 1.0, ALU.add)
        fw.act(cneg.v(), cneg.v(), AF.Ln)
        fw.ts("dve", cneg.v(), cneg.v(), -8.0, ALU.mult)
        fw.ts("dve", cneg2.v(), cneg.v(), 2.0, ALU.mult)

        with fw.scope():
            ident = fw.sbuf("ident", [128, 128], F32, d_ph)
            sl = fw.sbuf("sl", [128, 3, 16], F32, d_ph)
            bsl = fw.sbuf("bsl", [128, 2, 16, 32], F32, d_ph)
            csl = fw.sbuf("csl", [128, 2, 16, 32], F32, d_ph)
            fw.dma("sp", ident.v(), ident_d)
            fw.dma("sp", sl.v(), s5sl_d)
            fw.dma("sp", bsl.v(), bsl_d)
            fw.dma("sp", csl.v(), csl_d)
            SM = fw.sbuf("SM", [128, 48, 16], F32)
            _k = [0]

            def sm():
                _k[0] += 1
                return SM[:, _k[0] - 1, :]

            a_re, a_im, logdt = sl[:, 0, :], sl[:, 1, :], sl[:, 2, :]
            dt_, are, ang, mag = sm(), sm(), sm(), sm()
            fw.act(dt_, logdt, AF.Exp)
            fw.tt("dve", are, dt_, a_re, ALU.mult)
            fw.tt("dve", ang, dt_, a_im, ALU.mult)
            fw.act(mag, are, AF.Exp)
            tmp, tmp2 = sm(), sm()

            TWO_PI_HI = float(np.float32(2.0 * PI))
            TWO_PI_LO = float(np.float32(2.0 * PI - TWO_PI_HI))
            HALF_PI_HI = float(np.float32(PI / 2))
            HALF_PI_LO = float(np.float32(PI / 2 - HALF_PI_HI))

            def sin_reduced(dst, src):
                red, kf = sm(), sm()
                fw.ts("dve", kf, src, PI, ALU.is_gt)
                for k in range(1, 6):
                    fw.ts("dve", tmp, src, (2 * k + 1) * PI, ALU.is_gt)
                    fw.tt("dve", kf, kf, tmp, ALU.add)
                fw.stt("dve", red, kf, -TWO_PI_HI, src, ALU.mult, ALU.add)
                fw.stt("dve", red, kf, -TWO_PI_LO, red, ALU.mult, ALU.add)
                fw.act(dst, red, AF.Sin)

            sn, cs, angc = sm(), sm(), sm()
            sin_reduced(sn, ang)
            fw.ts("dve", angc, ang, HALF_PI_HI, ALU.add, HALF_PI_LO, ALU.add)
            sin_reduced(cs, angc)
            lr, li = sm(), sm()
            fw.tt("dve", lr, mag, cs, ALU.mult)
            fw.tt("dve", li, mag, sn, ALU.mult)
            lm1, den, qre, qim = sm(), sm(), sm(), sm()
            fw.ts("dve", lm1, lr, -1.0, ALU.add)
            fw.tt("dve", den, a_re, a_re, ALU.mult)
            fw.tt("dve", tmp, a_im, a_im, ALU.mult)
            fw.tt("dve", den, den, tmp, ALU.add)
            fw.recip(den, den)
            fw.tt("dve", qre, lm1, a_re, ALU.mult)
            fw.tt("dve", tmp, li, a_im, ALU.mult)
            fw.tt("dve", qre, qre, tmp, ALU.add)
            fw.tt("dve", qre, qre, den, ALU.mult)
            fw.tt("dve", qim, li, a_re, ALU.mult)
            fw.tt("dve", tmp, lm1, a_im, ALU.mult)
            fw.tt("dve", qim, qim, tmp, ALU.subtract)
            fw.tt("dve", qim, qim, den, ALU.mult)

            def b3(v):
                return v.un(2).bc([128, 16, 32])

            with fw.scope():
                Gb = [fw.sbuf(f"G{i}", [128, 2, 16, 32], F32) for i in range(2)]
                Fb = [fw.sbuf(f"F{i}", [128, 2, 16, 32], F32) for i in range(2)]
                F0 = fw.sbuf("F0", [128, 2, 16, 32], F32)
                T1 = fw.sbuf("T1", [128, 16, 32], F32)
                T2 = fw.sbuf("T2", [128, 16, 32], F32)
                KTl = fw.sbuf("KTl", [128, 4, 8, 128], BF16)
                BcT = fw.sbuf("BcT", [128, 4, 8, 2, 128], BF16)
                FB = fw.sbuf("FB", [128, 2, 8, 16, 32], BF16)
                fw.memset("dve", KTl.v(), 0.0)

                T3 = fw.sbuf("T3", [128, 16, 32], F32)
                T4 = fw.sbuf("T4", [128, 16, 32], F32)

                def cmul(dst, src, xr, xi, conj=False, e="dve"):
                    Ta, Tb = (T1, T2) if e == "dve" else (T3, T4)
                    sre, sim = src[:, 0], src[:, 1]
                    fw.tt(e, Ta.v(), sre, b3(xr), ALU.mult)
                    fw.tt(e, Tb.v(), sim, b3(xi), ALU.mult)
                    fw.tt(e, dst[:, 0], Ta.v(), Tb.v(), ALU.add if conj else ALU.subtract)
                    fw.tt(e, Ta.v(), sre, b3(xi), ALU.mult)
                    fw.tt(e, Tb.v(), sim, b3(xr), ALU.mult)
                    fw.tt(e, dst[:, 1], Tb.v(), Ta.v(), ALU.subtract if conj else ALU.add)

                lrp = fw.sbuf("lrp", [128, 2, 16], F32)
                fw.cp("dve", lrp[:, 0, :], lr)
                fw.cp("dve", lrp[:, 1, :], li)
                fw.cp("dve", F0[:, 0], csl[:, 0])
                fw.ts("dve", F0[:, 1], csl[:, 1], -1.0, ALU.mult)
                cmul(Gb[0].v(), bsl.v(), qre, qim)
                dvec = vecs[:, VC["s5_d"]:VC["s5_d"] + 4]
                for n in range(8):
                    G = Gb[n % 2]
                    for j in range(4):
                        kb = fw.bank()
                        for k in range(4):
                            q = 4 * j + k
                            for ri in range(2):
                                fw.mm(kb[32 * k:32 * k + 32, 32 * k:32 * k + 32], G[:, ri, q, :], F0[:, ri, q, :],
                                      ri == 0, ri == 1, tile_position=(0, 32 * k))
                        for k in range(4):
                            sl_ = slice(32 * k, 32 * k + 32)
                            if n == 0:
                                fw.stt("dve", KTl[sl_, j, 0, sl_], ident[sl_, sl_], dvec[sl_, j:j + 1], kb[sl_, sl_],
                                       ALU.mult, ALU.add)
                            else:
                                fw.cp("act", KTl[sl_, j, n, sl_], kb[sl_, sl_])
                    for j in range(4):
                        tb = fw.bank()
                        for ri in range(2):
                            fw.transpose(tb[:, ri * 128:(ri + 1) * 128], G[:, ri, 4 * j:4 * j + 4, :].re("p a b -> p (a b)"),
                                         ident.v())
                        fw.cp("act", BcT[:, j, n, :, :].re("p a b -> p (a b)"), tb[:, 0:256])
                    if n < 7:
                        cmul(Gb[(n + 1) % 2].v(), G.v(), lr, li)
                prev = F0
                for n in range(1, 9):
                    cur = Fb[n % 2]
                    cmul(cur.v(), prev.v(), lrp[:, 0, :], lrp[:, 1, :], conj=True, e="pool")
                    for ri in range(2):
                        fw.cp("pool", FB[:, ri, n - 1], cur[:, ri])
                    prev = cur
                pr, pi_, pr2, pi2 = sm(), sm(), sm(), sm()
                fw.cp("dve", pr, lr)
                fw.cp("dve", pi_, li)
                for _ in range(3):
                    fw.tt("dve", pr2, pr, pr, ALU.mult)
                    fw.tt("dve", tmp, pi_, pi_, ALU.mult)
                    fw.tt("dve", pr2, pr2, tmp, ALU.subtract)
                    fw.tt("dve", pi2, pr, pi_, ALU.mult)
                    fw.ts("dve", pi2, pi2, 2.0, ALU.mult)
                    fw.cp("dve", pr, pr2)
                    fw.cp("dve", pi_, pi2)
                rho8 = sm()
                fw.tt("dve", rho8, pr, pr, ALU.mult)
                fw.tt("dve", tmp, pi_, pi_, ALU.mult)
                fw.tt("dve", rho8, rho8, tmp, ALU.add)
                fw.act(rho8, rho8, AF.Sqrt)
                ure, uim = sm(), sm()
                fw.recip(tmp, rho8)
                fw.tt("dve", ure, pr, tmp, ALU.mult)
                fw.tt("dve", uim, pi_, tmp, ALU.mult)
                fw.dma("sp", sc_ktl.v(), KTl.v().re("p a b c -> p (a b c)"))
                fw.dma("sp", sc_bct.v(), BcT.v().re("p a b c d -> p (a b c d)"))
                fw.dma("sp", sc_fb.v(), FB.v().re("p a b c d -> p (a b c d)"))
                fw.dma("sp", sc_rho.v(), rho8)
            COS = fw.sbuf("COS", [128, 16, 256], F32)
            SIN = fw.sbuf("SIN", [128, 16, 256], F32)
            TA = fw.sbuf("TA", [128, 16, 128], F32)
            TB = fw.sbuf("TB", [128, 16, 128], F32)
            fw.memset("dve", COS[:, :, 0:1], 1.0)
            fw.memset("dve", SIN[:, :, 0:1], 0.0)
            for kk in range(8):
                L = 1 << kk

                def bL(v):
                    return v.un(2).bc([128, 16, L])

                c0, s0 = COS[:, :, 0:L], SIN[:, :, 0:L]
                fw.tt("dve", TA[:, :, 0:L], c0, bL(ure), ALU.mult)
                fw.tt("dve", TB[:, :, 0:L], s0, bL(uim), ALU.mult)
                fw.tt("dve", COS[:, :, L:2 * L], TA[:, :, 0:L], TB[:, :, 0:L], ALU.subtract)
                fw.tt("dve", TA[:, :, 0:L], s0, bL(ure), ALU.mult)
                fw.tt("dve", TB[:, :, 0:L], c0, bL(uim), ALU.mult)
                fw.tt("dve", SIN[:, :, L:2 * L], TA[:, :, 0:L], TB[:, :, 0:L], ALU.add)
                if kk < 7:
                    fw.tt("dve", pr2, ure, ure, ALU.mult)
                    fw.tt("dve", tmp, uim, uim, ALU.mult)
                    fw.tt("dve", pr2, pr2, tmp, ALU.subtract)
                    fw.tt("dve", pi2, ure, uim, ALU.mult)
                    fw.ts("dve", pi2, pi2, 2.0, ALU.mult)
                    fw.cp("dve", ure, pr2)
                    fw.cp("dve", uim, pi2)
            fw.dma("sp", sc_cos.v(), COS.v().re("p a b -> p (a b)"))
            fw.dma("sp", sc_sin.v(), SIN.v().re("p a b -> p (a b)"))

        with fw.scope():
            memt = fw.sbuf("memt", [128, 8, 2 * MEM], F32, d_ph)
            scr1 = fw.sbuf("scr1", [128, 8, 2 * MEM], BF16)
            scr2 = fw.sbuf("scr2", [128, 8, 2 * MEM], BF16)
            memb = fw.sbuf("memb", [128, 8, 2 * MEM], BF16)
            wv = fw.sbuf("wv", [128, 8, D], BF16, d_php)
            tmA = fw.sbuf("tmA", [128, 2 * MEM], F32)
            tmB = fw.sbuf("tmB", [128, 2 * MEM], F32)
            fw.dma("sp", memt.v(), memT_v)
            fw.dma("pool", wv.v(), wv_d.re("(kt p) n -> p kt n", p=128))
            layernorm(memt.v(), "mem_g", "mem_b", 2 * MEM, scr1.v(), scr2.v(), [tmA.v(), tmB.v()],
                      out_bf=memb.v(), keep_f32=False)
            linear(wb_k, 2, 4, 8, lambda kt: memb[:, kt, :], lambda m, ps: fw.cp("act", KTs[:, m, :], ps), N=2 * MEM)
            for sm_ in range(4):
                for half in range(2):
                    ps = fw.bank()
                    for kt in range(8):
                        fw.mm(ps.v(), memb[:, kt, sm_ * 128:(sm_ + 1) * 128], wv[:, kt, half * 512:(half + 1) * 512],
                              kt == 0, kt == 7)
                    fw.cp("act", Vs[:, sm_, half * 512:(half + 1) * 512], ps.v())

        for s in range(2):
            with fw.scope():
                U = fw.sbuf("U", [128, 4, SEQ], BF16)
                Uj = [fw.sub(U, f"U{j}") for j in range(4)]
                with fw.scope():
                    XLp = fw.sbuf("XLp", [128, 4, SEQ + 4], BF16)
                    GG = fw.sbuf("GG", [128, 4, SEQ], BF16)
                    fw.memset("dve", XLp[:, :, 0:4], 0.0)
                    with fw.scope():
                        xts = [fw.sbuf("xt0", [128, 8, NT], F32, d_x), fw.sbuf("xt1", [128, 8, NT], F32, d_x2)]
                        hbs = [fw.sbuf("hb0", [128, 8, NT], BF16), fw.sbuf("hb1", [128, 8, NT], BF16)]
                        scr1 = fw.sbuf("scr1", [128, 8, NT], BF16)
                        scr2 = fw.sbuf("scr2", [128, 8, NT], BF16)
                        tmA = fw.sbuf("tmA", [128, NT], F32)
                        tmB = fw.sbuf("tmB", [128, NT], F32)
                        gtms = [fw.sbuf(f"gtm{i}", [128, NT], F32) for i in range(4)]

                        def lnA(i):
                            tok0 = s * SEQ + i * NT
                            fw.dma("sp", xts[i % 2].v(), xT_v[:, :, tok0:tok0 + NT])
                            yield from layernorm_g(xts[i % 2].v(), "ln_in_g", "ln_in_b", NT, scr1.v(), scr2.v(),
                                                   [tmA.v(), tmB.v()], out_bf=hbs[i % 2].v(), keep_f32=False)

                        def linA(i):
                            hb = hbs[i % 2]

                            def epiA(m, ps):
                                if m < 4:
                                    dst = V(Uj[m], U.t[:, m, :]).re("p (t c) -> p t c", t=8)[:, :, i * 64:(i + 1) * 64]
                                    fw.cp("act", dst, ps.re("p (c t) -> p t c", t=8))
                                elif m < 8:
                                    fw.cp("act", XLp[:, m - 4, 4 + i * NT:4 + (i + 1) * NT], ps)
                                else:
                                    gelu_tanh(GG[:, m - 8, i * NT:(i + 1) * NT], ps, gtms[m - 8].v())

                            yield from linear_g(wb_in, 3, 4, 8, lambda kt: hb[:, kt, :], epiA)

                        for _ in lnA(0):
                            pass
                        for i in range(4):
                            gl = linA(i)
                            gn = lnA(i + 1) if i < 3 else iter(())
                            for _ in range(3):
                                next(gn, None)
                                next(gl, None)
                            for _ in gn:
                                pass
                            for _ in gl:
                                pass
                    with fw.scope():
                        wab = fw.sbuf("wab", [128, 2, 4, 128], BF16, d_php)
                        fw.dma("pool", wab.v(), wab_d)
                        xc = fw.sbuf("xc", [128, 4, NT], F32)
                        xcb = fw.sbuf("xcb", [128, 4, NT], BF16)
                        hs = [fw.sbuf(f"hs{i}", [128, 4, NT], F32) for i in range(2)]
                        y32 = fw.sbuf("y32", [128, 4, NT], F32)
                        sq4 = fw.sbuf("sq4", [128, 4, NT], BF16)
                        rs = fw.sbuf("rs", [128, NT], F32)
                        rrs = [fw.sbuf(f"rr{t}", [128, NT], F32) for t in range(4)]
                        igs = [fw.sbuf(f"ig{t}", [128, NT], F32) for t in range(4)]
                        aas = [fw.sbuf(f"aa{t}", [128, NT], F32) for t in range(4)]
                        mus = [fw.sbuf(f"mu{t}", [128, NT], F32) for t in range(4)]
                        xcs = [fw.sub(xc, f"xc{t}") for t in range(4)]
                        xcbs = [fw.sub(xcb, f"xcb{t}") for t in range(4)]
                        y32s = [fw.sub(y32, f"y32_{t}") for t in range(4)]
                        hss = [[fw.sub(hs[k], f"hs{k}_{t}") for t in range(4)] for k in range(2)]
                        cw = VC["conv_w"]

                        def chain(i, t):
                            rr, ig, aa, mu = rrs[t], igs[t], aas[t], mus[t]
                            xcv = V(xcs[t], xc.t[:, t, :])
                            xcbv = V(xcbs[t], xcb.t[:, t, :])
                            xin = XLp[:, t, i * NT:i * NT + NT + 4]
                            fw.act(xcv, xin[:, 4:4 + NT], AF.Identity, bias=vc("conv_b", t),
                                   scale=vecs[:, cw + 4 * t + 3:cw + 4 * t + 4])
                            yield
                            for k in range(3):
                                fw.stt("dve", xcv, xin[:, 1 + k:1 + k + NT],
                                       vecs[:, cw + 4 * t + k:cw + 4 * t + k + 1], xcv, ALU.mult, ALU.add)
                                yield
                            fw.cp("dve", xcbv, xcv)
                            yield
                            psr = fw.bank()
                            fw.mm(psr.v(), wab[:, 0, t, :], xcbv, True, True)
                            psi = fw.bank()
                            fw.mm(psi.v(), wab[:, 1, t, :], xcbv, True, True)
                            fw.act(rr.v(), psr.v(), AF.Sigmoid, bias=vc("b_a", t))
                            fw.act(ig.v(), psi.v(), AF.Sigmoid, bias=vc("b_x", t))
                            yield
                            fw.act(aa.v(), rr.v(), AF.Exp, scale=cneg[:, t:t + 1])
                            fw.tt("dve", ig.v(), ig.v(), xcv, ALU.mult)
                            yield
                            fw.act(mu.v(), rr.v(), AF.Exp, scale=cneg2[:, t:t + 1])
                            yield
                            fw.ts("dve", mu.v(), mu.v(), -1.0, ALU.mult, 1.0, ALU.add)
                            yield
                            fw.act(mu.v(), mu.v(), AF.Sqrt)
                            yield
                            fw.tt("dve", ig.v(), ig.v(), mu.v(), ALU.mult)
                            yield
                            hcv = V(hss[i % 2][t], hs[i % 2].t[:, t, :])
                            init = 0.0 if i == 0 else V(hss[(i + 1) % 2][t], hs[(i + 1) % 2].t[:, t, NT - 1:NT])
                            fw.scan(hcv, aa.v(), ig.v(), init)
                            yield
                            fw.tt("dve", V(y32s[t], y32.t[:, t, :]), hcv, GG[:, t, i * NT:(i + 1) * NT], ALU.mult)
                            yield

                        def rms_lru(i):
                            y4 = [V(y32s[t], y32.t[:, t, :]) for t in range(4)]
                            for t in range(4):
                                fw.act(sq4[:, t, :], y4[t], AF.Square)
                            ps = fw.bank()
                            for t in range(4):
                                fw.mm(ps.v(), ones_rms.v(), sq4[:, t, :], t == 0, t == 3)
                            fw.act(rs.v(), ps.v(), AF.Ln, bias=RMS_EPS)
                            fw.act(rs.v(), rs.v(), AF.Exp, scale=-0.5)
                            for t in range(4):
                                fw.stt("dve", YLRU[:, t, i * NT:(i + 1) * NT], y4[t], vc("g_lru", t), rs.v(),
                                       ALU.mult, ALU.mult)

                        for i in range(4):
                            gens = [chain(i, t) for t in range(4)]
                            alive = True
                            while alive:
                                alive = False
                                for g in gens:
                                    if next(g, "done") != "done":
                                        alive = True
                            rms_lru(i)
                Hprev = fw.sbuf("Hprev", [128, 16, 2, 256], BF16)
                KTl = fw.sbuf("KTl", [128, 4, 8, 128], BF16, d_ph2)
                FB = fw.sbuf("FB", [128, 2, 8, 16, 32], BF16, d_ph2)
                wglu = fw.sbuf("wglu", [128, 4, 512], BF16, d_php)
                with fw.scope():
                    BcT = fw.sbuf("BcT", [128, 4, 8, 2, 128], BF16, d_ph)
                    COS = fw.sbuf("COS", [128, 16, 256], F32, d_ph)
                    SIN = fw.sbuf("SIN", [128, 16, 256], F32, d_ph)
                    RHO = fw.sbuf("RHO", [128, 16], F32, d_ph)
                    fw.dma("sp", BcT.v().re("p a b c d -> p (a b c d)"), sc_bct.v())
                    fw.dma("sp", COS.v().re("p a b -> p (a b)"), sc_cos.v())
                    fw.dma("sp", SIN.v().re("p a b -> p (a b)"), sc_sin.v())
                    fw.dma("sp", RHO.v(), sc_rho.v())
                    fw.dma("sp", KTl.v().re("p a b c -> p (a b c)"), sc_ktl.v())
                    fw.dma("sp", FB.v().re("p a b c d -> p (a b c d)"), sc_fb.v())
                    fw.dma("pool", wglu.v(), wglu_d.re("(kt p) n -> p kt n", p=128))
                    A_ = fw.sbuf("A_", [128, 2, 256], F32)
                    B_ = fw.sbuf("B_", [128, 2, 256], F32)
                    R_ = fw.sbuf("R_", [128, 2, 256], F32)
                    Qs = [fw.sbuf(f"Q_{i}", [128, 2, 256], F32) for i in range(2)]
                    A2s = [fw.sbuf(f"A2_{i}", [128, 2, 256], F32) for i in range(2)]
                    B2s = [fw.sbuf(f"B2_{i}", [128, 2, 256], F32) for i in range(2)]
                    fw.memset("dve", Hprev[:, :, :, 0:1], 0.0)
                    for q in range(16):
                        j, k = divmod(q, 4)
                        Q_ = Qs[q % 2]
                        ps = fw.bank()
                        sl_ = slice(32 * k, 32 * k + 32)
                        for ri in range(2):
                            for s_ in range(8):
                                fw.mm(ps[:, ri * 256:(ri + 1) * 256], BcT[sl_, j, 7 - s_, ri, :],
                                      V(Uj[j], U.t[sl_, j, s_ * 256:(s_ + 1) * 256]), s_ == 0, s_ == 7,
                                      tile_position=(32 * k, 0))
                        S = ps.v().re("p (r c) -> p r c", r=2)
                        cb_ = COS[:, q, :].un(1).bc([128, 2, 256])
                        sb_ = SIN[:, q, :].un(1).bc([128, 2, 256])
                        fw.tt("dve", A_.v(), S, cb_, ALU.mult)
                        fw.tt("dve", B_.v(), S, sb_, ALU.mult)
                        fw.tt("dve", R_[:, 0, :], A_[:, 0, :], B_[:, 1, :], ALU.add)
                        fw.tt("dve", R_[:, 1, :], A_[:, 1, :], B_[:, 0, :], ALU.subtract)
                        rb = RHO[:, q:q + 1].bc([128, 256])
                        fw.scan(Q_[:, 0, :], rb, R_[:, 0, :], 0.0)
                        fw.scan(Q_[:, 1, :], rb, R_[:, 1, :], 0.0)
                        A2, B2 = A2s[q % 2], B2s[q % 2]
                        fw.tt("pool", A2.v(), Q_.v(), cb_, ALU.mult)
                        fw.tt("pool", B2.v(), Q_.v(), sb_, ALU.mult)
                        fw.tt("pool", Hprev[:, q, 0, 1:256], A2[:, 0, 0:255], B2[:, 1, 0:255], ALU.subtract)
                        fw.tt("pool", Hprev[:, q, 1, 1:256], A2[:, 1, 0:255], B2[:, 0, 0:255], ALU.add)
                    if debug and s == 0:
                        fw.dma("sp", dbg["d_hprev"], Hprev.v(), dsem=d_dbg)
                with fw.scope():
                    gtm2s = [fw.sbuf(f"gtm2_{i}", [128, NT], F32) for i in range(4)]
                    if debug and s == 0:
                        for j in range(4):
                            fw.dma("sp", dbg["d_u"][:, j, :], V(Uj[j], U.t[:, j, :]), dsem=d_dbg)
                    for j in range(4):
                        bk = [fw.bank() for _ in range(4)]
                        Ujv = V(Uj[j], U.t[:, j, :]).re("p (t c) -> p t c", t=8)
                        for b in range(4):
                            for d_ in range(0, 2 * b + 2):
                                if 2 * b >= d_:
                                    fw.mm(bk[b].v(), KTl[:, j, d_, :],
                                          Ujv[:, 2 * b - d_:2 * b - d_ + 2, :].re("p t c -> p (t c)"), d_ == 0, False)
                                else:
                                    fw.mm(bk[b][:, 256:512], KTl[:, j, d_, :], Ujv[:, 0, :], False, False)
                            for k in range(4):
                                q = 4 * j + k
                                sl_ = slice(32 * k, 32 * k + 32)
                                for tl in range(2):
                                    tau = 2 * b + tl
                                    for ri in range(2):
                                        last = False
                                        fw.mm(bk[b][sl_, tl * 256:(tl + 1) * 256], FB[:, ri, tau, q, :],
                                              Hprev[:, q, ri, :], False, last, tile_position=(0, 32 * k))
                        for b in range(4):
                            fw.mm(bk[b].v(), zeros_bf.v(), Ujv[:, 0:2, :].re("p t c -> p (t c)"), False, True)
                        Ynat = V(Uj[j], U.t[:, j, :]).re("p (c t) -> p t c", t=8)
                        for b in range(4):
                            gelu_tanh(Ynat[:, 2 * b:2 * b + 2, :], bk[b].v().re("p (t c) -> p t c", t=2),
                                      gtm2s[b].v().re("p (t c) -> p t c", t=2))
                    if debug and s == 0:
                        for j in range(4):
                            fw.dma("sp", dbg["d_yg"][:, j, :], V(Uj[j], U.t[:, j, :]), dsem=d_dbg)
                    sgs = [fw.sbuf(f"sg{i}", [128, NT], F32) for i in range(2)]
                    y32s_ = [fw.sbuf(f"y32b{i}", [128, 4, NT], F32) for i in range(2)]
                    sq4s = [fw.sbuf(f"sq4b{i}", [128, 4, NT], BF16) for i in range(2)]
                    rss = [fw.sbuf(f"rsb{i}", [128, NT], F32) for i in range(2)]
                    for i in range(4):
                        tsl = slice(i * NT, (i + 1) * NT)
                        y32 = y32s_[i % 2]
                        for m in range(4):
                            sg = sgs[m % 2]
                            ps = fw.bank()
                            for kt in range(4):
                                fw.mm(ps.v(), wglu[:, kt, m * 128:(m + 1) * 128], V(Uj[kt], U.t[:, kt, tsl]),
                                      kt == 0, kt == 3)
                            fw.act(sg.v(), ps.v(), AF.Sigmoid, bias=vc("b_glu", m))
                            fw.tt("dve", y32[:, m, :], V(Uj[m], U.t[:, m, tsl]), sg.v(), ALU.mult)
                        rmsnorm_finish(y32.v(), "g_s5", YS5[:, :, tsl], sq4s[i % 2].v(), rss[i % 2].v())
                if debug and s == 0:
                    fw.dma("sp", dbg["d_ys5"], YS5.v(), dsem=d_dbg)
                    fw.dma("sp", dbg["d_ylru"], YLRU.v(), dsem=d_dbg)
            with fw.scope():
                AB = [fw.sbuf("fa0", [128, 8, NT], F32), fw.sbuf("fa1", [128, 8, NT], F32)]
                fb_ = fw.sbuf("fb", [128, 8, NT], F32, d_x)
                ba = fw.sbuf("ba", [128, 8, NT], BF16)
                bb = fw.sbuf("bb", [128, 8, NT], BF16)
                h2b = fw.sbuf("h2b", [128, 8, NT], BF16)
                ffb = fw.sbuf("ffb", [128, 32, NT], BF16)
                tmx = [fw.sbuf("tmxA", [128, NT], F32).v(), fw.sbuf("tmxB", [128, NT], F32).v()]
                tmy = [fw.sbuf("tmyA", [128, NT], F32).v(), fw.sbuf("tmyB", [128, NT], F32).v()]
                tmz = [fw.sbuf("tmzA", [128, NT], F32).v(), fw.sbuf("tmzB", [128, NT], F32).v()]
                Es = [fw.sbuf(f"E{i}", [128, 2, NT], BF16) for i in range(2)]
                rdens = [fw.sbuf(f"rden{i}", [128, NT], F32) for i in range(2)]
                rl = fw.sbuf("rl", [128, NT], F32)

                def X_gen(i):
                    fa = AB[i % 2]
                    tok0 = s * SEQ + i * NT
                    tsl = slice(i * NT, (i + 1) * NT)
                    fw.dma("sp", fb_.v(), xT_v[:, :, tok0:tok0 + NT])
                    layernorm(fb_.v(), "ln_in_g", "ln_in_b", NT, ba.v(), bb.v(), tmz)
                    yield
                    yield from linear_g(wb_mix, 2, 4, 8,
                                        lambda kt: (YS5[:, kt, tsl] if kt < 4 else YLRU[:, kt - 4, tsl]),
                                        lambda m, ps: fw.stt("dve", fb_[:, m, :], fb_[:, m, :], ALPHA, ps, ALU.mult, ALU.add))
                    layernorm(fb_.v(), "ln1_g", "ln1_b", NT, ba.v(), bb.v(), tmx, out_bf=ba.v())
                    if debug and s == 0 and i == 0:
                        fw.dma("sp", dbg["d_h1"], fb_.v(), dsem=d_dbg)
                    yield
                    yield from linear_g(wb_q, 2, 4, 8, lambda kt: ba[:, kt, :],
                                        lambda m, ps: fw.act(bb[:, m, :], ps, AF.Copy, scale=0.0625))
                    for hp in range(2):
                        heads = (2 * hp, 2 * hp + 1)
                        pss = {}
                        for hd in heads:
                            for mt in range(2):
                                ps = fw.bank()
                                for kk in range(2):
                                    fw.mm(ps.v(), KTs[:, 2 * hd + kk, s * MEM + mt * 128:s * MEM + (mt + 1) * 128],
                                          bb[:, 2 * hd + kk, :], kk == 0, kk == 1)
                                pss[(hd, mt)] = ps
                        for hd in heads:
                            for mt in range(2):
                                fw.act(Es[hd % 2][:, mt, :], pss[(hd, mt)].v(), AF.Exp)
                        for hd in heads:
                            E, rden = Es[hd % 2], rdens[hd % 2]
                            pd = fw.bank()
                            for mt in range(2):
                                fw.mm(pd.v(), ones_1.v(), E[:, mt, :], mt == 0, mt == 1)
                            fw.act(rden.v(), pd.v(), AF.Ln)
                            fw.act(rden.v(), rden.v(), AF.Exp, scale=-1.0)
                            for dt in range(2):
                                po = fw.bank()
                                for mt in range(2):
                                    fw.mm(po.v(), Vs[:, s * 2 + mt, (2 * hd + dt) * 128:(2 * hd + dt + 1) * 128],
                                          E[:, mt, :], mt == 0, mt == 1)
                                fw.tt("dve", ba[:, 2 * hd + dt, :], po.v(), rden.v(), ALU.mult)
                        yield
                    yield from linear_g(wb_o, 2, 4, 8, lambda kt: ba[:, kt, :],
                                        lambda m, ps: fw.stt("dve", fa[:, m, :], fb_[:, m, :], ALPHA, ps, ALU.mult, ALU.add))
                    layernorm(fa.v(), "ln2_g", "ln2_b", NT, h2b.v(), bb.v(), tmx, out_bf=h2b.v())
                    if debug and s == 0 and i == 0:
                        fw.dma("sp", dbg["d_h2"], fa.v(), dsem=d_dbg)
                    yield

                def Y_gen(i):
                    h2 = AB[i % 2]
                    tok0 = s * SEQ + i * NT

                    def epiF(m, ps):
                        fw.act(rl.v(), ps, AF.Relu)
                        if m % 2 == 0:
                            fw.act(ffb[:, m, :], rl.v(), AF.Square)
                        else:
                            fw.tt("dve", ffb[:, m, :], rl.v(), rl.v(), ALU.mult)

                    yield from linear_g(wb_ff1, 8, 4, 8, lambda kt: h2b[:, kt, :], epiF)
                    yield from linear_g(wb_ff2, 8, 1, 32, lambda kt: ffb[:, kt, :],
                                        lambda m, ps: fw.stt("dve", h2[:, m, :], h2[:, m, :], ALPHA, ps, ALU.mult, ALU.add))
                    layernorm(h2.v(), "ln3_g", "ln3_b", NT, ffb[:, 0:8, :], ffb[:, 8:16, :], tmy)
                    fw.dma("sp", outT_v[:, :, tok0:tok0 + NT], h2.v(), dsem=d_out)
                    yield

                SCHED0 = "y x y y x x x y y y y x y x y x y x y x y x y x y x x n y n y n y"
                SCHED1 = "y x y y y y x y x y x y x y x y x x y y y n y n y n y y"
                gens = {i: X_gen(i) for i in range(4)}
                for _ in gens[0]:
                    pass
                for _ in range(3):
                    next(gens[1], None)
                for i in range(4):
                    gy = Y_gen(i)
                    gx = gens.get(i + 1)
                    gn = gens.get(i + 2)
                    for c in SCHED1.split():
                        if c == "y":
                            next(gy, None)
                        elif c == "x":
                            if gx is not None:
                                next(gx, None)
                        elif gn is not None:
                            next(gn, None)
                    for _ in gy:
                        pass
                    if gx is not None:
                        for _ in gx:
                            pass

    fw.dry = True
    emit()
    fw.dry = False
    ws.reset_for_real()
    fw.bank_i = 0
    emit()
    fw._wait("sp", ("dma", d_out, d_out.count))
    if debug:
        fw._wait("sp", ("dma", d_dbg, d_dbg.count))
    fw.barrier()
    fw.root.close()
    return nc, fw


def _blk(W, G):
    K, N = W.shape
    KT, MT = K // 128, N // 128
    a = W.reshape(KT, 128, MT // G, G, 128).transpose(2, 1, 3, 0, 4)
    return np.ascontiguousarray(a.reshape(MT // G, 128, G * KT * 128)).astype(np.float32)


def host_layout(inp):
    f = lambda k: np.asarray(inp[k], dtype=np.float32)
    shared = {}
    shared["wb_in"] = _blk(f("w_in")[0], 4)
    shared["wb_mix"] = _blk(f("w_mix_out")[0], 4)
    shared["wb_q"] = _blk(f("w_q")[0], 4)
    shared["wb_k"] = _blk(f("w_k")[0], 4)
    shared["wb_o"] = _blk(f("w_o")[0], 4)
    shared["wb_ff1"] = _blk(f("w_ff1")[0], 4)
    shared["wb_ff2"] = _blk(f("w_ff2")[0], 1)
    shared["w_v"] = np.ascontiguousarray(f("w_v")[0])
    shared["w_glu"] = np.ascontiguousarray(f("s5_w_glu")[0])
    shared["ident"] = np.eye(128, dtype=np.float32)
    vecs = np.zeros((128, NV), np.float32)

    def put8(name, v):
        vecs[:, VC[name]:VC[name] + 8] = v.reshape(8, 128).T

    def put4(name, v):
        vecs[:, VC[name]:VC[name] + 4] = v.reshape(4, 128).T

    put8("ln_in_g", f("ln_in_g")); put8("ln_in_b", f("ln_in_b"))
    put8("ln1_g", f("ln1_g")[0]); put8("ln1_b", f("ln1_b")[0])
    put8("ln2_g", f("ln2_g")[0]); put8("ln2_b", f("ln2_b")[0])
    put8("ln3_g", f("ln3_g")[0]); put8("ln3_b", f("ln3_b")[0])
    put8("mem_g", f("mem_ln_g")[0]); put8("mem_b", f("mem_ln_b")[0])
    put4("b_glu", f("s5_b_glu")[0]); put4("g_s5", f("g_s5")[0]); put4("g_lru", f("g_lru")[0])
    put4("conv_b", f("conv_b")[0]); put4("b_a", f("lru_b_a")[0].reshape(512)); put4("b_x", f("lru_b_x")[0].reshape(512))
    put4("lam", f("lru_lambda")[0]); put4("s5_d", f("s5_d")[0].reshape(512))
    cw = f("conv_w")[0]
    vecs[:, VC["conv_w"]:VC["conv_w"] + 16] = cw.reshape(4, 4, 128).transpose(2, 1, 0).reshape(128, 16)
    shared["vecs"] = vecs
    def sl16(a):
        return a.reshape(16, 2, 64).transpose(1, 2, 0).reshape(128, 16)

    a_re, a_im = f("s5_a_re")[0], f("s5_a_im")[0]
    logdt = np.repeat(f("s5_log_dt")[0][:, None], 64, axis=1)
    shared["s5sl"] = np.ascontiguousarray(np.stack([sl16(a_re), sl16(a_im), sl16(logdt)], axis=1))

    def bpad(b):
        o = np.zeros((2, 64, 16, 2, 16), np.float32)
        bb = b.reshape(16, 2, 64, 16)
        for g2 in range(2):
            o[g2, :, :, g2, :] = bb[:, g2].transpose(1, 0, 2)
        return o.reshape(128, 16, 32)

    def cpad(c):
        return bpad(c.transpose(0, 2, 1))

    shared["bsl"] = np.ascontiguousarray(np.stack([bpad(f("s5_b_re")[0]), bpad(f("s5_b_im")[0])], axis=1))
    shared["csl"] = np.ascontiguousarray(np.stack([cpad(f("s5_c_re")[0]), cpad(f("s5_c_im")[0])], axis=1))

    def bd(w):
        o = np.zeros((2, 64, 4, 2, 64), np.float32)
        ww = w.reshape(4, 2, 64, 64)
        for hh in range(2):
            o[hh, :, :, hh, :] = ww[:, hh].transpose(1, 0, 2)
        return o.reshape(128, 4, 128)

    shared["wab"] = np.ascontiguousarray(np.stack([bd(f("lru_w_a")[0]), bd(f("lru_w_x")[0])], axis=1))
    x = f("x")
    mem = f("mem")
    per_core = []
    for c in range(NCORES):
        d = dict(shared)
        d["xT"] = np.ascontiguousarray(x[2 * c:2 * c + 2].reshape(NTOK, D).T)
        d["memT"] = np.ascontiguousarray(mem[2 * c:2 * c + 2].reshape(2 * MEM, D).T)
        per_core.append(d)
    return per_core


_CACHE = {}


def kernel(**inputs):
    if "nc" not in _CACHE:
        _CACHE["nc"] = build_program(debug=False)[0]
    nc = _CACHE["nc"]
    in_maps = host_layout(inputs)
    res = run_bass_kernel_spmd(nc, in_maps, core_ids=list(range(NCORES)))
    out = np.empty((16, SEQ, D), np.float32)
    for c in range(NCORES):
        o = res.results[c]["outT"]
        out[2 * c:2 * c + 2] = o.T.reshape(2, SEQ, D)
    return out
```

```python
import contextlib
import math
import numpy as np
import concourse.bass as bass
import concourse.mybir as mybir
from concourse.bass_utils import run_bass_kernel_spmd

F32 = mybir.dt.float32
BF16 = mybir.dt.bfloat16
AF = mybir.ActivationFunctionType
ALU = mybir.AluOpType

NCORES = 8
D = 1024
SEQ = 2048
NTOK = 4096
MEM = 256
DFF = 4096
ALPHA = 2.0 ** 0.25
LN_EPS = 1e-5
RMS_EPS = 1e-6
NT = 512
PI = math.pi

VC = {}
_o = 0
for _n in ["ln_in_g", "ln_in_b", "ln1_g", "ln1_b", "ln2_g", "ln2_b", "ln3_g", "ln3_b", "mem_g", "mem_b"]:
    VC[_n] = _o
    _o += 8
for _n in ["b_glu", "g_s5", "g_lru", "conv_b", "b_a", "b_x", "lam"]:
    VC[_n] = _o
    _o += 4
VC["conv_w"] = _o
_o += 16
VC["s5_d"] = _o
_o += 4
NV = _o


class DSem:
    def __init__(self, fw, name):
        self.h = fw.root.enter_context(fw.nc.semaphore(name))
        self.count = 0
        self.name = name


class Buf:
    def __init__(self, name, t=None, dsem=None):
        self.name = name
        self.t = t
        self.w = None
        self.r = []
        self.dsem = dsem

    def v(self):
        return V(self, self.t[:])

    def __getitem__(self, k):
        return V(self, self.t[k])


class V:
    def __init__(self, buf, ap):
        self.buf = buf
        self.ap = ap

    def __getitem__(self, k):
        return V(self.buf, self.ap[k])

    def re(self, pat, **kw):
        return V(self.buf, self.ap.rearrange(pat, **kw))

    def bc(self, shape):
        return V(self.buf, self.ap.to_broadcast(list(shape)))

    def un(self, axis):
        return V(self.buf, self.ap.unsqueeze(axis))


COMPUTE = ("pe", "act", "dve", "pool")


def _bufs(*vs):
    out = []
    for v in vs:
        if isinstance(v, V) and v.buf is not None and v.buf not in out:
            out.append(v.buf)
    return out


def _ap(x):
    return x.ap if isinstance(x, V) else x


class FW:
    def __init__(self, nc, same_engine_sync=True):
        self.nc = nc
        self.root = contextlib.ExitStack()
        self.es = self.root
        self.eng = {"pe": nc.tensor, "act": nc.scalar, "dve": nc.vector, "pool": nc.gpsimd, "sp": nc.sync}
        self.sem = {}
        self.cnt = {}
        for e in COMPUTE:
            self.sem[e] = self.root.enter_context(nc.semaphore("sem_" + e))
            self.cnt[e] = 0
        self.known = {e: {} for e in self.eng}
        self.same_engine_sync = same_engine_sync
        self.dsems = []
        self.dry = False
        self.uid = 0
        self.banks = []
        self.bank_i = 0
        self.ninst = 0

    def _name(self, name):
        self.uid += 1
        return f"{name}_{self.uid}"

    def sbuf(self, name, shape, dt, dsem=None):
        t = self.es.enter_context(self.nc.sbuf_tensor(self._name(name), list(shape), dt))
        return Buf(name, t, dsem)

    def psum(self, name, shape, dt):
        t = self.es.enter_context(self.nc.psum_tensor(self._name(name), list(shape), dt))
        return Buf(name, t)

    def dsem(self, name):
        d = DSem(self, name)
        self.dsems.append(d)
        return d

    def sub(self, parent, name):
        return Buf(name, parent.t, parent.dsem)

    @contextlib.contextmanager
    def scope(self):
        outer = self.es
        self.es = contextlib.ExitStack()
        try:
            yield
        finally:
            self.barrier()
            self.es.close()
            self.es = outer

    def bank(self):
        b = self.banks[self.bank_i % len(self.banks)]
        self.bank_i += 1
        return b

    def _wait(self, e, dep):
        kind, key, val = dep
        if kind == "eng":
            if key == e and (e == "pe" or not self.same_engine_sync):
                return
            kk = ("eng", key)
            if self.known[e].get(kk, 0) >= val:
                return
            self.eng[e].wait_ge(self.sem[key], val)
            self.known[e][kk] = val
        else:
            val = key.count
            kk = ("dma", key.name)
            if self.known[e].get(kk, 0) >= val:
                return
            self.eng[e].wait_ge(key.h, val)
            self.known[e][kk] = val
        self.ninst += 1

    def _deps(self, e, reads, writes):
        for b in reads:
            if b.w is not None:
                self._wait(e, b.w)
        for b in writes:
            if b.w is not None:
                self._wait(e, b.w)
            for d in b.r:
                self._wait(e, d)

    def op(self, e, reads, writes, fn):
        if self.dry:
            return None
        self._deps(e, reads, writes)
        ins = fn(self.eng[e])
        self.cnt[e] += 1
        self.ninst += 1
        ins.then_inc(self.sem[e], 1)
        tag = ("eng", e, self.cnt[e])
        for b in reads:
            if b not in writes:
                b.r.append(tag)
        for b in writes:
            b.w = tag
            b.r = []
        return ins

    def mm(self, out, lhsT, rhs, start, stop, **kw):
        if self.dry:
            return None
        e = "pe"
        psb = out.buf
        rd = _bufs(lhsT, rhs)
        self._deps(e, rd, [psb] if start else [])
        ins = self.nc.tensor.matmul(out.ap, lhsT.ap, rhs.ap, start=start, stop=stop, **kw)
        self.ninst += 1
        if stop:
            self.cnt[e] += 1
            ins.then_inc(self.sem[e], 1)
            tag = ("eng", e, self.cnt[e])
        else:
            tag = ("eng", e, self.cnt[e] + 1)
        for b in rd:
            b.r.append(tag)
        psb.w = tag
        if start:
            psb.r = []
        return ins

    def transpose(self, out, in_, ident):
        if self.dry:
            return None
        e = "pe"
        rd = _bufs(in_, ident)
        self._deps(e, rd, [out.buf])
        ins = self.nc.tensor.transpose(out.ap, in_.ap, ident.ap)
        self.ninst += 1
        self.cnt[e] += 1
        ins.then_inc(self.sem[e], 1)
        tag = ("eng", e, self.cnt[e])
        for b in rd:
            b.r.append(tag)
        out.buf.w = tag
        out.buf.r = []
        return ins

    def dma(self, q, out, in_, dsem=None, **kw):
        if self.dry:
            return None
        reads = _bufs(in_)
        writes = _bufs(out)
        self._deps(q, reads, writes)
        if dsem is None:
            for b in writes + reads:
                if b.dsem is not None:
                    dsem = b.dsem
                    break
        assert dsem is not None, "dma needs a DSem"
        ins = self.eng[q].dma_start(out=out.ap, in_=in_.ap, **kw)
        self.ninst += 1
        dsem.count += 16
        ins.then_inc(dsem.h, 16)
        tag = ("dma", dsem, dsem.count)
        for b in reads:
            b.r.append(tag)
        for b in writes:
            b.w = tag
            b.r = []
        return ins

    def barrier(self, engines=("pe", "act", "dve", "pool", "sp")):
        if self.dry:
            return
        for e in engines:
            for f in COMPUTE:
                if f != e and self.cnt[f] > 0:
                    self._wait(e, ("eng", f, self.cnt[f]))
            for d in self.dsems:
                if d.count > 0:
                    self._wait(e, ("dma", d, d.count))

    def tt(self, e, out, a, b, op):
        return self.op(e, _bufs(a, b), _bufs(out),
                       lambda E: E.tensor_tensor(out=out.ap, in0=a.ap, in1=b.ap, op=op))

    def ts(self, e, out, a, s1, op0, s2=None, op1=None):
        kw = {}
        if op1 is not None:
            kw["op1"] = op1
        return self.op(e, _bufs(a, s1, s2), _bufs(out),
                       lambda E: E.tensor_scalar(out=out.ap, in0=a.ap, scalar1=_ap(s1), scalar2=_ap(s2), op0=op0, **kw))

    def stt(self, e, out, in0, scalar, in1, op0, op1):
        return self.op(e, _bufs(in0, scalar, in1), _bufs(out),
                       lambda E: E.scalar_tensor_tensor(out=out.ap, in0=in0.ap, scalar=_ap(scalar), in1=in1.ap,
                                                        op0=op0, op1=op1))

    def act(self, out, in_, func, bias=None, scale=None):
        kw = {}
        if bias is not None:
            kw["bias"] = _ap(bias)
        if scale is not None:
            kw["scale"] = _ap(scale)
        return self.op("act", _bufs(in_, bias, scale), _bufs(out),
                       lambda E: E.activation(out=out.ap, in_=in_.ap, func=func, **kw))

    def cp(self, e, out, in_):
        if e == "act":
            return self.act(out, in_, AF.Copy)
        return self.op(e, _bufs(in_), _bufs(out), lambda E: E.tensor_copy(out=out.ap, in_=in_.ap))

    def memset(self, e, out, val):
        return self.op(e, [], _bufs(out), lambda E: E.memset(out.ap, val))

    def scan(self, out, d0, d1, init, op0=ALU.mult, op1=ALU.add):
        return self.op("dve", _bufs(d0, d1, init), _bufs(out),
                       lambda E: E.tensor_tensor_scan(out=out.ap, data0=d0.ap, data1=d1.ap, initial=_ap(init),
                                                      op0=op0, op1=op1))

    def recip(self, out, in_):
        return self.op("dve", _bufs(in_), _bufs(out), lambda E: E.reciprocal(out=out.ap, in_=in_.ap))


class WS:
    def __init__(self, fw, nslots, elems):
        self.fw = fw
        self.n = nslots
        self.slots = [fw.sbuf(f"ring{i}", [128, elems], BF16, fw.dsem(f"dring{i}")) for i in range(nslots)]
        self.plan = []
        self.pos = 0
        self.issued = 0

    def reset_for_real(self):
        self.pos = 0
        self.issued = 0

    def get(self, dram_v, n):
        fw = self.fw
        if fw.dry:
            self.plan.append((dram_v, n))
            self.pos += 1
            return self.slots[(self.pos - 1) % self.n].v()
        i = self.pos
        self.pos += 1
        lim = min(len(self.plan), i + self.n)
        while self.issued < lim:
            j = self.issued
            dv, nn = self.plan[j]
            fw.dma("pool", self.slots[j % self.n][:, 0:nn], dv)
            self.issued += 1
        return self.slots[i % self.n].v()


def build_program(debug=False):
    nc = bass.Bass("TRN2", target_bir_lowering=False)
    fw = FW(nc)

    def din(name, shape, dt=F32):
        return V(None, nc.dram_tensor(name, list(shape), dt, kind="ExternalInput").ap())

    xT = din("xT", [D, NTOK])
    memT = din("memT", [D, 2 * MEM])
    vecs_d = din("vecs", [128, NV])
    ident_d = din("ident", [128, 128])
    s5sl_d = din("s5sl", [128, 3, 16])
    bsl_d = din("bsl", [128, 2, 16, 32])
    csl_d = din("csl", [128, 2, 16, 32])
    wab_d = din("wab", [128, 2, 4, 128])
    wb_in = din("wb_in", [3, 128, 4096])
    wb_mix = din("wb_mix", [2, 128, 4096])
    wb_q = din("wb_q", [2, 128, 4096])
    wb_k = din("wb_k", [2, 128, 4096])
    wb_o = din("wb_o", [2, 128, 4096])
    wb_ff1 = din("wb_ff1", [8, 128, 4096])
    wb_ff2 = din("wb_ff2", [8, 128, 4096])
    wv_d = din("w_v", [D, D])
    wglu_d = din("w_glu", [512, 512])
    outT = V(None, nc.dram_tensor("outT", [D, NTOK], F32, kind="ExternalOutput").ap())
    dbg = {}
    if debug:
        for nm, shp, dt_ in [("d_ys5", [128, 4, SEQ], BF16), ("d_ylru", [128, 4, SEQ], BF16), ("d_u", [128, 4, SEQ], BF16),
                        ("d_h1", [128, 8, NT], F32), ("d_h2", [128, 8, NT], F32), ("d_hprev", [128, 16, 2, 256], BF16),
                        ("d_yg", [128, 4, SEQ], BF16)]:
            dbg[nm] = V(None, nc.dram_tensor(nm, shp, dt_, kind="ExternalOutput").ap())

    xT_v = xT.re("(kt p) n -> p kt n", p=128)
    memT_v = memT.re("(kt p) n -> p kt n", p=128)
    outT_v = outT.re("(kt p) n -> p kt n", p=128)

    def dscratch(name, shape, dt):
        return Buf(name, nc.dram_tensor(name, list(shape), dt).ap())

    sc_ktl = dscratch("sc_ktl", [128, 4 * 8 * 128], BF16)
    sc_bct = dscratch("sc_bct", [128, 4 * 8 * 2 * 128], BF16)
    sc_fb = dscratch("sc_fb", [128, 2 * 8 * 16 * 32], BF16)
    sc_cos = dscratch("sc_cos", [128, 16 * 256], F32)
    sc_sin = dscratch("sc_sin", [128, 16 * 256], F32)
    sc_rho = dscratch("sc_rho", [128, 16], F32)

    d_init = fw.dsem("d_init")
    d_ph = fw.dsem("d_ph")
    d_ph2 = fw.dsem("d_ph2")
    d_initp = fw.dsem("d_initp")
    d_php = fw.dsem("d_php")
    d_x = fw.dsem("d_x")
    d_x2 = fw.dsem("d_x2")
    d_out = fw.dsem("d_out")
    d_sv = fw.dsem("d_sv")
    d_dbg = fw.dsem("d_dbg")
    for b_ in (sc_ktl, sc_bct, sc_fb, sc_cos, sc_sin, sc_rho):
        b_.dsem = d_sv

    fw.banks = [fw.psum(f"bank{i}", [128, 512], F32) for i in range(8)]
    vecs = fw.sbuf("vecs", [128, NV], F32, d_init)
    ones_ln = fw.sbuf("ones_ln", [128, 128], BF16)
    ones_rms = fw.sbuf("ones_rms", [128, 128], BF16)
    ones_1 = fw.sbuf("ones_1", [128, 128], BF16)
    zeros_bf = fw.sbuf("zeros_bf", [128, 128], BF16)
    cneg = fw.sbuf("cneg", [128, 4], F32)
    cneg2 = fw.sbuf("cneg2", [128, 4], F32)
    KTs = fw.sbuf("KTs", [128, 8, 2 * MEM], BF16)
    Vs = fw.sbuf("Vs", [128, 4, D], BF16)
    YS5 = fw.sbuf("YS5", [128, 4, SEQ], BF16)
    YLRU = fw.sbuf("YLRU", [128, 4, SEQ], BF16)
    ws = WS(fw, 4, 4096)

    def vc(name, i):
        c = VC[name] + i
        return vecs[:, c:c + 1]

    def layernorm_g(r, gname, bname, N, scr_xb, scr_sq, tm, out_bf=None, keep_f32=True, dst_f32=None):
        if dst_f32 is None:
            dst_f32 = r
        fw.cp("dve", scr_xb, r)
        fw.act(scr_sq, r, AF.Square)
        yield
        pm = fw.bank()
        pq = fw.bank()
        for kt in range(8):
            fw.mm(pm[:, 0:N], ones_ln.v(), scr_xb[:, kt, :], kt == 0, kt == 7)
        for kt in range(8):
            fw.mm(pq[:, 0:N], ones_ln.v(), scr_sq[:, kt, :], kt == 0, kt == 7)
        mean, t = tm
        fw.act(t, pm[:, 0:N], AF.Square)
        fw.act(mean, pm[:, 0:N], AF.Copy)
        fw.stt("dve", t, pq[:, 0:N], LN_EPS, t, ALU.add, ALU.subtract)
        fw.act(t, t, AF.Ln)
        fw.act(t, t, AF.Exp, scale=-0.5)
        fw.tt("dve", r, r, mean.un(1).bc([128, 8, N]), ALU.subtract)
        fw.tt("dve", r, r, t.un(1).bc([128, 8, N]), ALU.mult)
        yield
        for kt in range(8):
            if keep_f32:
                fw.act(dst_f32[:, kt, :], r[:, kt, :], AF.Identity, bias=vc(bname, kt), scale=vc(gname, kt))
            else:
                fw.act(out_bf[:, kt, :], r[:, kt, :], AF.Identity, bias=vc(bname, kt), scale=vc(gname, kt))
        if keep_f32 and out_bf is not None:
            fw.cp("dve", out_bf, dst_f32)
        yield

    def layernorm(*a, **kw):
        for _ in layernorm_g(*a, **kw):
            pass

    GC1 = 2.0 * math.sqrt(2.0 / math.pi)

    def gelu_tanh(dst, ps, tmp):
        fw.act(tmp, ps, AF.Square)
        fw.stt("dve", tmp, tmp, 1.0 / 0.044715, ps, ALU.add, ALU.mult)
        fw.act(tmp, tmp, AF.Sigmoid, scale=GC1 * 0.044715)
        fw.tt("dve", dst, tmp, ps, ALU.mult)

    def rmsnorm_finish(y32, gname, out_bf, sq4, rs):
        fw.act(sq4, y32, AF.Square)
        ps = fw.bank()
        for t in range(4):
            fw.mm(ps.v(), ones_rms.v(), sq4[:, t, :], t == 0, t == 3)
        fw.act(rs, ps.v(), AF.Ln, bias=RMS_EPS)
        fw.act(rs, rs, AF.Exp, scale=-0.5)
        for t in range(4):
            fw.stt("dve", out_bf[:, t, :], y32[:, t, :], vc(gname, t), rs, ALU.mult, ALU.mult)

    def linear_g(wb, nslots, G, KT, rhs_fn, epi, N=NT):
        for s_ in range(nslots):
            slot = ws.get(wb[s_], G * KT * 128)
            sv = slot.re("p (g kt c) -> p g kt c", g=G, kt=KT)
            for g in range(G):
                m = s_ * G + g
                ps = fw.bank()
                for kt in range(KT):
                    fw.mm(ps[:, 0:N], sv[:, g, kt, :], rhs_fn(kt), kt == 0, kt == KT - 1)
                epi(m, ps[:, 0:N])
            yield

    def linear(*a, **kw):
        for _ in linear_g(*a, **kw):
            pass

    def interleave(gy, gx, ny, nx):
        xi = 0
        for yi in range(ny):
            next(gy, None)
            tgt = ((yi + 1) * nx) // ny if gx is not None else 0
            while xi < tgt:
                next(gx, None)
                xi += 1
        for _ in gy:
            pass
        if gx is not None:
            for _ in gx:
                pass

    def emit():
        fw.dma("sp", vecs.v(), vecs_d)
        fw.memset("dve", ones_ln.v(), 1.0 / 1024.0)
        fw.memset("dve", ones_rms.v(), 1.0 / 512.0)
        fw.memset("dve", ones_1.v(), 1.0)
        fw.memset("dve", zeros_bf.v(), 0.0)
        lamv = vecs[:, VC["lam"]:VC["lam"] + 4]
        fw.act(cneg.v(), lamv, AF.Exp, scale=-1.0)
        fw.ts("dve", cneg.v(), cneg.v(), 1.0, ALU.add)
        fw.act(cneg.v(), cneg.v(), AF.Ln)
        fw.ts("dve", cneg.v(), cneg.v(), -8.0, ALU.mult)
        fw.ts("dve", cneg2.v(), cneg.v(), 2.0, ALU.mult)

        with fw.scope():
            ident = fw.sbuf("ident", [128, 128], F32, d_ph)
            sl = fw.sbuf("sl", [128, 3, 16], F32, d_ph)
            bsl = fw.sbuf("bsl", [128, 2, 16, 32], F32, d_ph)
            csl = fw.sbuf("csl", [128, 2, 16, 32], F32, d_ph)
            fw.dma("sp", ident.v(), ident_d)
            fw.dma("sp", sl.v(), s5sl_d)
            fw.dma("sp", bsl.v(), bsl_d)
            fw.dma("sp", csl.v(), csl_d)
            SM = fw.sbuf("SM", [128, 48, 16], F32)
            _k = [0]

            def sm():
                _k[0] += 1
                return SM[:, _k[0] - 1, :]

            a_re, a_im, logdt = sl[:, 0, :], sl[:, 1, :], sl[:, 2, :]
            dt_, are, ang, mag = sm(), sm(), sm(), sm()
            fw.act(dt_, logdt, AF.Exp)
            fw.tt("dve", are, dt_, a_re, ALU.mult)
            fw.tt("dve", ang, dt_, a_im, ALU.mult)
            fw.act(mag, are, AF.Exp)
            tmp, tmp2 = sm(), sm()

            TWO_PI_HI = float(np.float32(2.0 * PI))
            TWO_PI_LO = float(np.float32(2.0 * PI - TWO_PI_HI))
            HALF_PI_HI = float(np.float32(PI / 2))
            HALF_PI_LO = float(np.float32(PI / 2 - HALF_PI_HI))

            def sin_reduced(dst, src):
                red, kf = sm(), sm()
                fw.ts("dve", kf, src, PI, ALU.is_gt)
                for k in range(1, 6):
                    fw.ts("dve", tmp, src, (2 * k + 1) * PI, ALU.is_gt)
                    fw.tt("dve", kf, kf, tmp, ALU.add)
                fw.stt("dve", red, kf, -TWO_PI_HI, src, ALU.mult, ALU.add)
                fw.stt("dve", red, kf, -TWO_PI_LO, red, ALU.mult, ALU.add)
                fw.act(dst, red, AF.Sin)

            sn, cs, angc = sm(), sm(), sm()
            sin_reduced(sn, ang)
            fw.ts("dve", angc, ang, HALF_PI_HI, ALU.add, HALF_PI_LO, ALU.add)
            sin_reduced(cs, angc)
            lr, li = sm(), sm()
            fw.tt("dve", lr, mag, cs, ALU.mult)
            fw.tt("dve", li, mag, sn, ALU.mult)
            lm1, den, qre, qim = sm(), sm(), sm(), sm()
            fw.ts("dve", lm1, lr, -1.0, ALU.add)
            fw.tt("dve", den, a_re, a_re, ALU.mult)
            fw.tt("dve", tmp, a_im, a_im, ALU.mult)
            fw.tt("dve", den, den, tmp, ALU.add)
            fw.recip(den, den)
            fw.tt("dve", qre, lm1, a_re, ALU.mult)
            fw.tt("dve", tmp, li, a_im, ALU.mult)
            fw.tt("dve", qre, qre, tmp, ALU.add)
            fw.tt("dve", qre, qre, den, ALU.mult)
            fw.tt("dve", qim, li, a_re, ALU.mult)
            fw.tt("dve", tmp, lm1, a_im, ALU.mult)
            fw.tt("dve", qim, qim, tmp, ALU.subtract)
            fw.tt("dve", qim, qim, den, ALU.mult)

            def b3(v):
                return v.un(2).bc([128, 16, 32])

            with fw.scope():
                Gb = [fw.sbuf(f"G{i}", [128, 2, 16, 32], F32) for i in range(2)]
                Fb = [fw.sbuf(f"F{i}", [128, 2, 16, 32], F32) for i in range(2)]
                F0 = fw.sbuf("F0", [128, 2, 16, 32], F32)
                T1 = fw.sbuf("T1", [128, 16, 32], F32)
                T2 = fw.sbuf("T2", [128, 16, 32], F32)
                KTl = fw.sbuf("KTl", [128, 4, 8, 128], BF16)
                BcT = fw.sbuf("BcT", [128, 4, 8, 2, 128], BF16)
                FB = fw.sbuf("FB", [128, 2, 8, 16, 32], BF16)
                fw.memset("dve", KTl.v(), 0.0)

                T3 = fw.sbuf("T3", [128, 16, 32], F32)
                T4 = fw.sbuf("T4", [128, 16, 32], F32)

                def cmul(dst, src, xr, xi, conj=False, e="dve"):
                    Ta, Tb = (T1, T2) if e == "dve" else (T3, T4)
                    sre, sim = src[:, 0], src[:, 1]
                    fw.tt(e, Ta.v(), sre, b3(xr), ALU.mult)
                    fw.tt(e, Tb.v(), sim, b3(xi), ALU.mult)
                    fw.tt(e, dst[:, 0], Ta.v(), Tb.v(), ALU.add if conj else ALU.subtract)
                    fw.tt(e, Ta.v(), sre, b3(xi), ALU.mult)
                    fw.tt(e, Tb.v(), sim, b3(xr), ALU.mult)
                    fw.tt(e, dst[:, 1], Tb.v(), Ta.v(), ALU.subtract if conj else ALU.add)

                lrp = fw.sbuf("lrp", [128, 2, 16], F32)
                fw.cp("dve", lrp[:, 0, :], lr)
                fw.cp("dve", lrp[:, 1, :], li)
                fw.cp("dve", F0[:, 0], csl[:, 0])
                fw.ts("dve", F0[:, 1], csl[:, 1], -1.0, ALU.mult)
                cmul(Gb[0].v(), bsl.v(), qre, qim)
                dvec = vecs[:, VC["s5_d"]:VC["s5_d"] + 4]
                for n in range(8):
                    G = Gb[n % 2]
                    for j in range(4):
                        kb = fw.bank()
                        for k in range(4):
                            q = 4 * j + k
                            for ri in range(2):
                                fw.mm(kb[32 * k:32 * k + 32, 32 * k:32 * k + 32], G[:, ri, q, :], F0[:, ri, q, :],
                                      ri == 0, ri == 1, tile_position=(0, 32 * k))
                        for k in range(4):
                            sl_ = slice(32 * k, 32 * k + 32)
                            if n == 0:
                                fw.stt("dve", KTl[sl_, j, 0, sl_], ident[sl_, sl_], dvec[sl_, j:j + 1], kb[sl_, sl_],
                                       ALU.mult, ALU.add)
                            else:
                                fw.cp("act", KTl[sl_, j, n, sl_], kb[sl_, sl_])
                    for j in range(4):
                        tb = fw.bank()
                        for ri in range(2):
                            fw.transpose(tb[:, ri * 128:(ri + 1) * 128], G[:, ri, 4 * j:4 * j + 4, :].re("p a b -> p (a b)"),
                                         ident.v())
                        fw.cp("act", BcT[:, j, n, :, :].re("p a b -> p (a b)"), tb[:, 0:256])
                    if n < 7:
                        cmul(Gb[(n + 1) % 2].v(), G.v(), lr, li)
                prev = F0
                for n in range(1, 9):
                    cur = Fb[n % 2]
                    cmul(cur.v(), prev.v(), lrp[:, 0, :], lrp[:, 1, :], conj=True, e="pool")
                    for ri in range(2):
                        fw.cp("pool", FB[:, ri, n - 1], cur[:, ri])
                    prev = cur
                pr, pi_, pr2, pi2 = sm(), sm(), sm(), sm()
                fw.cp("dve", pr, lr)
                fw.cp("dve", pi_, li)
                for _ in range(3):
                    fw.tt("dve", pr2, pr, pr, ALU.mult)
                    fw.tt("dve", tmp, pi_, pi_, ALU.mult)
                    fw.tt("dve", pr2, pr2, tmp, ALU.subtract)
                    fw.tt("dve", pi2, pr, pi_, ALU.mult)
                    fw.ts("dve", pi2, pi2, 2.0, ALU.mult)
                    fw.cp("dve", pr, pr2)
                    fw.cp("dve", pi_, pi2)
                rho8 = sm()
                fw.tt("dve", rho8, pr, pr, ALU.mult)
                fw.tt("dve", tmp, pi_, pi_, ALU.mult)
                fw.tt("dve", rho8, rho8, tmp, ALU.add)
                fw.act(rho8, rho8, AF.Sqrt)
                ure, uim = sm(), sm()
                fw.recip(tmp, rho8)
                fw.tt("dve", ure, pr, tmp, ALU.mult)
                fw.tt("dve", uim, pi_, tmp, ALU.mult)
                fw.dma("sp", sc_ktl.v(), KTl.v().re("p a b c -> p (a b c)"))
                fw.dma("sp", sc_bct.v(), BcT.v().re("p a b c d -> p (a b c d)"))
                fw.dma("sp", sc_fb.v(), FB.v().re("p a b c d -> p (a b c d)"))
                fw.dma("sp", sc_rho.v(), rho8)
            COS = fw.sbuf("COS", [128, 16, 256], F32)
            SIN = fw.sbuf("SIN", [128, 16, 256], F32)
            TA = fw.sbuf("TA", [128, 16, 128], F32)
            TB = fw.sbuf("TB", [128, 16, 128], F32)
            fw.memset("dve", COS[:, :, 0:1], 1.0)
            fw.memset("dve", SIN[:, :, 0:1], 0.0)
            for kk in range(8):
                L = 1 << kk

                def bL(v):
                    return v.un(2).bc([128, 16, L])

                c0, s0 = COS[:, :, 0:L], SIN[:, :, 0:L]
                fw.tt("dve", TA[:, :, 0:L], c0, bL(ure), ALU.mult)
                fw.tt("dve", TB[:, :, 0:L], s0, bL(uim), ALU.mult)
                fw.tt("dve", COS[:, :, L:2 * L], TA[:, :, 0:L], TB[:, :, 0:L], ALU.subtract)
                fw.tt("dve", TA[:, :, 0:L], s0, bL(ure), ALU.mult)
                fw.tt("dve", TB[:, :, 0:L], c0, bL(uim), ALU.mult)
                fw.tt("dve", SIN[:, :, L:2 * L], TA[:, :, 0:L], TB[:, :, 0:L], ALU.add)
                if kk < 7:
                    fw.tt("dve", pr2, ure, ure, ALU.mult)
                    fw.tt("dve", tmp, uim, uim, ALU.mult)
                    fw.tt("dve", pr2, pr2, tmp, ALU.subtract)
                    fw.tt("dve", pi2, ure, uim, ALU.mult)
                    fw.ts("dve", pi2, pi2, 2.0, ALU.mult)
                    fw.cp("dve", ure, pr2)
                    fw.cp("dve", uim, pi2)
            fw.dma("sp", sc_cos.v(), COS.v().re("p a b -> p (a b)"))
            fw.dma("sp", sc_sin.v(), SIN.v().re("p a b -> p (a b)"))

        with fw.scope():
            memt = fw.sbuf("memt", [128, 8, 2 * MEM], F32, d_ph)
            scr1 = fw.sbuf("scr1", [128, 8, 2 * MEM], BF16)
            scr2 = fw.sbuf("scr2", [128, 8, 2 * MEM], BF16)
            memb = fw.sbuf("memb", [128, 8, 2 * MEM], BF16)
            wv = fw.sbuf("wv", [128, 8, D], BF16, d_php)
            tmA = fw.sbuf("tmA", [128, 2 * MEM], F32)
            tmB = fw.sbuf("tmB", [128, 2 * MEM], F32)
            fw.dma("sp", memt.v(), memT_v)
            fw.dma("pool", wv.v(), wv_d.re("(kt p) n -> p kt n", p=128))
            layernorm(memt.v(), "mem_g", "mem_b", 2 * MEM, scr1.v(), scr2.v(), [tmA.v(), tmB.v()],
                      out_bf=memb.v(), keep_f32=False)
            linear(wb_k, 2, 4, 8, lambda kt: memb[:, kt, :], lambda m, ps: fw.cp("act", KTs[:, m, :], ps), N=2 * MEM)
            for sm_ in range(4):
                for half in range(2):
                    ps = fw.bank()
                    for kt in range(8):
                        fw.mm(ps.v(), memb[:, kt, sm_ * 128:(sm_ + 1) * 128], wv[:, kt, half * 512:(half + 1) * 512],
                              kt == 0, kt == 7)
                    fw.cp("act", Vs[:, sm_, half * 512:(half + 1) * 512], ps.v())

        for s in range(2):
            with fw.scope():
                U = fw.sbuf("U", [128, 4, SEQ], BF16)
                Uj = [fw.sub(U, f"U{j}") for j in range(4)]
                with fw.scope():
                    XLp = fw.sbuf("XLp", [128, 4, SEQ + 4], BF16)
                    GG = fw.sbuf("GG", [128, 4, SEQ], BF16)
                    fw.memset("dve", XLp[:, :, 0:4], 0.0)
                    with fw.scope():
                        xts = [fw.sbuf("xt0", [128, 8, NT], F32, d_x), fw.sbuf("xt1", [128, 8, NT], F32, d_x2)]
                        hbs = [fw.sbuf("hb0", [128, 8, NT], BF16), fw.sbuf("hb1", [128, 8, NT], BF16)]
                        scr1 = fw.sbuf("scr1", [128, 8, NT], BF16)
                        scr2 = fw.sbuf("scr2", [128, 8, NT], BF16)
                        tmA = fw.sbuf("tmA", [128, NT], F32)
                        tmB = fw.sbuf("tmB", [128, NT], F32)
                        gtms = [fw.sbuf(f"gtm{i}", [128, NT], F32) for i in range(4)]

                        def lnA(i):
                            tok0 = s * SEQ + i * NT
                            fw.dma("sp", xts[i % 2].v(), xT_v[:, :, tok0:tok0 + NT])
                            yield from layernorm_g(xts[i % 2].v(), "ln_in_g", "ln_in_b", NT, scr1.v(), scr2.v(),
                                                   [tmA.v(), tmB.v()], out_bf=hbs[i % 2].v(), keep_f32=False)

                        def linA(i):
                            hb = hbs[i % 2]

                            def epiA(m, ps):
                                if m < 4:
                                    dst = V(Uj[m], U.t[:, m, :]).re("p (t c) -> p t c", t=8)[:, :, i * 64:(i + 1) * 64]
                                    fw.cp("act", dst, ps.re("p (c t) -> p t c", t=8))
                                elif m < 8:
                                    fw.cp("act", XLp[:, m - 4, 4 + i * NT:4 + (i + 1) * NT], ps)
                                else:
                                    gelu_tanh(GG[:, m - 8, i * NT:(i + 1) * NT], ps, gtms[m - 8].v())

                            yield from linear_g(wb_in, 3, 4, 8, lambda kt: hb[:, kt, :], epiA)

                        for _ in lnA(0):
                            pass
                        for i in range(4):
                            gl = linA(i)
                            gn = lnA(i + 1) if i < 3 else iter(())
                            for _ in range(3):
                                next(gn, None)
                                next(gl, None)
                            for _ in gn:
                                pass
                            for _ in gl:
                                pass
                    with fw.scope():
                        wab = fw.sbuf("wab", [128, 2, 4, 128], BF16, d_php)
                        fw.dma("pool", wab.v(), wab_d)
                        xc = fw.sbuf("xc", [128, 4, NT], F32)
                        xcb = fw.sbuf("xcb", [128, 4, NT], BF16)
                        hs = [fw.sbuf(f"hs{i}", [128, 4, NT], F32) for i in range(2)]
                        y32 = fw.sbuf("y32", [128, 4, NT], F32)
                        sq4 = fw.sbuf("sq4", [128, 4, NT], BF16)
                        rs = fw.sbuf("rs", [128, NT], F32)
                        rrs = [fw.sbuf(f"rr{t}", [128, NT], F32) for t in range(4)]
                        igs = [fw.sbuf(f"ig{t}", [128, NT], F32) for t in range(4)]
                        aas = [fw.sbuf(f"aa{t}", [128, NT], F32) for t in range(4)]
                        mus = [fw.sbuf(f"mu{t}", [128, NT], F32) for t in range(4)]
                        xcs = [fw.sub(xc, f"xc{t}") for t in range(4)]
                        xcbs = [fw.sub(xcb, f"xcb{t}") for t in range(4)]
                        y32s = [fw.sub(y32, f"y32_{t}") for t in range(4)]
                        hss = [[fw.sub(hs[k], f"hs{k}_{t}") for t in range(4)] for k in range(2)]
                        cw = VC["conv_w"]

                        def chain(i, t):
                            rr, ig, aa, mu = rrs[t], igs[t], aas[t], mus[t]
                            xcv = V(xcs[t], xc.t[:, t, :])
                            xcbv = V(xcbs[t], xcb.t[:, t, :])
                            xin = XLp[:, t, i * NT:i * NT + NT + 4]
                            fw.act(xcv, xin[:, 4:4 + NT], AF.Identity, bias=vc("conv_b", t),
                                   scale=vecs[:, cw + 4 * t + 3:cw + 4 * t + 4])
                            yield
                            for k in range(3):
                                fw.stt("dve", xcv, xin[:, 1 + k:1 + k + NT],
                                       vecs[:, cw + 4 * t + k:cw + 4 * t + k + 1], xcv, ALU.mult, ALU.add)
                                yield
                            fw.cp("dve", xcbv, xcv)
                            yield
                            psr = fw.bank()
                            fw.mm(psr.v(), wab[:, 0, t, :], xcbv, True, True)
                            psi = fw.bank()
                            fw.mm(psi.v(), wab[:, 1, t, :], xcbv, True, True)
                            fw.act(rr.v(), psr.v(), AF.Sigmoid, bias=vc("b_a", t))
                            fw.act(ig.v(), psi.v(), AF.Sigmoid, bias=vc("b_x", t))
                            yield
                            fw.act(aa.v(), rr.v(), AF.Exp, scale=cneg[:, t:t + 1])
                            fw.tt("dve", ig.v(), ig.v(), xcv, ALU.mult)
                            yield
                            fw.act(mu.v(), rr.v(), AF.Exp, scale=cneg2[:, t:t + 1])
                            yield
                            fw.ts("dve", mu.v(), mu.v(), -1.0, ALU.mult, 1.0, ALU.add)
                            yield
                            fw.act(mu.v(), mu.v(), AF.Sqrt)
                            yield
                            fw.tt("dve", ig.v(), ig.v(), mu.v(), ALU.mult)
                            yield
                            hcv = V(hss[i % 2][t], hs[i % 2].t[:, t, :])
                            init = 0.0 if i == 0 else V(hss[(i + 1) % 2][t], hs[(i + 1) % 2].t[:, t, NT - 1:NT])
                            fw.scan(hcv, aa.v(), ig.v(), init)
                            yield
                            fw.tt("dve", V(y32s[t], y32.t[:, t, :]), hcv, GG[:, t, i * NT:(i + 1) * NT], ALU.mult)
                            yield

                        def rms_lru(i):
                            y4 = [V(y32s[t], y32.t[:, t, :]) for t in range(4)]
                            for t in range(4):
                                fw.act(sq4[:, t, :], y4[t], AF.Square)
                            ps = fw.bank()
                            for t in range(4):
                                fw.mm(ps.v(), ones_rms.v(), sq4[:, t, :], t == 0, t == 3)
                            fw.act(rs.v(), ps.v(), AF.Ln, bias=RMS_EPS)
                            fw.act(rs.v(), rs.v(), AF.Exp, scale=-0.5)
                            for t in range(4):
                                fw.stt("dve", YLRU[:, t, i * NT:(i + 1) * NT], y4[t], vc("g_lru", t), rs.v(),
                                       ALU.mult, ALU.mult)

                        for i in range(4):
                            gens = [chain(i, t) for t in range(4)]
                            alive = True
                            while alive:
                                alive = False
                                for g in gens:
                                    if next(g, "done") != "done":
                                        alive = True
                            rms_lru(i)
                Hprev = fw.sbuf("Hprev", [128, 16, 2, 256], BF16)
                KTl = fw.sbuf("KTl", [128, 4, 8, 128], BF16, d_ph2)
                FB = fw.sbuf("FB", [128, 2, 8, 16, 32], BF16, d_ph2)
                wglu = fw.sbuf("wglu", [128, 4, 512], BF16, d_php)
                with fw.scope():
                    BcT = fw.sbuf("BcT", [128, 4, 8, 2, 128], BF16, d_ph)
                    COS = fw.sbuf("COS", [128, 16, 256], F32, d_ph)
                    SIN = fw.sbuf("SIN", [128, 16, 256], F32, d_ph)
                    RHO = fw.sbuf("RHO", [128, 16], F32, d_ph)
                    fw.dma("sp", BcT.v().re("p a b c d -> p (a b c d)"), sc_bct.v())
                    fw.dma("sp", COS.v().re("p a b -> p (a b)"), sc_cos.v())
                    fw.dma("sp", SIN.v().re("p a b -> p (a b)"), sc_sin.v())
                    fw.dma("sp", RHO.v(), sc_rho.v())
                    fw.dma("sp", KTl.v().re("p a b c -> p (a b c)"), sc_ktl.v())
                    fw.dma("sp", FB.v().re("p a b c d -> p (a b c d)"), sc_fb.v())
                    fw.dma("pool", wglu.v(), wglu_d.re("(kt p) n -> p kt n", p=128))
                    A_ = fw.sbuf("A_", [128, 2, 256], F32)
                    B_ = fw.sbuf("B_", [128, 2, 256], F32)
                    R_ = fw.sbuf("R_", [128, 2, 256], F32)
                    Qs = [fw.sbuf(f"Q_{i}", [128, 2, 256], F32) for i in range(2)]
                    A2s = [fw.sbuf(f"A2_{i}", [128, 2, 256], F32) for i in range(2)]
                    B2s = [fw.sbuf(f"B2_{i}", [128, 2, 256], F32) for i in range(2)]
                    fw.memset("dve", Hprev[:, :, :, 0:1], 0.0)
                    for q in range(16):
                        j, k = divmod(q, 4)
                        Q_ = Qs[q % 2]
                        ps = fw.bank()
                        sl_ = slice(32 * k, 32 * k + 32)
                        for ri in range(2):
                            for s_ in range(8):
                                fw.mm(ps[:, ri * 256:(ri + 1) * 256], BcT[sl_, j, 7 - s_, ri, :],
                                      V(Uj[j], U.t[sl_, j, s_ * 256:(s_ + 1) * 256]), s_ == 0, s_ == 7,
                                      tile_position=(32 * k, 0))
                        S = ps.v().re("p (r c) -> p r c", r=2)
                        cb_ = COS[:, q, :].un(1).bc([128, 2, 256])
                        sb_ = SIN[:, q, :].un(1).bc([128, 2, 256])
                        fw.tt("dve", A_.v(), S, cb_, ALU.mult)
                        fw.tt("dve", B_.v(), S, sb_, ALU.mult)
                        fw.tt("dve", R_[:, 0, :], A_[:, 0, :], B_[:, 1, :], ALU.add)
                        fw.tt("dve", R_[:, 1, :], A_[:, 1, :], B_[:, 0, :], ALU.subtract)
                        rb = RHO[:, q:q + 1].bc([128, 256])
                        fw.scan(Q_[:, 0, :], rb, R_[:, 0, :], 0.0)
                        fw.scan(Q_[:, 1, :], rb, R_[:, 1, :], 0.0)
                        A2, B2 = A2s[q % 2], B2s[q % 2]
                        fw.tt("pool", A2.v(), Q_.v(), cb_, ALU.mult)
                        fw.tt("pool", B2.v(), Q_.v(), sb_, ALU.mult)
                        fw.tt("pool", Hprev[:, q, 0, 1:256], A2[:, 0, 0:255], B2[:, 1, 0:255], ALU.subtract)
                        fw.tt("pool", Hprev[:, q, 1, 1:256], A2[:, 1, 0:255], B2[:, 0, 0:255], ALU.add)
                    if debug and s == 0:
                        fw.dma("sp", dbg["d_hprev"], Hprev.v(), dsem=d_dbg)
                with fw.scope():
                    gtm2s = [fw.sbuf(f"gtm2_{i}", [128, NT], F32) for i in range(4)]
                    if debug and s == 0:
                        for j in range(4):
                            fw.dma("sp", dbg["d_u"][:, j, :], V(Uj[j], U.t[:, j, :]), dsem=d_dbg)
                    for j in range(4):
                        bk = [fw.bank() for _ in range(4)]
                        Ujv = V(Uj[j], U.t[:, j, :]).re("p (t c) -> p t c", t=8)
                        for b in range(4):
                            for d_ in range(0, 2 * b + 2):
                                if 2 * b >= d_:
                                    fw.mm(bk[b].v(), KTl[:, j, d_, :],
                                          Ujv[:, 2 * b - d_:2 * b - d_ + 2, :].re("p t c -> p (t c)"), d_ == 0, False)
                                else:
                                    fw.mm(bk[b][:, 256:512], KTl[:, j, d_, :], Ujv[:, 0, :], False, False)
                            for k in range(4):
                                q = 4 * j + k
                                sl_ = slice(32 * k, 32 * k + 32)
                                for tl in range(2):
                                    tau = 2 * b + tl
                                    for ri in range(2):
                                        last = False
                                        fw.mm(bk[b][sl_, tl * 256:(tl + 1) * 256], FB[:, ri, tau, q, :],
                                              Hprev[:, q, ri, :], False, last, tile_position=(0, 32 * k))
                        for b in range(4):
                            fw.mm(bk[b].v(), zeros_bf.v(), Ujv[:, 0:2, :].re("p t c -> p (t c)"), False, True)
                        Ynat = V(Uj[j], U.t[:, j, :]).re("p (c t) -> p t c", t=8)
                        for b in range(4):
                            gelu_tanh(Ynat[:, 2 * b:2 * b + 2, :], bk[b].v().re("p (t c) -> p t c", t=2),
                                      gtm2s[b].v().re("p (t c) -> p t c", t=2))
                    if debug and s == 0:
                        for j in range(4):
                            fw.dma("sp", dbg["d_yg"][:, j, :], V(Uj[j], U.t[:, j, :]), dsem=d_dbg)
                    sgs = [fw.sbuf(f"sg{i}", [128, NT], F32) for i in range(2)]
                    y32s_ = [fw.sbuf(f"y32b{i}", [128, 4, NT], F32) for i in range(2)]
                    sq4s = [fw.sbuf(f"sq4b{i}", [128, 4, NT], BF16) for i in range(2)]
                    rss = [fw.sbuf(f"rsb{i}", [128, NT], F32) for i in range(2)]
                    for i in range(4):
                        tsl = slice(i * NT, (i + 1) * NT)
                        y32 = y32s_[i % 2]
                        for m in range(4):
                            sg = sgs[m % 2]
                            ps = fw.bank()
                            for kt in range(4):
                                fw.mm(ps.v(), wglu[:, kt, m * 128:(m + 1) * 128], V(Uj[kt], U.t[:, kt, tsl]),
                                      kt == 0, kt == 3)
                            fw.act(sg.v(), ps.v(), AF.Sigmoid, bias=vc("b_glu", m))
                            fw.tt("dve", y32[:, m, :], V(Uj[m], U.t[:, m, tsl]), sg.v(), ALU.mult)
                        rmsnorm_finish(y32.v(), "g_s5", YS5[:, :, tsl], sq4s[i % 2].v(), rss[i % 2].v())
                if debug and s == 0:
                    fw.dma("sp", dbg["d_ys5"], YS5.v(), dsem=d_dbg)
                    fw.dma("sp", dbg["d_ylru"], YLRU.v(), dsem=d_dbg)
            with fw.scope():
                AB = [fw.sbuf("fa0", [128, 8, NT], F32), fw.sbuf("fa1", [128, 8, NT], F32)]
                fb_ = fw.sbuf("fb", [128, 8, NT], F32, d_x)
                ba = fw.sbuf("ba", [128, 8, NT], BF16)
                bb = fw.sbuf("bb", [128, 8, NT], BF16)
                h2b = fw.sbuf("h2b", [128, 8, NT], BF16)
                ffb = fw.sbuf("ffb", [128, 32, NT], BF16)
                tmx = [fw.sbuf("tmxA", [128, NT], F32).v(), fw.sbuf("tmxB", [128, NT], F32).v()]
                tmy = [fw.sbuf("tmyA", [128, NT], F32).v(), fw.sbuf("tmyB", [128, NT], F32).v()]
                tmz = [fw.sbuf("tmzA", [128, NT], F32).v(), fw.sbuf("tmzB", [128, NT], F32).v()]
                Es = [fw.sbuf(f"E{i}", [128, 2, NT], BF16) for i in range(2)]
                rdens = [fw.sbuf(f"rden{i}", [128, NT], F32) for i in range(2)]
                rl = fw.sbuf("rl", [128, NT], F32)

                def X_gen(i):
                    fa = AB[i % 2]
                    tok0 = s * SEQ + i * NT
                    tsl = slice(i * NT, (i + 1) * NT)
                    fw.dma("sp", fb_.v(), xT_v[:, :, tok0:tok0 + NT])
                    layernorm(fb_.v(), "ln_in_g", "ln_in_b", NT, ba.v(), bb.v(), tmz)
                    yield
                    yield from linear_g(wb_mix, 2, 4, 8,
                                        lambda kt: (YS5[:, kt, tsl] if kt < 4 else YLRU[:, kt - 4, tsl]),
                                        lambda m, ps: fw.stt("dve", fb_[:, m, :], fb_[:, m, :], ALPHA, ps, ALU.mult, ALU.add))
                    layernorm(fb_.v(), "ln1_g", "ln1_b", NT, ba.v(), bb.v(), tmx, out_bf=ba.v())
                    if debug and s == 0 and i == 0:
                        fw.dma("sp", dbg["d_h1"], fb_.v(), dsem=d_dbg)
                    yield
                    yield from linear_g(wb_q, 2, 4, 8, lambda kt: ba[:, kt, :],
                                        lambda m, ps: fw.act(bb[:, m, :], ps, AF.Copy, scale=0.0625))
                    for hp in range(2):
                        heads = (2 * hp, 2 * hp + 1)
                        pss = {}
                        for hd in heads:
                            for mt in range(2):
                                ps = fw.bank()
                                for kk in range(2):
                                    fw.mm(ps.v(), KTs[:, 2 * hd + kk, s * MEM + mt * 128:s * MEM + (mt + 1) * 128],
                                          bb[:, 2 * hd + kk, :], kk == 0, kk == 1)
                                pss[(hd, mt)] = ps
                        for hd in heads:
                            for mt in range(2):
                                fw.act(Es[hd % 2][:, mt, :], pss[(hd, mt)].v(), AF.Exp)
                        for hd in heads:
                            E, rden = Es[hd % 2], rdens[hd % 2]
                            pd = fw.bank()
                            for mt in range(2):
                                fw.mm(pd.v(), ones_1.v(), E[:, mt, :], mt == 0, mt == 1)
                            fw.act(rden.v(), pd.v(), AF.Ln)
                            fw.act(rden.v(), rden.v(), AF.Exp, scale=-1.0)
                            for dt in range(2):
                                po = fw.bank()
                                for mt in range(2):
                                    fw.mm(po.v(), Vs[:, s * 2 + mt, (2 * hd + dt) * 128:(2 * hd + dt + 1) * 128],
                                          E[:, mt, :], mt == 0, mt == 1)
                                fw.tt("dve", ba[:, 2 * hd + dt, :], po.v(), rden.v(), ALU.mult)
                        yield
                    yield from linear_g(wb_o, 2, 4, 8, lambda kt: ba[:, kt, :],
                                        lambda m, ps: fw.stt("dve", fa[:, m, :], fb_[:, m, :], ALPHA, ps, ALU.mult, ALU.add))
                    layernorm(fa.v(), "ln2_g", "ln2_b", NT, h2b.v(), bb.v(), tmx, out_bf=h2b.v())
                    if debug and s == 0 and i == 0:
                        fw.dma("sp", dbg["d_h2"], fa.v(), dsem=d_dbg)
                    yield

                def Y_gen(i):
                    h2 = AB[i % 2]
                    tok0 = s * SEQ + i * NT

                    def epiF(m, ps):
                        fw.act(rl.v(), ps, AF.Relu)
                        if m % 2 == 0:
                            fw.act(ffb[:, m, :], rl.v(), AF.Square)
                        else:
                            fw.tt("dve", ffb[:, m, :], rl.v(), rl.v(), ALU.mult)

                    yield from linear_g(wb_ff1, 8, 4, 8, lambda kt: h2b[:, kt, :], epiF)
                    yield from linear_g(wb_ff2, 8, 1, 32, lambda kt: ffb[:, kt, :],
                                        lambda m, ps: fw.stt("dve", h2[:, m, :], h2[:, m, :], ALPHA, ps, ALU.mult, ALU.add))
                    layernorm(h2.v(), "ln3_g", "ln3_b", NT, ffb[:, 0:8, :], ffb[:, 8:16, :], tmy)
                    fw.dma("sp", outT_v[:, :, tok0:tok0 + NT], h2.v(), dsem=d_out)
                    yield

                SCHED0 = "y x y y x x x y y y y x y x y x y x y x y x y x y x x n y n y n y"
                SCHED1 = "y x y y y y y x x y x y x y x y x x y y y y n y n y n y"
                gens = {i: X_gen(i) for i in range(4)}
                for _ in gens[0]:
                    pass
                for _ in range(3):
                    next(gens[1], None)
                for i in range(4):
                    gy = Y_gen(i)
                    gx = gens.get(i + 1)
                    gn = gens.get(i + 2)
                    for c in SCHED1.split():
                        if c == "y":
                            next(gy, None)
                        elif c == "x":
                            if gx is not None:
                                next(gx, None)
                        elif gn is not None:
                            next(gn, None)
                    for _ in gy:
                        pass
                    if gx is not None:
                        for _ in gx:
                            pass

    fw.dry = True
    emit()
    fw.dry = False
    ws.reset_for_real()
    fw.bank_i = 0
    emit()
    fw._wait("sp", ("dma", d_out, d_out.count))
    if debug:
        fw._wait("sp", ("dma", d_dbg, d_dbg.count))
    fw.barrier()
    fw.root.close()
    return nc, fw


def _blk(W, G):
    K, N = W.shape
    KT, MT = K // 128, N // 128
    a = W.reshape(KT, 128, MT // G, G, 128).transpose(2, 1, 3, 0, 4)
    return np.ascontiguousarray(a.reshape(MT // G, 128, G * KT * 128)).astype(np.float32)


def host_layout(inp):
    f = lambda k: np.asarray(inp[k], dtype=np.float32)
    shared = {}
    shared["wb_in"] = _blk(f("w_in")[0], 4)
    shared["wb_mix"] = _blk(f("w_mix_out")[0], 4)
    shared["wb_q"] = _blk(f("w_q")[0], 4)
    shared["wb_k"] = _blk(f("w_k")[0], 4)
    shared["wb_o"] = _blk(f("w_o")[0], 4)
    shared["wb_ff1"] = _blk(f("w_ff1")[0], 4)
    shared["wb_ff2"] = _blk(f("w_ff2")[0], 1)
    shared["w_v"] = np.ascontiguousarray(f("w_v")[0])
    shared["w_glu"] = np.ascontiguousarray(f("s5_w_glu")[0])
    shared["ident"] = np.eye(128, dtype=np.float32)
    vecs = np.zeros((128, NV), np.float32)

    def put8(name, v):
        vecs[:, VC[name]:VC[name] + 8] = v.reshape(8, 128).T

    def put4(name, v):
        vecs[:, VC[name]:VC[name] + 4] = v.reshape(4, 128).T

    put8("ln_in_g", f("ln_in_g")); put8("ln_in_b", f("ln_in_b"))
    put8("ln1_g", f("ln1_g")[0]); put8("ln1_b", f("ln1_b")[0])
    put8("ln2_g", f("ln2_g")[0]); put8("ln2_b", f("ln2_b")[0])
    put8("ln3_g", f("ln3_g")[0]); put8("ln3_b", f("ln3_b")[0])
    put8("mem_g", f("mem_ln_g")[0]); put8("mem_b", f("mem_ln_b")[0])
    put4("b_glu", f("s5_b_glu")[0]); put4("g_s5", f("g_s5")[0]); put4("g_lru", f("g_lru")[0])
    put4("conv_b", f("conv_b")[0]); put4("b_a", f("lru_b_a")[0].reshape(512)); put4("b_x", f("lru_b_x")[0].reshape(512))
    put4("lam", f("lru_lambda")[0]); put4("s5_d", f("s5_d")[0].reshape(512))
    cw = f("conv_w")[0]
    vecs[:, VC["conv_w"]:VC["conv_w"] + 16] = cw.reshape(4, 4, 128).transpose(2, 1, 0).reshape(128, 16)
    shared["vecs"] = vecs
    def sl16(a):
        return a.reshape(16, 2, 64).transpose(1, 2, 0).reshape(128, 16)

    a_re, a_im = f("s5_a_re")[0], f("s5_a_im")[0]
    logdt = np.repeat(f("s5_log_dt")[0][:, None], 64, axis=1)
    shared["s5sl"] = np.ascontiguousarray(np.stack([sl16(a_re), sl16(a_im), sl16(logdt)], axis=1))

    def bpad(b):
        o = np.zeros((2, 64, 16, 2, 16), np.float32)
        bb = b.reshape(16, 2, 64, 16)
        for g2 in range(2):
            o[g2, :, :, g2, :] = bb[:, g2].transpose(1, 0, 2)
        return o.reshape(128, 16, 32)

    def cpad(c):
        return bpad(c.transpose(0, 2, 1))

    shared["bsl"] = np.ascontiguousarray(np.stack([bpad(f("s5_b_re")[0]), bpad(f("s5_b_im")[0])], axis=1))
    shared["csl"] = np.ascontiguousarray(np.stack([cpad(f("s5_c_re")[0]), cpad(f("s5_c_im")[0])], axis=1))

    def bd(w):
        o = np.zeros((2, 64, 4, 2, 64), np.float32)
        ww = w.reshape(4, 2, 64, 64)
        for hh in range(2):
            o[hh, :, :, hh, :] = ww[:, hh].transpose(1, 0, 2)
        return o.reshape(128, 4, 128)

    shared["wab"] = np.ascontiguousarray(np.stack([bd(f("lru_w_a")[0]), bd(f("lru_w_x")[0])], axis=1))
    x = f("x")
    mem = f("mem")
    per_core = []
    for c in range(NCORES):
        d = dict(shared)
        d["xT"] = np.ascontiguousarray(x[2 * c:2 * c + 2].reshape(NTOK, D).T)
        d["memT"] = np.ascontiguousarray(mem[2 * c:2 * c + 2].reshape(2 * MEM, D).T)
        per_core.append(d)
    return per_core


_CACHE = {}


def kernel(**inputs):
    if "nc" not in _CACHE:
        _CACHE["nc"] = build_program(debug=False)[0]
    nc = _CACHE["nc"]
    in_maps = host_layout(inputs)
    res = run_bass_kernel_spmd(nc, in_maps, core_ids=list(range(NCORES)))
    out = np.empty((16, SEQ, D), np.float32)
    for c in range(NCORES):
        o = res.results[c]["outT"]
        out[2 * c:2 * c + 2] = o.T.reshape(2, SEQ, D)
    return out
```

```python
import contextlib
import math
import numpy as np
import concourse.bass as bass
import concourse.mybir as mybir
from concourse.bass_utils import run_bass_kernel_spmd

F32 = mybir.dt.float32
BF16 = mybir.dt.bfloat16
AF = mybir.ActivationFunctionType
ALU = mybir.AluOpType

NCORES = 8
D = 1024
SEQ = 2048
NTOK = 4096
MEM = 256
DFF = 4096
ALPHA = 2.0 ** 0.25
LN_EPS = 1e-5
RMS_EPS = 1e-6
NT = 512
PI = math.pi

VC = {}
_o = 0
for _n in ["ln_in_g", "ln_in_b", "ln1_g", "ln1_b", "ln2_g", "ln2_b", "ln3_g", "ln3_b", "mem_g", "mem_b"]:
    VC[_n] = _o
    _o += 8
for _n in ["b_glu", "g_s5", "g_lru", "conv_b", "b_a", "b_x", "lam"]:
    VC[_n] = _o
    _o += 4
VC["conv_w"] = _o
_o += 16
VC["s5_d"] = _o
_o += 4
NV = _o


class DSem:
    def __init__(self, fw, name):
        self.h = fw.root.enter_context(fw.nc.semaphore(name))
        self.count = 0
        self.name = name


class Buf:
    def __init__(self, name, t=None, dsem=None):
        self.name = name
        self.t = t
        self.w = None
        self.r = []
        self.dsem = dsem

    def v(self):
        return V(self, self.t[:])

    def __getitem__(self, k):
        return V(self, self.t[k])


class V:
    def __init__(self, buf, ap):
        self.buf = buf
        self.ap = ap

    def __getitem__(self, k):
        return V(self.buf, self.ap[k])

    def re(self, pat, **kw):
        return V(self.buf, self.ap.rearrange(pat, **kw))

    def bc(self, shape):
        return V(self.buf, self.ap.to_broadcast(list(shape)))

    def un(self, axis):
        return V(self.buf, self.ap.unsqueeze(axis))


COMPUTE = ("pe", "act", "dve", "pool")


def _bufs(*vs):
    out = []
    for v in vs:
        if isinstance(v, V) and v.buf is not None and v.buf not in out:
            out.append(v.buf)
    return out


def _ap(x):
    return x.ap if isinstance(x, V) else x


class FW:
    def __init__(self, nc, same_engine_sync=True):
        self.nc = nc
        self.root = contextlib.ExitStack()
        self.es = self.root
        self.eng = {"pe": nc.tensor, "act": nc.scalar, "dve": nc.vector, "pool": nc.gpsimd, "sp": nc.sync}
        self.sem = {}
        self.cnt = {}
        for e in COMPUTE:
            self.sem[e] = self.root.enter_context(nc.semaphore("sem_" + e))
            self.cnt[e] = 0
        self.known = {e: {} for e in self.eng}
        self.same_engine_sync = same_engine_sync
        self.dsems = []
        self.dry = False
        self.uid = 0
        self.banks = []
        self.bank_i = 0
        self.ninst = 0

    def _name(self, name):
        self.uid += 1
        return f"{name}_{self.uid}"

    def sbuf(self, name, shape, dt, dsem=None):
        t = self.es.enter_context(self.nc.sbuf_tensor(self._name(name), list(shape), dt))
        return Buf(name, t, dsem)

    def psum(self, name, shape, dt):
        t = self.es.enter_context(self.nc.psum_tensor(self._name(name), list(shape), dt))
        return Buf(name, t)

    def dsem(self, name):
        d = DSem(self, name)
        self.dsems.append(d)
        return d

    def sub(self, parent, name):
        return Buf(name, parent.t, parent.dsem)

    @contextlib.contextmanager
    def scope(self):
        outer = self.es
        self.es = contextlib.ExitStack()
        try:
            yield
        finally:
            self.barrier()
            self.es.close()
            self.es = outer

    def bank(self):
        b = self.banks[self.bank_i % len(self.banks)]
        self.bank_i += 1
        return b

    def _wait(self, e, dep):
        kind, key, val = dep
        if kind == "eng":
            if key == e and (e == "pe" or not self.same_engine_sync):
                return
            kk = ("eng", key)
            if self.known[e].get(kk, 0) >= val:
                return
            self.eng[e].wait_ge(self.sem[key], val)
            self.known[e][kk] = val
        else:
            val = key.count
            kk = ("dma", key.name)
            if self.known[e].get(kk, 0) >= val:
                return
            self.eng[e].wait_ge(key.h, val)
            self.known[e][kk] = val
        self.ninst += 1

    def _deps(self, e, reads, writes):
        for b in reads:
            if b.w is not None:
                self._wait(e, b.w)
        for b in writes:
            if b.w is not None:
                self._wait(e, b.w)
            for d in b.r:
                self._wait(e, d)

    def op(self, e, reads, writes, fn):
        if self.dry:
            return None
        self._deps(e, reads, writes)
        ins = fn(self.eng[e])
        self.cnt[e] += 1
        self.ninst += 1
        ins.then_inc(self.sem[e], 1)
        tag = ("eng", e, self.cnt[e])
        for b in reads:
            if b not in writes:
                b.r.append(tag)
        for b in writes:
            b.w = tag
            b.r = []
        return ins

    def mm(self, out, lhsT, rhs, start, stop, **kw):
        if self.dry:
            return None
        e = "pe"
        psb = out.buf
        rd = _bufs(lhsT, rhs)
        self._deps(e, rd, [psb] if start else [])
        ins = self.nc.tensor.matmul(out.ap, lhsT.ap, rhs.ap, start=start, stop=stop, **kw)
        self.ninst += 1
        if stop:
            self.cnt[e] += 1
            ins.then_inc(self.sem[e], 1)
            tag = ("eng", e, self.cnt[e])
        else:
            tag = ("eng", e, self.cnt[e] + 1)
        for b in rd:
            b.r.append(tag)
        psb.w = tag
        if start:
            psb.r = []
        return ins

    def transpose(self, out, in_, ident):
        if self.dry:
            return None
        e = "pe"
        rd = _bufs(in_, ident)
        self._deps(e, rd, [out.buf])
        ins = self.nc.tensor.transpose(out.ap, in_.ap, ident.ap)
        self.ninst += 1
        self.cnt[e] += 1
        ins.then_inc(self.sem[e], 1)
        tag = ("eng", e, self.cnt[e])
        for b in rd:
            b.r.append(tag)
        out.buf.w = tag
        out.buf.r = []
        return ins

    def dma(self, q, out, in_, dsem=None, **kw):
        if self.dry:
            return None
        reads = _bufs(in_)
        writes = _bufs(out)
        self._deps(q, reads, writes)
        if dsem is None:
            for b in writes + reads:
                if b.dsem is not None:
                    dsem = b.dsem
                    break
        assert dsem is not None, "dma needs a DSem"
        ins = self.eng[q].dma_start(out=out.ap, in_=in_.ap, **kw)
        self.ninst += 1
        dsem.count += 16
        ins.then_inc(dsem.h, 16)
        tag = ("dma", dsem, dsem.count)
        for b in reads:
            b.r.append(tag)
        for b in writes:
            b.w = tag
            b.r = []
        return ins

    def barrier(self, engines=("pe", "act", "dve", "pool", "sp")):
        if self.dry:
            return
        for e in engines:
            for f in COMPUTE:
                if f != e and self.cnt[f] > 0:
                    self._wait(e, ("eng", f, self.cnt[f]))
            for d in self.dsems:
                if d.count > 0:
                    self._wait(e, ("dma", d, d.count))

    def tt(self, e, out, a, b, op):
        return self.op(e, _bufs(a, b), _bufs(out),
                       lambda E: E.tensor_tensor(out=out.ap, in0=a.ap, in1=b.ap, op=op))

    def ts(self, e, out, a, s1, op0, s2=None, op1=None):
        kw = {}
        if op1 is not None:
            kw["op1"] = op1
        return self.op(e, _bufs(a, s1, s2), _bufs(out),
                       lambda E: E.tensor_scalar(out=out.ap, in0=a.ap, scalar1=_ap(s1), scalar2=_ap(s2), op0=op0, **kw))

    def stt(self, e, out, in0, scalar, in1, op0, op1):
        return self.op(e, _bufs(in0, scalar, in1), _bufs(out),
                       lambda E: E.scalar_tensor_tensor(out=out.ap, in0=in0.ap, scalar=_ap(scalar), in1=in1.ap,
                                                        op0=op0, op1=op1))

    def act(self, out, in_, func, bias=None, scale=None):
        kw = {}
        if bias is not None:
            kw["bias"] = _ap(bias)
        if scale is not None:
            kw["scale"] = _ap(scale)
        return self.op("act", _bufs(in_, bias, scale), _bufs(out),
                       lambda E: E.activation(out=out.ap, in_=in_.ap, func=func, **kw))

    def cp(self, e, out, in_):
        if e == "act":
            return self.act(out, in_, AF.Copy)
        return self.op(e, _bufs(in_), _bufs(out), lambda E: E.tensor_copy(out=out.ap, in_=in_.ap))

    def memset(self, e, out, val):
        return self.op(e, [], _bufs(out), lambda E: E.memset(out.ap, val))

    def scan(self, out, d0, d1, init, op0=ALU.mult, op1=ALU.add):
        return self.op("dve", _bufs(d0, d1, init), _bufs(out),
                       lambda E: E.tensor_tensor_scan(out=out.ap, data0=d0.ap, data1=d1.ap, initial=_ap(init),
                                                      op0=op0, op1=op1))

    def recip(self, out, in_):
        return self.op("dve", _bufs(in_), _bufs(out), lambda E: E.reciprocal(out=out.ap, in_=in_.ap))


class WS:
    def __init__(self, fw, nslots, elems):
        self.fw = fw
        self.n = nslots
        self.slots = [fw.sbuf(f"ring{i}", [128, elems], BF16, fw.dsem(f"dring{i}")) for i in range(nslots)]
        self.plan = []
        self.pos = 0
        self.issued = 0

    def reset_for_real(self):
        self.pos = 0
        self.issued = 0

    def get(self, dram_v, n):
        fw = self.fw
        if fw.dry:
            self.plan.append((dram_v, n))
            self.pos += 1
            return self.slots[(self.pos - 1) % self.n].v()
        i = self.pos
        self.pos += 1
        lim = min(len(self.plan), i + self.n)
        while self.issued < lim:
            j = self.issued
            dv, nn = self.plan[j]
            fw.dma("pool", self.slots[j % self.n][:, 0:nn], dv)
            self.issued += 1
        return self.slots[i % self.n].v()


def build_program(debug=False):
    nc = bass.Bass("TRN2", target_bir_lowering=False)
    fw = FW(nc)

    def din(name, shape, dt=F32):
        return V(None, nc.dram_tensor(name, list(shape), dt, kind="ExternalInput").ap())

    xT = din("xT", [D, NTOK])
    memT = din("memT", [D, 2 * MEM])
    vecs_d = din("vecs", [128, NV])
    ident_d = din("ident", [128, 128])
    s5sl_d = din("s5sl", [128, 3, 16])
    bsl_d = din("bsl", [128, 2, 16, 32])
    csl_d = din("csl", [128, 2, 16, 32])
    wab_d = din("wab", [128, 2, 4, 128])
    wb_in = din("wb_in", [3, 128, 4096])
    wb_mix = din("wb_mix", [2, 128, 4096])
    wb_q = din("wb_q", [2, 128, 4096])
    wb_k = din("wb_k", [2, 128, 4096])
    wb_o = din("wb_o", [2, 128, 4096])
    wb_ff1 = din("wb_ff1", [8, 128, 4096])
    wb_ff2 = din("wb_ff2", [8, 128, 4096])
    wv_d = din("w_v", [D, D])
    wglu_d = din("w_glu", [512, 512])
    outT = V(None, nc.dram_tensor("outT", [D, NTOK], F32, kind="ExternalOutput").ap())
    dbg = {}
    if debug:
        for nm, shp, dt_ in [("d_ys5", [128, 4, SEQ], BF16), ("d_ylru", [128, 4, SEQ], BF16), ("d_u", [128, 4, SEQ], BF16),
                        ("d_h1", [128, 8, NT], F32), ("d_h2", [128, 8, NT], F32), ("d_hprev", [128, 16, 2, 256], BF16),
                        ("d_yg", [128, 4, SEQ], BF16)]:
            dbg[nm] = V(None, nc.dram_tensor(nm, shp, dt_, kind="ExternalOutput").ap())

    xT_v = xT.re("(kt p) n -> p kt n", p=128)
    memT_v = memT.re("(kt p) n -> p kt n", p=128)
    outT_v = outT.re("(kt p) n -> p kt n", p=128)

    def dscratch(name, shape, dt):
        return Buf(name, nc.dram_tensor(name, list(shape), dt).ap())

    sc_ktl = dscratch("sc_ktl", [128, 4 * 8 * 128], BF16)
    sc_bct = dscratch("sc_bct", [128, 4 * 8 * 2 * 128], BF16)
    sc_fb = dscratch("sc_fb", [128, 2 * 8 * 16 * 32], BF16)
    sc_cos = dscratch("sc_cos", [128, 16 * 256], F32)
    sc_sin = dscratch("sc_sin", [128, 16 * 256], F32)
    sc_rho = dscratch("sc_rho", [128, 16], F32)

    d_init = fw.dsem("d_init")
    d_ph = fw.dsem("d_ph")
    d_ph2 = fw.dsem("d_ph2")
    d_initp = fw.dsem("d_initp")
    d_php = fw.dsem("d_php")
    d_x = fw.dsem("d_x")
    d_x2 = fw.dsem("d_x2")
    d_out = fw.dsem("d_out")
    d_sv = fw.dsem("d_sv")
    d_dbg = fw.dsem("d_dbg")
    for b_ in (sc_ktl, sc_bct, sc_fb, sc_cos, sc_sin, sc_rho):
        b_.dsem = d_sv

    fw.banks = [fw.psum(f"bank{i}", [128, 512], F32) for i in range(8)]
    vecs = fw.sbuf("vecs", [128, NV], F32, d_init)
    ones_ln = fw.sbuf("ones_ln", [128, 128], BF16)
    ones_rms = fw.sbuf("ones_rms", [128, 128], BF16)
    ones_1 = fw.sbuf("ones_1", [128, 128], BF16)
    zeros_bf = fw.sbuf("zeros_bf", [128, 128], BF16)
    cneg = fw.sbuf("cneg", [128, 4], F32)
    cneg2 = fw.sbuf("cneg2", [128, 4], F32)
    KTs = fw.sbuf("KTs", [128, 8, 2 * MEM], BF16)
    Vs = fw.sbuf("Vs", [128, 4, D], BF16)
    YS5 = fw.sbuf("YS5", [128, 4, SEQ], BF16)
    YLRU = fw.sbuf("YLRU", [128, 4, SEQ], BF16)
    ws = WS(fw, 4, 4096)

    def vc(name, i):
        c = VC[name] + i
        return vecs[:, c:c + 1]

    def layernorm_g(r, gname, bname, N, scr_xb, scr_sq, tm, out_bf=None, keep_f32=True, dst_f32=None):
        if dst_f32 is None:
            dst_f32 = r
        fw.cp("dve", scr_xb, r)
        fw.act(scr_sq, r, AF.Square)
        yield
        pm = fw.bank()
        pq = fw.bank()
        for kt in range(8):
            fw.mm(pm[:, 0:N], ones_ln.v(), scr_xb[:, kt, :], kt == 0, kt == 7)
        for kt in range(8):
            fw.mm(pq[:, 0:N], ones_ln.v(), scr_sq[:, kt, :], kt == 0, kt == 7)
        mean, t = tm
        fw.act(t, pm[:, 0:N], AF.Square)
        fw.act(mean, pm[:, 0:N], AF.Copy)
        fw.stt("dve", t, pq[:, 0:N], LN_EPS, t, ALU.add, ALU.subtract)
        fw.act(t, t, AF.Ln)
        fw.act(t, t, AF.Exp, scale=-0.5)
        fw.tt("dve", r, r, mean.un(1).bc([128, 8, N]), ALU.subtract)
        fw.tt("dve", r, r, t.un(1).bc([128, 8, N]), ALU.mult)
        yield
        for kt in range(8):
            if keep_f32:
                fw.act(dst_f32[:, kt, :], r[:, kt, :], AF.Identity, bias=vc(bname, kt), scale=vc(gname, kt))
            else:
                fw.act(out_bf[:, kt, :], r[:, kt, :], AF.Identity, bias=vc(bname, kt), scale=vc(gname, kt))
        if keep_f32 and out_bf is not None:
            fw.cp("dve", out_bf, dst_f32)
        yield

    def layernorm(*a, **kw):
        for _ in layernorm_g(*a, **kw):
            pass

    GC1 = 2.0 * math.sqrt(2.0 / math.pi)

    def gelu_tanh(dst, ps, tmp):
        fw.act(tmp, ps, AF.Square)
        fw.stt("dve", tmp, tmp, 1.0 / 0.044715, ps, ALU.add, ALU.mult)
        fw.act(tmp, tmp, AF.Sigmoid, scale=GC1 * 0.044715)
        fw.tt("dve", dst, tmp, ps, ALU.mult)

    def rmsnorm_finish(y32, gname, out_bf, sq4, rs):
        fw.act(sq4, y32, AF.Square)
        ps = fw.bank()
        for t in range(4):
            fw.mm(ps.v(), ones_rms.v(), sq4[:, t, :], t == 0, t == 3)
        fw.act(rs, ps.v(), AF.Ln, bias=RMS_EPS)
        fw.act(rs, rs, AF.Exp, scale=-0.5)
        for t in range(4):
            fw.stt("dve", out_bf[:, t, :], y32[:, t, :], vc(gname, t), rs, ALU.mult, ALU.mult)

    def linear_g(wb, nslots, G, KT, rhs_fn, epi, N=NT):
        for s_ in range(nslots):
            slot = ws.get(wb[s_], G * KT * 128)
            sv = slot.re("p (g kt c) -> p g kt c", g=G, kt=KT)
            for g in range(G):
                m = s_ * G + g
                ps = fw.bank()
                for kt in range(KT):
                    fw.mm(ps[:, 0:N], sv[:, g, kt, :], rhs_fn(kt), kt == 0, kt == KT - 1)
                epi(m, ps[:, 0:N])
            yield

    def linear(*a, **kw):
        for _ in linear_g(*a, **kw):
            pass

    def interleave(gy, gx, ny, nx):
        xi = 0
        for yi in range(ny):
            next(gy, None)
            tgt = ((yi + 1) * nx) // ny if gx is not None else 0
            while xi < tgt:
                next(gx, None)
                xi += 1
        for _ in gy:
            pass
        if gx is not None:
            for _ in gx:
                pass

    def emit():
        fw.dma("sp", vecs.v(), vecs_d)
        fw.memset("dve", ones_ln.v(), 1.0 / 1024.0)
        fw.memset("dve", ones_rms.v(), 1.0 / 512.0)
        fw.memset("dve", ones_1.v(), 1.0)
        fw.memset("dve", zeros_bf.v(), 0.0)
        lamv = vecs[:, VC["lam"]:VC["lam"] + 4]
        fw.act(cneg.v(), lamv, AF.Exp, scale=-1.0)
        fw.ts("dve", cneg.v(), cneg.v(), 1.0, ALU.add)
        fw.act(cneg.v(), cneg.v(), AF.Ln)
        fw.ts("dve", cneg.v(), cneg.v(), -8.0, ALU.mult)
        fw.ts("dve", cneg2.v(), cneg.v(), 2.0, ALU.mult)

        with fw.scope():
            ident = fw.sbuf("ident", [128, 128], F32, d_ph)
            sl = fw.sbuf("sl", [128, 3, 16], F32, d_ph)
            bsl = fw.sbuf("bsl", [128, 2, 16, 32], F32, d_ph)
            csl = fw.sbuf("csl", [128, 2, 16, 32], F32, d_ph)
            fw.dma("sp", ident.v(), ident_d)
            fw.dma("sp", sl.v(), s5sl_d)
            fw.dma("sp", bsl.v(), bsl_d)
            fw.dma("sp", csl.v(), csl_d)
            SM = fw.sbuf("SM", [128, 48, 16], F32)
            _k = [0]

            def sm():
                _k[0] += 1
                return SM[:, _k[0] - 1, :]

            a_re, a_im, logdt = sl[:, 0, :], sl[:, 1, :], sl[:, 2, :]
            dt_, are, ang, mag = sm(), sm(), sm(), sm()
            fw.act(dt_, logdt, AF.Exp)
            fw.tt("dve", are, dt_, a_re, ALU.mult)
            fw.tt("dve", ang, dt_, a_im, ALU.mult)
            fw.act(mag, are, AF.Exp)
            tmp, tmp2 = sm(), sm()

            TWO_PI_HI = float(np.float32(2.0 * PI))
            TWO_PI_LO = float(np.float32(2.0 * PI - TWO_PI_HI))
            HALF_PI_HI = float(np.float32(PI / 2))
            HALF_PI_LO = float(np.float32(PI / 2 - HALF_PI_HI))

            def sin_reduced(dst, src):
                red, kf = sm(), sm()
                fw.ts("dve", kf, src, PI, ALU.is_gt)
                for k in range(1, 6):
                    fw.ts("dve", tmp, src, (2 * k + 1) * PI, ALU.is_gt)
                    fw.tt("dve", kf, kf, tmp, ALU.add)
                fw.stt("dve", red, kf, -TWO_PI_HI, src, ALU.mult, ALU.add)
                fw.stt("dve", red, kf, -TWO_PI_LO, red, ALU.mult, ALU.add)
                fw.act(dst, red, AF.Sin)

            sn, cs, angc = sm(), sm(), sm()
            sin_reduced(sn, ang)
            fw.ts("dve", angc, ang, HALF_PI_HI, ALU.add, HALF_PI_LO, ALU.add)
            sin_reduced(cs, angc)
            lr, li = sm(), sm()
            fw.tt("dve", lr, mag, cs, ALU.mult)
            fw.tt("dve", li, mag, sn, ALU.mult)
            lm1, den, qre, qim = sm(), sm(), sm(), sm()
            fw.ts("dve", lm1, lr, -1.0, ALU.add)
            fw.tt("dve", den, a_re, a_re, ALU.mult)
            fw.tt("dve", tmp, a_im, a_im, ALU.mult)
            fw.tt("dve", den, den, tmp, ALU.add)
            fw.recip(den, den)
            fw.tt("dve", qre, lm1, a_re, ALU.mult)
            fw.tt("dve", tmp, li, a_im, ALU.mult)
            fw.tt("dve", qre, qre, tmp, ALU.add)
            fw.tt("dve", qre, qre, den, ALU.mult)
            fw.tt("dve", qim, li, a_re, ALU.mult)
            fw.tt("dve", tmp, lm1, a_im, ALU.mult)
            fw.tt("dve", qim, qim, tmp, ALU.subtract)
            fw.tt("dve", qim, qim, den, ALU.mult)

            def b3(v):
                return v.un(2).bc([128, 16, 32])

            with fw.scope():
                Gb = [fw.sbuf(f"G{i}", [128, 2, 16, 32], F32) for i in range(2)]
                Fb = [fw.sbuf(f"F{i}", [128, 2, 16, 32], F32) for i in range(2)]
                F0 = fw.sbuf("F0", [128, 2, 16, 32], F32)
                T1 = fw.sbuf("T1", [128, 16, 32], F32)
                T2 = fw.sbuf("T2", [128, 16, 32], F32)
                KTl = fw.sbuf("KTl", [128, 4, 8, 128], BF16)
                BcT = fw.sbuf("BcT", [128, 4, 8, 2, 128], BF16)
                FB = fw.sbuf("FB", [128, 2, 8, 16, 32], BF16)
                fw.memset("dve", KTl.v(), 0.0)

                T3 = fw.sbuf("T3", [128, 16, 32], F32)
                T4 = fw.sbuf("T4", [128, 16, 32], F32)

                def cmul(dst, src, xr, xi, conj=False, e="dve"):
                    Ta, Tb = (T1, T2) if e == "dve" else (T3, T4)
                    sre, sim = src[:, 0], src[:, 1]
                    fw.tt(e, Ta.v(), sre, b3(xr), ALU.mult)
                    fw.tt(e, Tb.v(), sim, b3(xi), ALU.mult)
                    fw.tt(e, dst[:, 0], Ta.v(), Tb.v(), ALU.add if conj else ALU.subtract)
                    fw.tt(e, Ta.v(), sre, b3(xi), ALU.mult)
                    fw.tt(e, Tb.v(), sim, b3(xr), ALU.mult)
                    fw.tt(e, dst[:, 1], Tb.v(), Ta.v(), ALU.subtract if conj else ALU.add)

                lrp = fw.sbuf("lrp", [128, 2, 16], F32)
                fw.cp("dve", lrp[:, 0, :], lr)
                fw.cp("dve", lrp[:, 1, :], li)
                fw.cp("dve", F0[:, 0], csl[:, 0])
                fw.ts("dve", F0[:, 1], csl[:, 1], -1.0, ALU.mult)
                cmul(Gb[0].v(), bsl.v(), qre, qim)
                dvec = vecs[:, VC["s5_d"]:VC["s5_d"] + 4]
                for n in range(8):
                    G = Gb[n % 2]
                    for j in range(4):
                        kb = fw.bank()
                        for k in range(4):
                            q = 4 * j + k
                            for ri in range(2):
                                fw.mm(kb[32 * k:32 * k + 32, 32 * k:32 * k + 32], G[:, ri, q, :], F0[:, ri, q, :],
                                      ri == 0, ri == 1, tile_position=(0, 32 * k))
                        for k in range(4):
                            sl_ = slice(32 * k, 32 * k + 32)
                            if n == 0:
                                fw.stt("dve", KTl[sl_, j, 0, sl_], ident[sl_, sl_], dvec[sl_, j:j + 1], kb[sl_, sl_],
                                       ALU.mult, ALU.add)
                            else:
                                fw.cp("act", KTl[sl_, j, n, sl_], kb[sl_, sl_])
                    for j in range(4):
                        tb = fw.bank()
                        for ri in range(2):
                            fw.transpose(tb[:, ri * 128:(ri + 1) * 128], G[:, ri, 4 * j:4 * j + 4, :].re("p a b -> p (a b)"),
                                         ident.v())
                        fw.cp("act", BcT[:, j, n, :, :].re("p a b -> p (a b)"), tb[:, 0:256])
                    if n < 7:
                        cmul(Gb[(n + 1) % 2].v(), G.v(), lr, li)
                prev = F0
                for n in range(1, 9):
                    cur = Fb[n % 2]
                    cmul(cur.v(), prev.v(), lrp[:, 0, :], lrp[:, 1, :], conj=True, e="pool")
                    for ri in range(2):
                        fw.cp("pool", FB[:, ri, n - 1], cur[:, ri])
                    prev = cur
                pr, pi_, pr2, pi2 = sm(), sm(), sm(), sm()
                fw.cp("dve", pr, lr)
                fw.cp("dve", pi_, li)
                for _ in range(3):
                    fw.tt("dve", pr2, pr, pr, ALU.mult)
                    fw.tt("dve", tmp, pi_, pi_, ALU.mult)
                    fw.tt("dve", pr2, pr2, tmp, ALU.subtract)
                    fw.tt("dve", pi2, pr, pi_, ALU.mult)
                    fw.ts("dve", pi2, pi2, 2.0, ALU.mult)
                    fw.cp("dve", pr, pr2)
                    fw.cp("dve", pi_, pi2)
                rho8 = sm()
                fw.tt("dve", rho8, pr, pr, ALU.mult)
                fw.tt("dve", tmp, pi_, pi_, ALU.mult)
                fw.tt("dve", rho8, rho8, tmp, ALU.add)
                fw.act(rho8, rho8, AF.Sqrt)
                ure, uim = sm(), sm()
                fw.recip(tmp, rho8)
                fw.tt("dve", ure, pr, tmp, ALU.mult)
                fw.tt("dve", uim, pi_, tmp, ALU.mult)
                fw.dma("sp", sc_ktl.v(), KTl.v().re("p a b c -> p (a b c)"))
                fw.dma("sp", sc_bct.v(), BcT.v().re("p a b c d -> p (a b c d)"))
                fw.dma("sp", sc_fb.v(), FB.v().re("p a b c d -> p (a b c d)"))
                fw.dma("sp", sc_rho.v(), rho8)
            COS = fw.sbuf("COS", [128, 16, 256], F32)
            SIN = fw.sbuf("SIN", [128, 16, 256], F32)
            TA = fw.sbuf("TA", [128, 16, 128], F32)
            TB = fw.sbuf("TB", [128, 16, 128], F32)
            fw.memset("dve", COS[:, :, 0:1], 1.0)
            fw.memset("dve", SIN[:, :, 0:1], 0.0)
            for kk in range(8):
                L = 1 << kk

                def bL(v):
                    return v.un(2).bc([128, 16, L])

                c0, s0 = COS[:, :, 0:L], SIN[:, :, 0:L]
                fw.tt("dve", TA[:, :, 0:L], c0, bL(ure), ALU.mult)
                fw.tt("dve", TB[:, :, 0:L], s0, bL(uim), ALU.mult)
                fw.tt("dve", COS[:, :, L:2 * L], TA[:, :, 0:L], TB[:, :, 0:L], ALU.subtract)
                fw.tt("dve", TA[:, :, 0:L], s0, bL(ure), ALU.mult)
                fw.tt("dve", TB[:, :, 0:L], c0, bL(uim), ALU.mult)
                fw.tt("dve", SIN[:, :, L:2 * L], TA[:, :, 0:L], TB[:, :, 0:L], ALU.add)
                if kk < 7:
                    fw.tt("dve", pr2, ure, ure, ALU.mult)
                    fw.tt("dve", tmp, uim, uim, ALU.mult)
                    fw.tt("dve", pr2, pr2, tmp, ALU.subtract)
                    fw.tt("dve", pi2, ure, uim, ALU.mult)
                    fw.ts("dve", pi2, pi2, 2.0, ALU.mult)
                    fw.cp("dve", ure, pr2)
                    fw.cp("dve", uim, pi2)
            fw.dma("sp", sc_cos.v(), COS.v().re("p a b -> p (a b)"))
            fw.dma("sp", sc_sin.v(), SIN.v().re("p a b -> p (a b)"))

        with fw.scope():
            memt = fw.sbuf("memt", [128, 8, 2 * MEM], F32, d_ph)
            scr1 = fw.sbuf("scr1", [128, 8, 2 * MEM], BF16)
            scr2 = fw.sbuf("scr2", [128, 8, 2 * MEM], BF16)
            memb = fw.sbuf("memb", [128, 8, 2 * MEM], BF16)
            wv = fw.sbuf("wv", [128, 8, D], BF16, d_php)
            tmA = fw.sbuf("tmA", [128, 2 * MEM], F32)
            tmB = fw.sbuf("tmB", [128, 2 * MEM], F32)
            fw.dma("sp", memt.v(), memT_v)
            fw.dma("pool", wv.v(), wv_d.re("(kt p) n -> p kt n", p=128))
            layernorm(memt.v(), "mem_g", "mem_b", 2 * MEM, scr1.v(), scr2.v(), [tmA.v(), tmB.v()],
                      out_bf=memb.v(), keep_f32=False)
            linear(wb_k, 2, 4, 8, lambda kt: memb[:, kt, :], lambda m, ps: fw.cp("act", KTs[:, m, :], ps), N=2 * MEM)
            for sm_ in range(4):
                for half in range(2):
                    ps = fw.bank()
                    for kt in range(8):
                        fw.mm(ps.v(), memb[:, kt, sm_ * 128:(sm_ + 1) * 128], wv[:, kt, half * 512:(half + 1) * 512],
                              kt == 0, kt == 7)
                    fw.cp("act", Vs[:, sm_, half * 512:(half + 1) * 512], ps.v())

        for s in range(2):
            with fw.scope():
                U = fw.sbuf("U", [128, 4, SEQ], BF16)
                Uj = [fw.sub(U, f"U{j}") for j in range(4)]
                with fw.scope():
                    XLp = fw.sbuf("XLp", [128, 4, SEQ + 4], BF16)
                    GG = fw.sbuf("GG", [128, 4, SEQ], BF16)
                    fw.memset("dve", XLp[:, :, 0:4], 0.0)
                    with fw.scope():
                        xts = [fw.sbuf("xt0", [128, 8, NT], F32, d_x), fw.sbuf("xt1", [128, 8, NT], F32, d_x2)]
                        hbs = [fw.sbuf("hb0", [128, 8, NT], BF16), fw.sbuf("hb1", [128, 8, NT], BF16)]
                        scr1 = fw.sbuf("scr1", [128, 8, NT], BF16)
                        scr2 = fw.sbuf("scr2", [128, 8, NT], BF16)
                        tmA = fw.sbuf("tmA", [128, NT], F32)
                        tmB = fw.sbuf("tmB", [128, NT], F32)
                        gtms = [fw.sbuf(f"gtm{i}", [128, NT], F32) for i in range(4)]

                        def lnA(i):
                            tok0 = s * SEQ + i * NT
                            fw.dma("sp", xts[i % 2].v(), xT_v[:, :, tok0:tok0 + NT])
                            yield from layernorm_g(xts[i % 2].v(), "ln_in_g", "ln_in_b", NT, scr1.v(), scr2.v(),
                                                   [tmA.v(), tmB.v()], out_bf=hbs[i % 2].v(), keep_f32=False)

                        def linA(i):
                            hb = hbs[i % 2]

                            def epiA(m, ps):
                                if m < 4:
                                    dst = V(Uj[m], U.t[:, m, :]).re("p (t c) -> p t c", t=8)[:, :, i * 64:(i + 1) * 64]
                                    fw.cp("act", dst, ps.re("p (c t) -> p t c", t=8))
                                elif m < 8:
                                    fw.cp("act", XLp[:, m - 4, 4 + i * NT:4 + (i + 1) * NT], ps)
                                else:
                                    gelu_tanh(GG[:, m - 8, i * NT:(i + 1) * NT], ps, gtms[m - 8].v())

                            yield from linear_g(wb_in, 3, 4, 8, lambda kt: hb[:, kt, :], epiA)

                        for _ in lnA(0):
                            pass
                        for i in range(4):
                            gl = linA(i)
                            gn = lnA(i + 1) if i < 3 else iter(())
                            for _ in range(3):
                                next(gn, None)
                                next(gl, None)
                            for _ in gn:
                                pass
                            for _ in gl:
                                pass
                    with fw.scope():
                        wab = fw.sbuf("wab", [128, 2, 4, 128], BF16, d_php)
                        fw.dma("pool", wab.v(), wab_d)
                        xc = fw.sbuf("xc", [128, 4, NT], F32)
                        xcb = fw.sbuf("xcb", [128, 4, NT], BF16)
                        hs = [fw.sbuf(f"hs{i}", [128, 4, NT], F32) for i in range(2)]
                        y32 = fw.sbuf("y32", [128, 4, NT], F32)
                        sq4 = fw.sbuf("sq4", [128, 4, NT], BF16)
                        rs = fw.sbuf("rs", [128, NT], F32)
                        rrs = [fw.sbuf(f"rr{t}", [128, NT], F32) for t in range(4)]
                        igs = [fw.sbuf(f"ig{t}", [128, NT], F32) for t in range(4)]
                        aas = [fw.sbuf(f"aa{t}", [128, NT], F32) for t in range(4)]
                        mus = [fw.sbuf(f"mu{t}", [128, NT], F32) for t in range(4)]
                        xcs = [fw.sub(xc, f"xc{t}") for t in range(4)]
                        xcbs = [fw.sub(xcb, f"xcb{t}") for t in range(4)]
                        y32s = [fw.sub(y32, f"y32_{t}") for t in range(4)]
                        hss = [[fw.sub(hs[k], f"hs{k}_{t}") for t in range(4)] for k in range(2)]
                        cw = VC["conv_w"]

                        def chain(i, t):
                            rr, ig, aa, mu = rrs[t], igs[t], aas[t], mus[t]
                            xcv = V(xcs[t], xc.t[:, t, :])
                            xcbv = V(xcbs[t], xcb.t[:, t, :])
                            xin = XLp[:, t, i * NT:i * NT + NT + 4]
                            fw.act(xcv, xin[:, 4:4 + NT], AF.Identity, bias=vc("conv_b", t),
                                   scale=vecs[:, cw + 4 * t + 3:cw + 4 * t + 4])
                            yield
                            for k in range(3):
                                fw.stt("dve", xcv, xin[:, 1 + k:1 + k + NT],
                                       vecs[:, cw + 4 * t + k:cw + 4 * t + k + 1], xcv, ALU.mult, ALU.add)
                                yield
                            fw.cp("dve", xcbv, xcv)
                            yield
                            psr = fw.bank()
                            fw.mm(psr.v(), wab[:, 0, t, :], xcbv, True, True)
                            psi = fw.bank()
                            fw.mm(psi.v(), wab[:, 1, t, :], xcbv, True, True)
                            fw.act(rr.v(), psr.v(), AF.Sigmoid, bias=vc("b_a", t))
                            fw.act(ig.v(), psi.v(), AF.Sigmoid, bias=vc("b_x", t))
                            yield
                            fw.act(aa.v(), rr.v(), AF.Exp, scale=cneg[:, t:t + 1])
                            fw.tt("dve", ig.v(), ig.v(), xcv, ALU.mult)
                            yield
                            fw.act(mu.v(), rr.v(), AF.Exp, scale=cneg2[:, t:t + 1])
                            yield
                            fw.ts("dve", mu.v(), mu.v(), -1.0, ALU.mult, 1.0, ALU.add)
                            yield
                            fw.act(mu.v(), mu.v(), AF.Sqrt)
                            yield
                            fw.tt("dve", ig.v(), ig.v(), mu.v(), ALU.mult)
                            yield
                            hcv = V(hss[i % 2][t], hs[i % 2].t[:, t, :])
                            init = 0.0 if i == 0 else V(hss[(i + 1) % 2][t], hs[(i + 1) % 2].t[:, t, NT - 1:NT])
                            fw.scan(hcv, aa.v(), ig.v(), init)
                            yield
                            fw.tt("dve", V(y32s[t], y32.t[:, t, :]), hcv, GG[:, t, i * NT:(i + 1) * NT], ALU.mult)
                            yield

                        def rms_lru(i):
                            y4 = [V(y32s[t], y32.t[:, t, :]) for t in range(4)]
                            for t in range(4):
                                fw.act(sq4[:, t, :], y4[t], AF.Square)
                            ps = fw.bank()
                            for t in range(4):
                                fw.mm(ps.v(), ones_rms.v(), sq4[:, t, :], t == 0, t == 3)
                            fw.act(rs.v(), ps.v(), AF.Ln, bias=RMS_EPS)
                            fw.act(rs.v(), rs.v(), AF.Exp, scale=-0.5)
                            for t in range(4):
                                fw.stt("dve", YLRU[:, t, i * NT:(i + 1) * NT], y4[t], vc("g_lru", t), rs.v(),
                                       ALU.mult, ALU.mult)

                        for i in range(4):
                            gens = [chain(i, t) for t in range(4)]
                            alive = True
                            while alive:
                                alive = False
                                for g in gens:
                                    if next(g, "done") != "done":
                                        alive = True
                            rms_lru(i)
                Hprev = fw.sbuf("Hprev", [128, 16, 2, 256], BF16)
                KTl = fw.sbuf("KTl", [128, 4, 8, 128], BF16, d_ph2)
                FB = fw.sbuf("FB", [128, 2, 8, 16, 32], BF16, d_ph2)
                wglu = fw.sbuf("wglu", [128, 4, 512], BF16, d_php)
                with fw.scope():
                    BcT = fw.sbuf("BcT", [128, 4, 8, 2, 128], BF16, d_ph)
                    COS = fw.sbuf("COS", [128, 16, 256], F32, d_ph)
                    SIN = fw.sbuf("SIN", [128, 16, 256], F32, d_ph)
                    RHO = fw.sbuf("RHO", [128, 16], F32, d_ph)
                    fw.dma("sp", BcT.v().re("p a b c d -> p (a b c d)"), sc_bct.v())
                    fw.dma("sp", COS.v().re("p a b -> p (a b)"), sc_cos.v())
                    fw.dma("sp", SIN.v().re("p a b -> p (a b)"), sc_sin.v())
                    fw.dma("sp", RHO.v(), sc_rho.v())
                    fw.dma("sp", KTl.v().re("p a b c -> p (a b c)"), sc_ktl.v())
                    fw.dma("sp", FB.v().re("p a b c d -> p (a b c d)"), sc_fb.v())
                    fw.dma("pool", wglu.v(), wglu_d.re("(kt p) n -> p kt n", p=128))
                    A_ = fw.sbuf("A_", [128, 2, 256], F32)
                    B_ = fw.sbuf("B_", [128, 2, 256], F32)
                    R_ = fw.sbuf("R_", [128, 2, 256], F32)
                    Qs = [fw.sbuf(f"Q_{i}", [128, 2, 256], F32) for i in range(2)]
                    A2s = [fw.sbuf(f"A2_{i}", [128, 2, 256], F32) for i in range(2)]
                    B2s = [fw.sbuf(f"B2_{i}", [128, 2, 256], F32) for i in range(2)]
                    fw.memset("dve", Hprev[:, :, :, 0:1], 0.0)
                    for q in range(16):
                        j, k = divmod(q, 4)
                        Q_ = Qs[q % 2]
                        ps = fw.bank()
                        sl_ = slice(32 * k, 32 * k + 32)
                        for ri in range(2):
                            for s_ in range(8):
                                fw.mm(ps[:, ri * 256:(ri + 1) * 256], BcT[sl_, j, 7 - s_, ri, :],
                                      V(Uj[j], U.t[sl_, j, s_ * 256:(s_ + 1) * 256]), s_ == 0, s_ == 7,
                                      tile_position=(32 * k, 0))
                        S = ps.v().re("p (r c) -> p r c", r=2)
                        cb_ = COS[:, q, :].un(1).bc([128, 2, 256])
                        sb_ = SIN[:, q, :].un(1).bc([128, 2, 256])
                        fw.tt("dve", A_.v(), S, cb_, ALU.mult)
                        fw.tt("dve", B_.v(), S, sb_, ALU.mult)
                        fw.tt("dve", R_[:, 0, :], A_[:, 0, :], B_[:, 1, :], ALU.add)
                        fw.tt("dve", R_[:, 1, :], A_[:, 1, :], B_[:, 0, :], ALU.subtract)
                        rb = RHO[:, q:q + 1].bc([128, 256])
                        fw.scan(Q_[:, 0, :], rb, R_[:, 0, :], 0.0)
                        fw.scan(Q_[:, 1, :], rb, R_[:, 1, :], 0.0)
                        A2, B2 = A2s[q % 2], B2s[q % 2]
                        fw.tt("pool", A2.v(), Q_.v(), cb_, ALU.mult)
                        fw.tt("pool", B2.v(), Q_.v(), sb_, ALU.mult)
                        fw.tt("pool", Hprev[:, q, 0, 1:256], A2[:, 0, 0:255], B2[:, 1, 0:255], ALU.subtract)
                        fw.tt("pool", Hprev[:, q, 1, 1:256], A2[:, 1, 0:255], B2[:, 0, 0:255], ALU.add)
                    if debug and s == 0:
                        fw.dma("sp", dbg["d_hprev"], Hprev.v(), dsem=d_dbg)
                with fw.scope():
                    gtm2s = [fw.sbuf(f"gtm2_{i}", [128, NT], F32) for i in range(4)]
                    if debug and s == 0:
                        for j in range(4):
                            fw.dma("sp", dbg["d_u"][:, j, :], V(Uj[j], U.t[:, j, :]), dsem=d_dbg)
                    for j in range(4):
                        bk = [fw.bank() for _ in range(4)]
                        Ujv = V(Uj[j], U.t[:, j, :]).re("p (t c) -> p t c", t=8)
                        for b in range(4):
                            for d_ in range(0, 2 * b + 2):
                                if 2 * b >= d_:
                                    fw.mm(bk[b].v(), KTl[:, j, d_, :],
                                          Ujv[:, 2 * b - d_:2 * b - d_ + 2, :].re("p t c -> p (t c)"), d_ == 0, False)
                                else:
                                    fw.mm(bk[b][:, 256:512], KTl[:, j, d_, :], Ujv[:, 0, :], False, False)
                            for k in range(4):
                                q = 4 * j + k
                                sl_ = slice(32 * k, 32 * k + 32)
                                for tl in range(2):
                                    tau = 2 * b + tl
                                    for ri in range(2):
                                        last = False
                                        fw.mm(bk[b][sl_, tl * 256:(tl + 1) * 256], FB[:, ri, tau, q, :],
                                              Hprev[:, q, ri, :], False, last, tile_position=(0, 32 * k))
                        for b in range(4):
                            fw.mm(bk[b].v(), zeros_bf.v(), Ujv[:, 0:2, :].re("p t c -> p (t c)"), False, True)
                        Ynat = V(Uj[j], U.t[:, j, :]).re("p (c t) -> p t c", t=8)
                        for b in range(4):
                            gelu_tanh(Ynat[:, 2 * b:2 * b + 2, :], bk[b].v().re("p (t c) -> p t c", t=2),
                                      gtm2s[b].v().re("p (t c) -> p t c", t=2))
                    if debug and s == 0:
                        for j in range(4):
                            fw.dma("sp", dbg["d_yg"][:, j, :], V(Uj[j], U.t[:, j, :]), dsem=d_dbg)
                    sgs = [fw.sbuf(f"sg{i}", [128, NT], F32) for i in range(2)]
                    y32s_ = [fw.sbuf(f"y32b{i}", [128, 4, NT], F32) for i in range(2)]
                    sq4s = [fw.sbuf(f"sq4b{i}", [128, 4, NT], BF16) for i in range(2)]
                    rss = [fw.sbuf(f"rsb{i}", [128, NT], F32) for i in range(2)]
                    for i in range(4):
                        tsl = slice(i * NT, (i + 1) * NT)
                        y32 = y32s_[i % 2]
                        for m in range(4):
                            sg = sgs[m % 2]
                            ps = fw.bank()
                            for kt in range(4):
                                fw.mm(ps.v(), wglu[:, kt, m * 128:(m + 1) * 128], V(Uj[kt], U.t[:, kt, tsl]),
                                      kt == 0, kt == 3)
                            fw.act(sg.v(), ps.v(), AF.Sigmoid, bias=vc("b_glu", m))
                            fw.tt("dve", y32[:, m, :], V(Uj[m], U.t[:, m, tsl]), sg.v(), ALU.mult)
                        rmsnorm_finish(y32.v(), "g_s5", YS5[:, :, tsl], sq4s[i % 2].v(), rss[i % 2].v())
                if debug and s == 0:
                    fw.dma("sp", dbg["d_ys5"], YS5.v(), dsem=d_dbg)
                    fw.dma("sp", dbg["d_ylru"], YLRU.v(), dsem=d_dbg)
            with fw.scope():
                AB = [fw.sbuf("fa0", [128, 8, NT], F32), fw.sbuf("fa1", [128, 8, NT], F32)]
                fb_ = fw.sbuf("fb", [128, 8, NT], F32, d_x)
                ba = fw.sbuf("ba", [128, 8, NT], BF16)
                bb = fw.sbuf("bb", [128, 8, NT], BF16)
                h2b = fw.sbuf("h2b", [128, 8, NT], BF16)
                ffb = fw.sbuf("ffb", [128, 32, NT], BF16)
                tmx = [fw.sbuf("tmxA", [128, NT], F32).v(), fw.sbuf("tmxB", [128, NT], F32).v()]
                tmy = [fw.sbuf("tmyA", [128, NT], F32).v(), fw.sbuf("tmyB", [128, NT], F32).v()]
                tmz = [fw.sbuf("tmzA", [128, NT], F32).v(), fw.sbuf("tmzB", [128, NT], F32).v()]
                Es = [fw.sbuf(f"E{i}", [128, 2, NT], BF16) for i in range(2)]
                rdens = [fw.sbuf(f"rden{i}", [128, NT], F32) for i in range(2)]
                rl = fw.sbuf("rl", [128, NT], F32)

                def X_gen(i):
                    fa = AB[i % 2]
                    tok0 = s * SEQ + i * NT
                    tsl = slice(i * NT, (i + 1) * NT)
                    fw.dma("sp", fb_.v(), xT_v[:, :, tok0:tok0 + NT])
                    layernorm(fb_.v(), "ln_in_g", "ln_in_b", NT, ba.v(), bb.v(), tmz)
                    yield
                    yield from linear_g(wb_mix, 2, 4, 8,
                                        lambda kt: (YS5[:, kt, tsl] if kt < 4 else YLRU[:, kt - 4, tsl]),
                                        lambda m, ps: fw.stt("dve", fb_[:, m, :], fb_[:, m, :], ALPHA, ps, ALU.mult, ALU.add))
                    layernorm(fb_.v(), "ln1_g", "ln1_b", NT, ba.v(), bb.v(), tmx, out_bf=ba.v())
                    if debug and s == 0 and i == 0:
                        fw.dma("sp", dbg["d_h1"], fb_.v(), dsem=d_dbg)
                    yield
                    yield from linear_g(wb_q, 2, 4, 8, lambda kt: ba[:, kt, :],
                                        lambda m, ps: fw.act(bb[:, m, :], ps, AF.Copy, scale=0.0625))
                    for hp in range(2):
                        heads = (2 * hp, 2 * hp + 1)
                        pss = {}
                        for hd in heads:
                            for mt in range(2):
                                ps = fw.bank()
                                for kk in range(2):
                                    fw.mm(ps.v(), KTs[:, 2 * hd + kk, s * MEM + mt * 128:s * MEM + (mt + 1) * 128],
                                          bb[:, 2 * hd + kk, :], kk == 0, kk == 1)
                                pss[(hd, mt)] = ps
                        for hd in heads:
                            for mt in range(2):
                                fw.act(Es[hd % 2][:, mt, :], pss[(hd, mt)].v(), AF.Exp)
                        for hd in heads:
                            E, rden = Es[hd % 2], rdens[hd % 2]
                            pd = fw.bank()
                            for mt in range(2):
                                fw.mm(pd.v(), ones_1.v(), E[:, mt, :], mt == 0, mt == 1)
                            fw.act(rden.v(), pd.v(), AF.Ln)
                            fw.act(rden.v(), rden.v(), AF.Exp, scale=-1.0)
                            for dt in range(2):
                                po = fw.bank()
                                for mt in range(2):
                                    fw.mm(po.v(), Vs[:, s * 2 + mt, (2 * hd + dt) * 128:(2 * hd + dt + 1) * 128],
                                          E[:, mt, :], mt == 0, mt == 1)
                                fw.tt("dve", ba[:, 2 * hd + dt, :], po.v(), rden.v(), ALU.mult)
                        yield
                    yield from linear_g(wb_o, 2, 4, 8, lambda kt: ba[:, kt, :],
                                        lambda m, ps: fw.stt("dve", fa[:, m, :], fb_[:, m, :], ALPHA, ps, ALU.mult, ALU.add))
                    layernorm(fa.v(), "ln2_g", "ln2_b", NT, h2b.v(), bb.v(), tmx, out_bf=h2b.v())
                    if debug and s == 0 and i == 0:
                        fw.dma("sp", dbg["d_h2"], fa.v(), dsem=d_dbg)
                    yield

                def Y_gen(i):
                    h2 = AB[i % 2]
                    tok0 = s * SEQ + i * NT

                    def epiF(m, ps):
                        fw.act(rl.v(), ps, AF.Relu)
                        if m % 2 == 0:
                            fw.act(ffb[:, m, :], rl.v(), AF.Square)
                        else:
                            fw.tt("dve", ffb[:, m, :], rl.v(), rl.v(), ALU.mult)

                    yield from linear_g(wb_ff1, 8, 4, 8, lambda kt: h2b[:, kt, :], epiF)
                    yield from linear_g(wb_ff2, 8, 1, 32, lambda kt: ffb[:, kt, :],
                                        lambda m, ps: fw.stt("dve", h2[:, m, :], h2[:, m, :], ALPHA, ps, ALU.mult, ALU.add))
                    layernorm(h2.v(), "ln3_g", "ln3_b", NT, ffb[:, 0:8, :], ffb[:, 8:16, :], tmy)
                    fw.dma("sp", outT_v[:, :, tok0:tok0 + NT], h2.v(), dsem=d_out)
                    yield

                SCHED0 = "y x y y x x x y y y y x y x y x y x y x y x y x y x x n y n y n y"
                SCHED1 = "y x y y y y y x x y x x y x y x x y y y y n y n y n y y"
                gens = {i: X_gen(i) for i in range(4)}
                for _ in gens[0]:
                    pass
                for _ in range(3):
                    next(gens[1], None)
                for i in range(4):
                    gy = Y_gen(i)
                    gx = gens.get(i + 1)
                    gn = gens.get(i + 2)
                    for c in SCHED1.split():
                        if c == "y":
                            next(gy, None)
                        elif c == "x":
                            if gx is not None:
                                next(gx, None)
                        elif gn is not None:
                            next(gn, None)
                    for _ in gy:
                        pass
                    if gx is not None:
                        for _ in gx:
                            pass

    fw.dry = True
    emit()
    fw.dry = False
    ws.reset_for_real()
    fw.bank_i = 0
    emit()
    fw._wait("sp", ("dma", d_out, d_out.count))
    if debug:
        fw._wait("sp", ("dma", d_dbg, d_dbg.count))
    fw.barrier()
    fw.root.close()
    return nc, fw


def _blk(W, G):
    K, N = W.shape
    KT, MT = K // 128, N // 128
    a = W.reshape(KT, 128, MT // G, G, 128).transpose(2, 1, 3, 0, 4)
    return np.ascontiguousarray(a.reshape(MT // G, 128, G * KT * 128)).astype(np.float32)


def host_layout(inp):
    f = lambda k: np.asarray(inp[k], dtype=np.float32)
    shared = {}
    shared["wb_in"] = _blk(f("w_in")[0], 4)
    shared["wb_mix"] = _blk(f("w_mix_out")[0], 4)
    shared["wb_q"] = _blk(f("w_q")[0], 4)
    shared["wb_k"] = _blk(f("w_k")[0], 4)
    shared["wb_o"] = _blk(f("w_o")[0], 4)
    shared["wb_ff1"] = _blk(f("w_ff1")[0], 4)
    shared["wb_ff2"] = _blk(f("w_ff2")[0], 1)
    shared["w_v"] = np.ascontiguousarray(f("w_v")[0])
    shared["w_glu"] = np.ascontiguousarray(f("s5_w_glu")[0])
    shared["ident"] = np.eye(128, dtype=np.float32)
    vecs = np.zeros((128, NV), np.float32)

    def put8(name, v):
        vecs[:, VC[name]:VC[name] + 8] = v.reshape(8, 128).T

    def put4(name, v):
        vecs[:, VC[name]:VC[name] + 4] = v.reshape(4, 128).T

    put8("ln_in_g", f("ln_in_g")); put8("ln_in_b", f("ln_in_b"))
    put8("ln1_g", f("ln1_g")[0]); put8("ln1_b", f("ln1_b")[0])
    put8("ln2_g", f("ln2_g")[0]); put8("ln2_b", f("ln2_b")[0])
    put8("ln3_g", f("ln3_g")[0]); put8("ln3_b", f("ln3_b")[0])
    put8("mem_g", f("mem_ln_g")[0]); put8("mem_b", f("mem_ln_b")[0])
    put4("b_glu", f("s5_b_glu")[0]); put4("g_s5", f("g_s5")[0]); put4("g_lru", f("g_lru")[0])
    put4("conv_b", f("conv_b")[0]); put4("b_a", f("lru_b_a")[0].reshape(512)); put4("b_x", f("lru_b_x")[0].reshape(512))
    put4("lam", f("lru_lambda")[0]); put4("s5_d", f("s5_d")[0].reshape(512))
    cw = f("conv_w")[0]
    vecs[:, VC["conv_w"]:VC["conv_w"] + 16] = cw.reshape(4, 4, 128).transpose(2, 1, 0).reshape(128, 16)
    shared["vecs"] = vecs
    def sl16(a):
        return a.reshape(16, 2, 64).transpose(1, 2, 0).reshape(128, 16)

    a_re, a_im = f("s5_a_re")[0], f("s5_a_im")[0]
    logdt = np.repeat(f("s5_log_dt")[0][:, None], 64, axis=1)
    shared["s5sl"] = np.ascontiguousarray(np.stack([sl16(a_re), sl16(a_im), sl16(logdt)], axis=1))

    def bpad(b):
        o = np.zeros((2, 64, 16, 2, 16), np.float32)
        bb = b.reshape(16, 2, 64, 16)
        for g2 in range(2):
            o[g2, :, :, g2, :] = bb[:, g2].transpose(1, 0, 2)
        return o.reshape(128, 16, 32)

    def cpad(c):
        return bpad(c.transpose(0, 2, 1))

    shared["bsl"] = np.ascontiguousarray(np.stack([bpad(f("s5_b_re")[0]), bpad(f("s5_b_im")[0])], axis=1))
    shared["csl"] = np.ascontiguousarray(np.stack([cpad(f("s5_c_re")[0]), cpad(f("s5_c_im")[0])], axis=1))

    def bd(w):
        o = np.zeros((2, 64, 4, 2, 64), np.float32)
        ww = w.reshape(4, 2, 64, 64)
        for hh in range(2):
            o[hh, :, :, hh, :] = ww[:, hh].transpose(1, 0, 2)
        return o.reshape(128, 4, 128)

    shared["wab"] = np.ascontiguousarray(np.stack([bd(f("lru_w_a")[0]), bd(f("lru_w_x")[0])], axis=1))
    x = f("x")
    mem = f("mem")
    per_core = []
    for c in range(NCORES):
        d = dict(shared)
        d["xT"] = np.ascontiguousarray(x[2 * c:2 * c + 2].reshape(NTOK, D).T)
        d["memT"] = np.ascontiguousarray(mem[2 * c:2 * c + 2].reshape(2 * MEM, D).T)
        per_core.append(d)
    return per_core


_CACHE = {}


def kernel(**inputs):
    if "nc" not in _CACHE:
        _CACHE["nc"] = build_program(debug=False)[0]
    nc = _CACHE["nc"]
    in_maps = host_layout(inputs)
    res = run_bass_kernel_spmd(nc, in_maps, core_ids=list(range(NCORES)))
    out = np.empty((16, SEQ, D), np.float32)
    for c in range(NCORES):
        o = res.results[c]["outT"]
        out[2 * c:2 * c + 2] = o.T.reshape(2, SEQ, D)
    return out
```
